# Optimizing a Trainium2 kernel written in Bass

```python
import math
import jax, jax.numpy as jnp
from jax import lax
import numpy as np

D_MODEL = 1024
BATCH = 8
SEQ = 4096
DEPTH = 2

HEAD_DIM = 64
SB_HEADS = 8
SB_WIDTH = SB_HEADS * HEAD_DIM
DIFF_HEADS = 4
DIFF_QK_WIDTH = DIFF_HEADS * 2 * HEAD_DIM
DIFF_V_WIDTH = DIFF_HEADS * 2 * HEAD_DIM
CONV_CH = 512
CONV_WIDTH = 31
N_BRANCH = 3
XA_HEADS = 4
XA_HEAD_DIM = 128
XA_WIDTH = XA_HEADS * XA_HEAD_DIM
MEM_LEN = 256
D_FF = 4 * D_MODEL
NUM_BUCKETS = 32
MAX_EXACT = NUM_BUCKETS // 2
MAX_DISTANCE = 128
Q_BLOCK = 128
EPS = 1e-6
NEG_INF = -1e30

IN_SPLITS = (SB_WIDTH, SB_WIDTH, SB_WIDTH,
             DIFF_QK_WIDTH, DIFF_QK_WIDTH, DIFF_V_WIDTH,
             2 * CONV_CH,
             D_MODEL, D_MODEL, D_MODEL)
IN_COLS = sum(IN_SPLITS)
SPLIT_POINTS = tuple(sum(IN_SPLITS[:i + 1]) for i in range(len(IN_SPLITS) - 1))

kernel_name = "hybrid_gated_sb_diff_conformer_block"


def rmsnorm(x, g):
    xf = x.astype(jnp.float32)
    y = xf * lax.rsqrt(jnp.mean(xf * xf, axis=-1, keepdims=True) + EPS)
    return (y * g.astype(jnp.float32)).astype(x.dtype)


def layernorm(x, g, b):
    xf = x.astype(jnp.float32)
    mu = jnp.mean(xf, axis=-1, keepdims=True)
    var = jnp.mean(jnp.square(xf - mu), axis=-1, keepdims=True)
    y = (xf - mu) * lax.rsqrt(var + EPS)
    return (y * g.astype(jnp.float32) + b.astype(jnp.float32)).astype(x.dtype)


def rel_bucket(rel):
    n = jnp.maximum(rel, 0)
    is_small = n < MAX_EXACT
    nf = jnp.maximum(n, 1).astype(jnp.float32)
    large = MAX_EXACT + (jnp.log(nf / MAX_EXACT) / math.log(MAX_DISTANCE / MAX_EXACT)
                         * (NUM_BUCKETS - MAX_EXACT)).astype(jnp.int32)
    large = jnp.minimum(large, NUM_BUCKETS - 1)
    return jnp.where(is_small, n, large)


def to_query_blocks(q):
    B, S = q.shape[:2]
    nb = S // Q_BLOCK
    qb = q.reshape((B, nb, Q_BLOCK) + q.shape[2:])
    return jnp.moveaxis(qb, 1, 0), jnp.arange(nb, dtype=jnp.int32) * Q_BLOCK


def from_query_blocks(o):
    o = jnp.moveaxis(o, 0, 1)
    return o.reshape((o.shape[0], o.shape[1] * o.shape[2]) + o.shape[3:])


def stick_breaking_attention(q, k, v):
    S = q.shape[1]
    scale = q.shape[-1] ** -0.5
    key_pos = jnp.arange(S, dtype=jnp.int32)
    qb, starts = to_query_blocks(q)

    def block(args):
        q_blk, start = args
        z = jnp.einsum('bqhd,bkhd->bhqk', q_blk, k,
                       preferred_element_type=jnp.float32) * scale
        q_pos = start + jnp.arange(Q_BLOCK, dtype=jnp.int32)
        strict = key_pos[None, :] < q_pos[:, None]
        log_1m_beta = jnp.where(strict, -jax.nn.softplus(z), 0.0)
        after = lax.cumsum(log_1m_beta, axis=3, reverse=True) - log_1m_beta
        a = jnp.where(strict, jnp.exp(jax.nn.log_sigmoid(z) + after), 0.0)
        return jnp.einsum('bhqk,bkhd->bqhd', a.astype(v.dtype), v)

    return from_query_blocks(lax.map(block, (qb, starts)))


def diff_attention(q, k, v, lam, bias_table):
    S = q.shape[1]
    scale = q.shape[-1] ** -0.5
    key_pos = jnp.arange(S, dtype=jnp.int32)
    table = bias_table.astype(jnp.float32)
    qb, starts = to_query_blocks(q)

    def block(args):
        q_blk, start = args
        z = jnp.einsum('bqhmd,bkhmd->bhmqk', q_blk, k,
                       preferred_element_type=jnp.float32) * scale
        q_pos = start + jnp.arange(Q_BLOCK, dtype=jnp.int32)
        rel = q_pos[:, None] - key_pos[None, :]
        bias = jnp.transpose(table[rel_bucket(rel)], (2, 0, 1))
        z = jnp.where(rel >= 0, z + bias[:, None], NEG_INF)
        p = jax.nn.softmax(z, axis=-1)
        a = p[:, :, 0] - lam * p[:, :, 1]
        return jnp.einsum('bhqk,bkhe->bqhe', a.astype(v.dtype), v)

    return from_query_blocks(lax.map(block, (qb, starts)))


def conformer_conv(u, w_dw, b_dw, ln_g, ln_b):
    a, g = jnp.split(u, 2, axis=-1)
    h = a * jax.nn.sigmoid(g)
    h = lax.conv_general_dilated(
        h, w_dw.astype(h.dtype), window_strides=(1,), padding=[(CONV_WIDTH - 1, 0)],
        dimension_numbers=('NWC', 'WIO', 'NWC'), feature_group_count=CONV_CH) + b_dw
    h = layernorm(h, ln_g, ln_b)
    return jax.nn.silu(h)


def cross_attention(xn, memn, w_q, w_kv, w_o):
    B, S, _ = xn.shape
    q = (xn @ w_q).reshape(B, S, XA_HEADS, XA_HEAD_DIM)
    k, v = jnp.split(memn @ w_kv, 2, axis=-1)
    k = k.reshape(B, MEM_LEN, XA_HEADS, XA_HEAD_DIM)
    v = v.reshape(B, MEM_LEN, XA_HEADS, XA_HEAD_DIM)
    s = jnp.einsum('bqhd,bmhd->bhqm', q, k,
                   preferred_element_type=jnp.float32) * (XA_HEAD_DIM ** -0.5)
    p = jax.nn.softmax(s, axis=-1)
    o = jnp.einsum('bhqm,bmhd->bqhd', p.astype(v.dtype), v).reshape(B, S, XA_WIDTH)
    return o @ w_o


def sqrelu_mlp(xn, w_up, w_down):
    return jnp.square(jax.nn.relu(xn @ w_up)) @ w_down


def setup_inputs(seed: int = 0) -> dict:
    key = jax.random.key(seed)
    ks = jax.random.split(key, 24)
    f32 = jnp.float32

    def w(k, shape, fan_in):
        return jax.random.normal(k, shape, f32) * (fan_in ** -0.5)

    def gain(k, shape):
        return 1.0 + 0.1 * jax.random.normal(k, shape, f32)

    return {
        "x": jax.random.normal(ks[0], (BATCH, SEQ, D_MODEL), f32),
        "mem": jax.random.normal(ks[1], (BATCH, MEM_LEN, D_MODEL), f32),
        "rel_bias": 0.5 * jax.random.normal(ks[2], (NUM_BUCKETS, DIFF_HEADS), f32),
        "ln_mix": gain(ks[3], (DEPTH, D_MODEL)),
        "w_in": w(ks[4], (DEPTH, D_MODEL, IN_COLS), D_MODEL),
        "diff_lambda": 0.1 * jax.random.normal(ks[5], (DEPTH, 4, HEAD_DIM), f32),
        "diff_subln": gain(ks[6], (DEPTH, 2 * HEAD_DIM)),
        "conv_w": w(ks[7], (DEPTH, CONV_WIDTH, 1, CONV_CH), CONV_WIDTH),
        "conv_b": 0.02 * jax.random.normal(ks[8], (DEPTH, CONV_CH), f32),
        "conv_ln_g": gain(ks[9], (DEPTH, CONV_CH)),
        "conv_ln_b": 0.02 * jax.random.normal(ks[10], (DEPTH, CONV_CH), f32),
        "w_sb_proj": w(ks[11], (DEPTH, SB_WIDTH, D_MODEL), SB_WIDTH),
        "w_diff_proj": w(ks[12], (DEPTH, DIFF_V_WIDTH, D_MODEL), DIFF_V_WIDTH),
        "w_conv_proj": w(ks[13], (DEPTH, CONV_CH, D_MODEL), CONV_CH),
        "w_out": w(ks[14], (DEPTH, D_MODEL, D_MODEL), D_MODEL),
        "ln_xattn": gain(ks[15], (DEPTH, D_MODEL)),
        "ln_mem": gain(ks[16], (DEPTH, D_MODEL)),
        "xa_w_q": w(ks[17], (DEPTH, D_MODEL, XA_WIDTH), D_MODEL),
        "xa_w_kv": w(ks[18], (DEPTH, D_MODEL, 2 * XA_WIDTH), D_MODEL),
        "xa_w_o": w(ks[19], (DEPTH, XA_WIDTH, D_MODEL), XA_WIDTH),
        "ln_mlp": gain(ks[20], (DEPTH, D_MODEL)),
        "w_up": w(ks[21], (DEPTH, D_MODEL, D_FF), D_MODEL),
        "w_down": w(ks[22], (DEPTH, D_FF, D_MODEL), D_FF),
        "ln_final": gain(ks[23], (D_MODEL,)),
    }


def reference(x, mem, rel_bias, ln_mix, w_in, diff_lambda, diff_subln, conv_w, conv_b,
              conv_ln_g, conv_ln_b, w_sb_proj, w_diff_proj, w_conv_proj, w_out,
              ln_xattn, ln_mem, xa_w_q, xa_w_kv, xa_w_o, ln_mlp, w_up, w_down, ln_final):
    B, S, _ = x.shape
    for l in range(DEPTH):
        xn = rmsnorm(x, ln_mix[l])
        proj = xn @ w_in[l]
        (sb_q, sb_k, sb_v, df_q, df_k, df_v, cv_u,
         g_sb, g_df, g_cv) = jnp.split(proj, SPLIT_POINTS, axis=-1)

        o_sb = stick_breaking_attention(
            sb_q.reshape(B, S, SB_HEADS, HEAD_DIM),
            sb_k.reshape(B, S, SB_HEADS, HEAD_DIM),
            sb_v.reshape(B, S, SB_HEADS, HEAD_DIM)).reshape(B, S, SB_WIDTH)

        lam_init = 0.8 - 0.6 * math.exp(-0.3 * l)
        lam_p = diff_lambda[l].astype(jnp.float32)
        lam = (jnp.exp(jnp.sum(lam_p[0] * lam_p[1]))
               - jnp.exp(jnp.sum(lam_p[2] * lam_p[3])) + lam_init)
        o_df = diff_attention(
            df_q.reshape(B, S, DIFF_HEADS, 2, HEAD_DIM),
            df_k.reshape(B, S, DIFF_HEADS, 2, HEAD_DIM),
            df_v.reshape(B, S, DIFF_HEADS, 2 * HEAD_DIM), lam, rel_bias)
        o_df = (rmsnorm(o_df, diff_subln[l]) * (1.0 - lam_init)).reshape(B, S, DIFF_V_WIDTH)

        o_cv = conformer_conv(cv_u, conv_w[l], conv_b[l], conv_ln_g[l], conv_ln_b[l])

        y = (jax.nn.sigmoid(g_sb) * (o_sb @ w_sb_proj[l])
             + jax.nn.sigmoid(g_df) * (o_df @ w_diff_proj[l])
             + jax.nn.sigmoid(g_cv) * (o_cv @ w_conv_proj[l]))
        x = x + y @ w_out[l]

        x = x + cross_attention(rmsnorm(x, ln_xattn[l]), rmsnorm(mem, ln_mem[l]),
                                xa_w_q[l], xa_w_kv[l], xa_w_o[l])

        x = x + sqrelu_mlp(rmsnorm(x, ln_mlp[l]), w_up[l], w_down[l])
    return rmsnorm(x, ln_final)
```

```python
import math
from contextlib import ExitStack
import numpy as np
import concourse.bass as bass
import concourse.mybir as mybir
from concourse.bass_utils import run_bass_kernel_spmd

F32 = mybir.dt.float32
BF16 = mybir.dt.bfloat16
AF = mybir.ActivationFunctionType
ALU = mybir.AluOpType
AX = mybir.AxisListType

D = 1024
NL = 2
KC = 8
IN_COLS = 7168
DFF = 4096
MEM = 256
EPS = 1e-6
NEG = -30000.0
NUM_BUCKETS = 32
CONV_W = 31
HALO = CONV_W - 1


def _bucket(n):
    if n < 16:
        return n
    v = 16 + int(np.float32(np.log(np.float32(n) / np.float32(16)) / np.float32(math.log(8.0)) * np.float32(16)))
    return min(v, 31)


class Buf:
    __slots__ = ("t", "w", "pw", "r", "dsem", "dcnt", "name")

    def __init__(self, t=None, name=""):
        self.t = t
        self.w = {}
        self.pw = {}
        self.r = {}
        self.dsem = {}
        self.dcnt = {}
        self.name = name

    def __getitem__(self, k):
        return self.t[k]


class Prog:
    ENGS = ("pe", "act", "dve", "pool", "sp")
    ROLL = 30000

    def __init__(self, nc):
        self.nc = nc
        self.eng = {"pe": nc.tensor, "act": nc.scalar, "dve": nc.vector, "pool": nc.gpsimd, "sp": nc.sync}
        self.sem = {}
        self.cnt = {}
        self.seen = {e: {} for e in self.ENGS}
        self.pending = {e: [] for e in self.ENGS}
        self.allsems = {}
        self.free_dsems = {"hw": [], "sw": []}
        self.nsem = 0
        self.ninst = 0
        for e in self.ENGS:
            self._new_eng_sem(e)

    def _alloc_sem(self, name):
        self.nsem += 1
        s = self.nc.alloc_semaphore(f"{name}_{self.nsem}")
        self.allsems[s.num] = [s, 0]
        return s

    def _new_eng_sem(self, e):
        self.sem[e] = self._alloc_sem("e" + e)
        self.cnt[e] = 0

    def _get_dsem(self, kind):
        if self.free_dsems[kind]:
            return self.free_dsems[kind].pop()
        return (self._alloc_sem("d" + kind), 0)

    def _waits(self, eng, reads, writes, pwrites):
        waits = {}

        def add(d):
            for k, ev in d.items():
                if k not in waits or waits[k][1] < ev[1]:
                    waits[k] = ev

        for b in reads:
            add(b.w)
            add(b.pw)
        for b in writes:
            add(b.w)
            add(b.pw)
            add(b.r)
        for b in pwrites:
            add(b.w)
            add(b.r)
        e = self.eng[eng]
        seen = self.seen[eng]
        for k, (s, v) in waits.items():
            if eng == "pe" and k == self.sem["pe"].num:
                continue
            if seen.get(k, 0) >= v:
                continue
            seen[k] = v
            e.wait_ge(s, v)

    def _apply(self, key, ev, reads, writes, pwrites):
        for b in reads:
            b.r[key] = ev
        for b in writes:
            b.w = {key: ev}
            b.pw = {}
            b.r = {}
        for b in pwrites:
            b.pw[key] = ev

    def op(self, eng, fn, reads=(), writes=(), pwrites=(), inc=True):
        self._waits(eng, reads, writes, pwrites)
        ins = fn(self.eng[eng])
        self.ninst += 1
        if not inc:
            self.pending[eng].append((reads, writes, pwrites))
            return
        if self.cnt[eng] >= self.ROLL:
            self._new_eng_sem(eng)
        self.cnt[eng] += 1
        sem = self.sem[eng]
        ins.then_inc(sem, 1)
        self.allsems[sem.num][1] = self.cnt[eng]
        ev = (sem, self.cnt[eng])
        self._apply(sem.num, ev, reads, writes, pwrites)
        for (r, w, pw) in self.pending[eng]:
            self._apply(sem.num, ev, r, w, pw)
        self.pending[eng] = []

    def dma(self, q, out, in_, sb, reads=(), writes=(), pwrites=(), **kw):
        self._waits(q, reads, writes, pwrites)
        kind = "sw" if q == "pool" else "hw"
        if kind not in sb.dsem:
            sb.dsem[kind], sb.dcnt[kind] = self._get_dsem(kind)
        ins = self.eng[q].dma_start(out=out, in_=in_, **kw)
        self.ninst += 1
        sb.dcnt[kind] += 16
        assert sb.dcnt[kind] < 60000
        sem = sb.dsem[kind]
        ins.then_inc(sem, 16)
        self.allsems[sem.num][1] = sb.dcnt[kind]
        ev = (sem, sb.dcnt[kind])
        self._apply(sem.num, ev, reads, writes, pwrites)

    def release(self, bufs):
        for b in bufs:
            for kind, sem in b.dsem.items():
                self.free_dsems[kind].append((sem, b.dcnt[kind]))
            b.dsem = {}
            b.dcnt = {}

    def barrier(self, engs=None):
        for e in (engs or self.ENGS):
            assert not self.pending[e]
            seen = self.seen[e]
            for k, (s, v) in self.allsems.items():
                if v > 0 and seen.get(k, 0) < v:
                    seen[k] = v
                    self.eng[e].wait_ge(s, v)


class Stage:
    def __init__(self, P, name):
        self.P = P
        self.nc = P.nc
        self.name = name
        self.es = ExitStack()
        self.bufs = []
        self.n = 0

    def sb(self, shape, dtype, name="t"):
        self.n += 1
        t = self.es.enter_context(self.nc.sbuf_tensor(f"{self.name}_{name}{self.n}", list(shape), dtype))
        b = Buf(t, f"{self.name}_{name}")
        self.bufs.append(b)
        return b

    def ps(self, shape, dtype=F32, name="p"):
        self.n += 1
        t = self.es.enter_context(self.nc.psum_tensor(f"{self.name}_{name}{self.n}", list(shape), dtype))
        b = Buf(t, f"{self.name}_{name}")
        self.bufs.append(b)
        return b

    def close(self):
        self.P.barrier()
        self.P.release(self.bufs)
        self.es.close()


class Rot:
    def __init__(self, items):
        self.items = items
        self.i = 0

    def next(self):
        b = self.items[self.i % len(self.items)]
        self.i += 1
        return b


def build(S, debug=False, stages=None, nlayers=NL):
    NT = S // 128
    NB = S // 512
    nc = bass.Bass("TRN2", target_bir_lowering=False)
    P = Prog(nc)

    def dram_in(name, shape):
        return nc.dram_tensor(name, list(shape), F32, kind="ExternalInput").ap()

    okind = "ExternalOutput" if debug else "Internal"

    def scratch(name, shape, dtype):
        return nc.dram_tensor(name, list(shape), dtype, kind=okind).ap()

    x_in = dram_in("x", [S, D])
    mem_in = dram_in("mem", [MEM, D])
    rel_bias = dram_in("rel_bias", [NUM_BUCKETS, 4])
    ln_mix = dram_in("ln_mix", [NL, D])
    w_in = dram_in("w_in", [NL, D, IN_COLS])
    diff_lambda = dram_in("diff_lambda", [NL, 4, 64])
    diff_subln = dram_in("diff_subln", [NL, 128])
    conv_w = dram_in("conv_w", [NL, CONV_W, 1, 512])
    conv_b = dram_in("conv_b", [NL, 512])
    conv_ln_g = dram_in("conv_ln_g", [NL, 512])
    conv_ln_b = dram_in("conv_ln_b", [NL, 512])
    w_sb_proj = dram_in("w_sb_proj", [NL, 512, D])
    w_diff_proj = dram_in("w_diff_proj", [NL, 512, D])
    w_conv_proj = dram_in("w_conv_proj", [NL, 512, D])
    w_out = dram_in("w_out", [NL, D, D])
    ln_xattn = dram_in("ln_xattn", [NL, D])
    ln_mem = dram_in("ln_mem", [NL, D])
    xa_w_q = dram_in("xa_w_q", [NL, D, 512])
    xa_w_kv = dram_in("xa_w_kv", [NL, D, D])
    xa_w_o = dram_in("xa_w_o", [NL, 512, D])
    ln_mlp = dram_in("ln_mlp", [NL, D])
    w_up = dram_in("w_up", [NL, D, DFF])
    w_down = dram_in("w_down", [NL, DFF, D])
    ln_final = dram_in("ln_final", [D])
    y_out = nc.dram_tensor("y", [S, D], F32, kind="ExternalOutput").ap()

    xres = scratch("xres", [S, D], F32)
    xnT_d = scratch("xnT", [KC, 128, S], BF16)
    featT_d = scratch("featT", [24, 128, S], BF16)
    vtok_d = scratch("vtok", [2, S, 512], BF16)
    obrT_d = scratch("obrT", [12, 128, S], BF16)
    def wload(dst, dst_ap, src2d, first=True):
        K = src2d.shape[0]
        for k in range(K // 128):
            P.dma("pool", dst_ap[:, k, :], src2d[k * 128:(k + 1) * 128, :], dst,
                  writes=[dst] if (first and k == 0) else (), pwrites=() if (first and k == 0) else [dst])

    d_xres = [Buf(name=f"xres{i}") for i in range(NT)]
    d_xnT = [Buf(name=f"xnT{i}") for i in range(NT)]
    d_misc = Buf(name="dmisc")

    def want(s):
        return stages is None or s in stages

    G = Stage(P, "g")
    ident_f = G.sb([128, 128], F32, "identf")
    ident_b = G.sb([128, 128], BF16, "identb")
    ones_f = G.sb([128, 512], F32, "onesf")
    ones_b = G.sb([128, 128], BF16, "onesb")
    U_b = G.sb([128, 128], BF16, "U")
    LU_b = G.sb([128, 128], BF16, "LU")
    cols = G.sb([128, 128], F32, "cols")
    epsc = G.sb([128, 1], F32, "eps")

    P.op("pool", lambda e: e.memset(ones_f[:], 1.0), writes=[ones_f])
    P.op("pool", lambda e: e.memset(epsc[:], EPS), writes=[epsc])
    P.op("dve", lambda e: e.tensor_copy(out=ones_b[:], in_=ones_f[:, 0:128]), reads=[ones_f], writes=[ones_b])
    P.op("pool", lambda e: e.affine_select(out=ident_f[:], in_=ones_f[:, 0:128], pattern=[[1, 128]],
                                           compare_op=ALU.is_equal, fill=0.0, base=0, channel_multiplier=-1),
         reads=[ones_f], writes=[ident_f])
    P.op("dve", lambda e: e.tensor_copy(out=ident_b[:], in_=ident_f[:]), reads=[ident_f], writes=[ident_b])
    P.op("pool", lambda e: e.affine_select(out=U_b[:], in_=ones_f[:, 0:128], pattern=[[-1, 128]],
                                           compare_op=ALU.is_ge, fill=0.0, base=0, channel_multiplier=1),
         reads=[ones_f], writes=[U_b])
    P.op("pool", lambda e: e.affine_select(out=LU_b[:], in_=ones_f[:, 0:128], pattern=[[1, 128]],
                                           compare_op=ALU.is_gt, fill=0.0, base=0, channel_multiplier=-1),
         reads=[ones_f], writes=[LU_b])
    COL = {}
    with ExitStack() as es0:
        rows_t = es0.enter_context(nc.sbuf_tensor("rows", [128, 128], F32))
        rows = Buf(rows_t, "rows")
        pst = Buf(es0.enter_context(nc.psum_tensor("rowsT", [128, 128], F32)))
        P.op("pool", lambda e: e.memset(rows[:], 0.0), writes=[rows])
        r0 = 0

        def addrows(key, ap, n):
            nonlocal r0
            COL[key] = r0
            P.dma("sp", rows[r0:r0 + n, :], ap.rearrange("(c p) -> c p", p=128), rows, pwrites=[rows])
            r0 += n

        for l in range(NL):
            addrows(("ln_mix", l), ln_mix[l], 8)
            addrows(("ln_xattn", l), ln_xattn[l], 8)
            addrows(("ln_mem", l), ln_mem[l], 8)
            addrows(("ln_mlp", l), ln_mlp[l], 8)
            addrows(("conv_b", l), conv_b[l], 4)
            addrows(("conv_ln_g", l), conv_ln_g[l], 4)
            addrows(("conv_ln_b", l), conv_ln_b[l], 4)
            addrows(("subln", l), diff_subln[l], 1)
        assert r0 <= 128
        P.op("pe", lambda e: e.transpose(out=pst[:], in_=rows[:], identity=ident_f[:]),
             reads=[rows, ident_f], writes=[pst])
        P.op("dve", lambda e: e.tensor_copy(out=cols[:], in_=pst[:]), reads=[pst], writes=[cols])
        P.barrier()

    def colap(key, c=0, n=1):
        b = COL[key] + c
        return cols[:, b:b + n]

    class NormT:
        def __init__(self, st, nxnb=2):
            self.junk = Rot([st.sb([128, D], BF16, "junk") for _ in range(2)])
            self.xnb = Rot([st.sb([128, D], BF16, "xnb") for _ in range(nxnb)])
            self.stat = Rot([st.sb([128, 4], F32, "stat") for _ in range(max(4, nxnb + 2))])
            self.pT = Rot([st.ps([128, KC, 128], BF16, "pT") for _ in range(2)])

        def rstd(self, xt, width=D):
            stt = self.stat.next()
            junk = self.junk.next()
            P.op("act", lambda e: e.activation(out=junk[:, 0:width], in_=xt[:, 0:width], func=AF.Square,
                                               accum_out=stt[:, 0:1]),
                 reads=[xt], writes=[junk, stt])
            P.op("act", lambda e: e.activation(out=stt[:, 1:2], in_=stt[:, 0:1], func=AF.Ln, bias=epsc[:, 0:1],
                                               scale=1.0 / width),
                 reads=[epsc], writes=[stt])
            P.op("act", lambda e: e.activation(out=stt[:, 2:3], in_=stt[:, 1:2], func=AF.Exp, scale=-0.5),
                 writes=[stt])
            return stt

        def prep(self, xt):
            stt = self.rstd(xt)
            xnb = self.xnb.next()
            P.op("act", lambda e: e.activation(out=xnb[:], in_=xt[:], func=AF.Copy, scale=stt[:, 2:3]),
                 reads=[xt, stt], writes=[xnb])
            return xnb

        def finish(self, xnb, gkey, dst, dst_cols):
            pT = self.pT.next()
            for c in range(KC):
                P.op("pe", lambda e, c=c: e.transpose(out=pT[:, c, :], in_=xnb[:, c * 128:(c + 1) * 128],
                                                      identity=ident_b[:]),
                     reads=[xnb, ident_b], pwrites=[pT] if c else (), writes=() if c else [pT], inc=(c == KC - 1))
            g = colap(gkey, 0, KC)
            P.op("dve", lambda e: e.tensor_tensor(out=dst[:, :, dst_cols], in0=pT[:, :, :],
                                                  in1=g.unsqueeze(2).to_broadcast([128, KC, 128]), op=ALU.mult),
                 reads=[pT, cols], pwrites=[dst])

        def run(self, xt, gkey, dst, dst_cols):
            self.finish(self.prep(xt), gkey, dst, dst_cols)

    class Tail:
        def __init__(self, st, ntiles, final=False, nps=2):
            self.nt = NormT(st, nxnb=ntiles)
            self.xt = [st.sb([128, D], F32, "xt") for _ in range(ntiles)]
            self.ps_r = Rot([st.ps([128, 512], F32, "tp") for _ in range(nps)])
            self.ob_r = Rot([st.sb([128, KC, 128], BF16, "tob") for _ in range(2)])
            self.yt_r = Rot([st.sb([128, D], F32, "yt") for _ in range(2)]) if final else None
            self.gfin = None
            if final:
                self.gfin = st.sb([128, D], F32, "gfin")
                P.dma("sp", self.gfin[:], ln_final.partition_broadcast(128), self.gfin, writes=[self.gfin])

        def block(self, t0, ntiles, lhs, K, w, xsrc, gkey_next, final=False):
            xnbs = []
            for tt in range(ntiles):
                t = t0 + tt
                tcol = tt * 128
                xt = self.xt[tt]
                rows = slice(t * 128, (t + 1) * 128)
                P.dma("sp", xt[:], xsrc[rows, :], xt, reads=[d_xres[t]], writes=[xt])
                for half in range(2):
                    ps = self.ps_r.next()
                    hsl = slice(half * 512, (half + 1) * 512)
                    for k in range(K):
                        P.op("pe", lambda e, k=k, ps=ps, hsl=hsl, tcol=tcol: e.matmul(
                            ps[:], lhsT=lhs[:, k, tcol:tcol + 128], rhs=w[:, k, hsl], start=(k == 0), stop=(k == K - 1)),
                             reads=[lhs, w], writes=[ps] if k == 0 else (), pwrites=[ps] if k else (), inc=(k == K - 1))
                    P.op("dve", lambda e, ps=ps, hsl=hsl, xt=xt: e.tensor_tensor(out=xt[:, hsl], in0=xt[:, hsl], in1=ps[:],
                                                                                 op=ALU.add),
                         reads=[ps, xt], pwrites=[xt])
                if final:
                    stt = self.nt.rstd(xt)
                    yt = self.yt_r.next()
                    P.op("dve", lambda e, xt=xt, stt=stt, yt=yt: e.scalar_tensor_tensor(
                        out=yt[:], in0=xt[:], scalar=stt[:, 2:3], in1=self.gfin[:], op0=ALU.mult, op1=ALU.mult),
                         reads=[xt, stt, self.gfin], writes=[yt])
                    P.dma("pool", y_out[rows, :], yt[:], yt, reads=[yt], pwrites=[d_misc])
                else:
                    P.dma("pool", xres[rows, :], xt[:], xt, reads=[xt], writes=[d_xres[t]])
                    xnbs.append(self.nt.prep(xt))
            if not final:
                for tt in range(ntiles):
                    t = t0 + tt
                    rows = slice(t * 128, (t + 1) * 128)
                    ob = self.ob_r.next()
                    self.nt.finish(xnbs[tt], gkey_next, ob, slice(0, 128))
                    P.dma("pool", xnT_d[:, :, rows].rearrange("c p t -> p c t"), ob[:], ob, reads=[ob],
                          writes=[d_xnT[t]])

    gbias_d = nc.dram_tensor("gbias", [4, 128, 1024], F32, kind="Internal").ap()
    tb = G.sb([128, 128], F32, "tb")
    P.dma("sp", tb[:], rel_bias.rearrange("b h -> (b h)").partition_broadcast(128), tb, writes=[tb])
    thr = [0] * 32
    for n in range(0, 200):
        b = _bucket(n)
        for k in range(1, b + 1):
            if thr[k] == 0:
                thr[k] = n
    if want("A0"):
        st = Stage(P, "A0")
        nt = NormT(st)
        xt_r = Rot([st.sb([128, D], F32, "xt") for _ in range(3)])
        ob_r = Rot([st.sb([128, KC, 128], BF16, "ob") for _ in range(3)])
        for t in range(NT):
            xt = xt_r.next()
            P.dma("sp", xt[:], x_in[t * 128:(t + 1) * 128, :], xt, writes=[xt])
            ob = ob_r.next()
            nt.run(xt, ("ln_mix", 0), ob, slice(0, 128))
            P.dma("pool", xnT_d[:, :, t * 128:(t + 1) * 128].rearrange("c p t -> p c t"), ob[:], ob,
                  reads=[ob], writes=[d_xnT[t]])
        st.close()

    for l in range(nlayers):
        lam_init = 0.8 - 0.6 * math.exp(-0.3 * l)

        if want("B"):
            stBC = Stage(P, f"BC{l}")
            xn_all = stBC.sb([128, KC, S], BF16, "xnall")
            for c in range(KC):
                P.dma("sp", xn_all[:, c, :], xnT_d[c], xn_all, reads=d_xnT, pwrites=[xn_all])
            wb_r = Rot([stBC.sb([128, KC, 512], BF16, "wb") for _ in range(2)])
            ob_r = Rot([stBC.sb([128, 512], BF16, "ob") for _ in range(4)])
            st = Stage(P, f"B{l}")
            ps_r = Rot([st.ps([128, 512], F32, "ps") for _ in range(4)])
            do_gb = (l == 0)
            ev_r = Rot(["act"] if do_gb else ["act", "dve"])
            if do_gb:
                relt = st.sb([128, 1024], F32, "rel")
                dtb = st.sb([128, 128], F32, "dtb")
                acc = st.sb([128, 1024], F32, "acc")
                tmp_r = Rot([st.sb([128, 1024], F32, "tmp") for _ in range(2)])
                gout_r = Rot([st.sb([128, 1024], F32, "gout") for _ in range(2)])
                P.op("pool", lambda e: e.iota(relt[:], pattern=[[1, 1024]], base=-384, channel_multiplier=-1,
                                              allow_small_or_imprecise_dtypes=True), writes=[relt])
                P.op("dve", lambda e: e.tensor_sub(out=dtb[:, 4:128], in0=tb[:, 4:128], in1=tb[:, 0:124]), reads=[tb],
                     writes=[dtb])
                pending_store = []

                def gb_head(h):
                    bs_ = slice(384, 640)
                    P.op("dve", lambda e: e.tensor_scalar(out=acc[:, bs_], in0=relt[:, bs_], scalar1=0.0, scalar2=tb[:, h:h + 1],
                                                          op0=ALU.mult, op1=ALU.add), reads=[relt, tb], writes=[acc])
                    for k in range(1, 32):
                        tmp = tmp_r.next()
                        P.op("dve", lambda e: e.tensor_scalar(out=tmp[:, bs_], in0=relt[:, bs_], scalar1=float(thr[k]) - 0.5,
                                                              scalar2=dtb[:, 4 * k + h:4 * k + h + 1], op0=ALU.is_ge,
                                                              op1=ALU.mult), reads=[relt, dtb], writes=[tmp])
                        P.op("dve", lambda e: e.tensor_tensor(out=acc[:, bs_], in0=acc[:, bs_], in1=tmp[:, bs_], op=ALU.add),
                             reads=[tmp], writes=[acc])
                    gout = gout_r.next()
                    P.op("pool", lambda e: e.memset(gout[:, 0:384], NEG), writes=[gout])
                    P.op("dve", lambda e: e.tensor_scalar(out=gout[:, 640:1024], in0=relt[:, 640:1024], scalar1=0.0,
                                                          scalar2=tb[:, 124 + h:125 + h], op0=ALU.mult, op1=ALU.add),
                         reads=[relt, tb], pwrites=[gout])
                    P.op("pool", lambda e: e.affine_select(out=gout[:, bs_], in_=acc[:, bs_], pattern=[[1, 256]],
                                                           compare_op=ALU.is_ge, fill=NEG, base=0, channel_multiplier=-1),
                         reads=[acc], pwrites=[gout])
                    pending_store.append((h, gout))

                def gb_flush():
                    while pending_store:
                        h, gout = pending_store.pop(0)
                        P.dma("sp", gbias_d[h], gout[:], gout, reads=[gout], pwrites=[d_misc])


            def evac(ps, ob):
                ce = ev_r.next()
                if ce == "act":
                    P.op("act", lambda e: e.copy(out=ob[:], in_=ps[:]), reads=[ps], writes=[ob])
                else:
                    P.op("dve", lambda e: e.tensor_copy(out=ob[:], in_=ps[:]), reads=[ps], writes=[ob])

            groups = [("f", 0, 0), ("f", 512, 4), ("t", 1024, 0),
                      ("f", 1536, 8), ("f", 2048, 12), ("f", 3072, 16), ("f", 3584, 20), ("t", 2560, 1)]
            NHEAD = 3
            wbs = []

            def issue_w(gi):
                wb = wb_r.next()
                c0_ = groups[gi][1]
                wload(wb, wb, w_in[l][:, c0_:c0_ + 512])
                wbs.append(wb)

            def unit(gi, u, ps, evac_fn):
                kind, c0_, fb = groups[gi]
                wb = wbs[gi]
                if kind == "f":
                    blk, m = divmod(u, 4)
                    for k in range(KC):
                        P.op("pe", lambda e, k=k: e.matmul(
                            ps[:], lhsT=wb[:, k, m * 128:(m + 1) * 128],
                            rhs=xn_all[:, k, blk * 512:(blk + 1) * 512], start=(k == 0), stop=(k == KC - 1)),
                             reads=[wb, xn_all], writes=[ps] if k == 0 else (), pwrites=[ps] if k else (),
                             inc=(k == KC - 1))
                    ob = ob_r.next()
                    evac_fn(ps, ob)
                    P.dma("sp", featT_d[fb + m, :, blk * 512:(blk + 1) * 512], ob[:], ob, reads=[ob],
                          pwrites=[d_misc])
                else:
                    t = u
                    for k in range(KC):
                        P.op("pe", lambda e, k=k: e.matmul(
                            ps[:], lhsT=xn_all[:, k, t * 128:(t + 1) * 128], rhs=wb[:, k, :],
                            start=(k == 0), stop=(k == KC - 1)),
                             reads=[wb, xn_all], writes=[ps] if k == 0 else (), pwrites=[ps] if k else (),
                             inc=(k == KC - 1))
                    ob = ob_r.next()
                    evac_fn(ps, ob)
                    P.dma("sp", vtok_d[fb, t * 128:(t + 1) * 128, :], ob[:], ob, reads=[ob], pwrites=[d_misc])

            def nunits(gi):
                return NB * 4 if groups[gi][0] == "f" else NT

            issue_w(0)
            for gi in range(NHEAD):
                issue_w(gi + 1)
                if do_gb:
                    gb_flush()
                    if gi < 2:
                        gb_head(2 * gi)
                        gb_head(2 * gi + 1)
                for u in range(nunits(gi)):
                    unit(gi, u, ps_r.next(), evac)
            st.close()

            def btail_units():
                def evac_dve(ps, ob):
                    P.op("dve", lambda e: e.tensor_copy(out=ob[:], in_=ps[:]), reads=[ps], writes=[ob])
                for gi in range(NHEAD, len(groups)):
                    if gi + 1 < len(groups):
                        issue_w(gi + 1)
                    for u in range(nunits(gi)):
                        yield (gi, u, evac_dve)

        if want("C"):
            st = Stage(P, f"C{l}")
            qz_r = Rot([[st.sb([128, S], BF16, "qz") for _ in range(2)] for _ in range(1)])
            kT_r = Rot([st.sb([128, S], BF16, "kT") for _ in range(1)])
            vz_r = Rot([[st.sb([128, NT, 128], BF16, "vz") for _ in range(2)] for _ in range(1)])
            oT_r = Rot([st.sb([128, S], BF16, "oT") for _ in range(1)])
            psBt = st.ps([128, 512], F32, "Bt")
            bt_units = btail_units()
            for pair in qz_r.items + vz_r.items:
                for bz in pair:
                    P.op("pool", lambda e, bz=bz: e.memset(bz[:], 0.0), writes=[bz])
            masks = st.sb([128, 4, 512], F32, "masks")
            for r in range(4):
                P.op("pool", lambda e, r=r: e.affine_select(out=masks[:, r, :], in_=ones_f[:], pattern=[[1, 512]],
                                                             compare_op=ALU.is_gt, fill=0.0, base=-128 * r,
                                                             channel_multiplier=-1),
                     reads=[ones_f], writes=[masks] if r == 0 else (), pwrites=[masks] if r else ())
            psZ = Rot([st.ps([128, 512], F32, "Z") for _ in range(3)])
            psP = [st.ps([128, 512], F32, "P") for _ in range(2)]
            psO = Rot([st.ps([128, 512], F32, "O") for _ in range(2)])
            E_r = Rot([st.sb([128, 512], F32, "E") for _ in range(4)])
            SP_r = Rot([st.sb([128, 512], BF16, "SP") for _ in range(4)])
            X_r = Rot([st.sb([128, 512], F32, "X") for _ in range(4)])
            A_r = Rot([st.sb([128, 512], BF16, "A") for _ in range(4)])
            masks_b = st.sb([128, 4, 512], BF16, "masksb")
            P.op("dve", lambda e: e.tensor_copy(out=masks_b[:], in_=masks[:]), reads=[masks], writes=[masks_b])
            pair_res = {}

            def load_pair(j):
                if j >= 4 or j in pair_res:
                    return
                qz, kT, vz, oT = qz_r.next(), kT_r.next(), vz_r.next(), oT_r.next()
                P.dma("sp", kT[:], featT_d[4 + j], kT, reads=[d_misc], writes=[kT])
                for hh in range(2):
                    hs_ = slice(64 * hh, 64 * hh + 64)
                    P.dma("sp", qz[hh][hs_, :], featT_d[j, hs_, :], qz[hh], reads=[d_misc], pwrites=[qz[hh]])
                    P.dma("sp", vz[hh][:, :, hs_],
                          vtok_d[0, :, j * 128 + 64 * hh:j * 128 + 64 * hh + 64].rearrange("(t p) c -> p t c", p=128),
                          vz[hh], reads=[d_misc], pwrites=[vz[hh]])
                pair_res[j] = (qz, kT, vz, oT)

            steps = [(j, i, s_) for j in range(4) for i in range(NB) for s_ in range(4 * i + 4)]
            cur = {}
            Obank = {}

            def sub_of(s):
                return slice(128 * (3 - s), 512) if s < 3 else slice(0, 512)

            def emitZ(j, i, hh, s):
                qz, kT, vz, oT = pair_res[j]
                kb = 4 * i + 3 - s
                sub = sub_of(s)
                Z = psZ.next()
                P.op("pe", lambda e: e.matmul(Z[:, sub], lhsT=kT[:, kb * 128:(kb + 1) * 128],
                                              rhs=qz[hh][:, i * 512 + sub.start:(i + 1) * 512], start=True, stop=True),
                     reads=[kT, qz[hh]], writes=[Z])
                cur[("Z", j, i, hh, s)] = Z

            load_pair(0)
            for hh in (0, 1):
                emitZ(0, 0, hh, 0)

            for gi, (j, i, s) in enumerate(steps):
                if ("Z", j, i, 0, s) not in cur:
                    load_pair(j)
                    for hh in (0, 1):
                        emitZ(j, i, hh, s)
                qz, kT, vz, oT = pair_res[j]
                n = 4 * i + 4
                if s == 0:
                    Obank[(j, i)] = psO.next()
                if gi % 3 == 2:
                    nu = next(bt_units, None)
                    if nu is not None:
                        unit(nu[0], nu[1], psBt, nu[2])
                O = Obank[(j, i)]
                qs = slice(i * 512, (i + 1) * 512)
                sub = sub_of(s)
                nxt = steps[gi + 1] if gi + 1 < len(steps) else None
                Es, SPs, Xs, As = {}, {}, {}, {}
                for hh in (0, 1):
                    Z = cur.pop(("Z", j, i, hh, s))
                    E = E_r.next()
                    P.op("act", lambda e, E=E, Z=Z: e.activation(out=E[:, sub], in_=Z[:, sub], func=AF.Exp, scale=0.125),
                         reads=[Z], writes=[E])
                    Es[hh] = E
                for hh in (0, 1):
                    SPb = SP_r.next()
                    P.op("act", lambda e, SPb=SPb, hh=hh: e.activation(out=SPb[:, sub], in_=Es[hh][:, sub], func=AF.Ln,
                                                                       bias=1.0),
                         reads=[Es[hh]], writes=[SPb])
                    SPs[hh] = SPb
                if s < 4:
                    r = 3 - s
                    for hh in (0, 1):
                        P.op("dve", lambda e, hh=hh: e.tensor_tensor(out=SPs[hh][:, sub], in0=SPs[hh][:, sub],
                                                                     in1=masks_b[:, r, sub], op=ALU.mult),
                             reads=[masks_b], writes=[SPs[hh]])
                        P.op("pool", lambda e, hh=hh: e.tensor_tensor(out=Es[hh][:, sub], in0=Es[hh][:, sub],
                                                                      in1=masks[:, r, sub], op=ALU.mult),
                             reads=[masks], writes=[Es[hh]])
                for hh in (0, 1):
                    Pb = psP[hh]
                    P.op("pe", lambda e, Pb=Pb, hh=hh: e.matmul(Pb[:, sub], lhsT=U_b[:], rhs=SPs[hh][:, sub], start=(s == 0),
                                                                stop=False, skip_group_check=True),
                         reads=[U_b, SPs[hh]], writes=[Pb] if s == 0 else (), pwrites=[Pb] if s else ())
                    if nxt is not None and nxt[0] == j:
                        emitZ(nxt[0], nxt[1], hh, nxt[2])
                for hh in (0, 1):
                    X = X_r.next()
                    P.op("act", lambda e, X=X, hh=hh: e.activation(out=X[:, sub], in_=psP[hh][:, sub], func=AF.Exp,
                                                                   scale=-1.0),
                         reads=[psP[hh]], writes=[X])
                    Xs[hh] = X
                for hh in (0, 1):
                    A = A_r.next()
                    P.op("dve", lambda e, A=A, hh=hh: e.tensor_tensor(out=A[:, sub], in0=Es[hh][:, sub], in1=Xs[hh][:, sub],
                                                                      op=ALU.mult),
                         reads=[Es[hh], Xs[hh]], writes=[A])
                    As[hh] = A
                for hh in (0, 1):
                    kb = 4 * i + 3 - s
                    if s < n - 1:
                        P.op("pe", lambda e, hh=hh: e.matmul(psP[hh][:, sub], lhsT=LU_b[:], rhs=SPs[hh][:, sub], start=False,
                                                             stop=True, skip_group_check=True),
                             reads=[LU_b, SPs[hh]], pwrites=[psP[hh]])
                    first = (s == 0 and hh == 0)
                    P.op("pe", lambda e, hh=hh, kb=kb, first=first: e.matmul(
                        O[:, sub], lhsT=vz[hh][:, kb, :], rhs=As[hh][:, sub], start=first, stop=(s == n - 1 and hh == 1),
                        skip_group_check=True),
                         reads=[vz[hh], As[hh]], writes=[O] if first else (), pwrites=() if first else [O])
                if s == n - 1:
                    P.op("dve", lambda e: e.tensor_copy(out=oT[:, qs], in_=O[:]), reads=[O], pwrites=[oT])
                    if i == NB - 1:
                        P.dma("pool", obrT_d[j], oT[:], oT, reads=[oT], pwrites=[d_misc])
            for nu in bt_units:
                unit(nu[0], nu[1], psBt, nu[2])
            st.close()
            stBC.close()

        if want("D"):
            st = Stage(P, f"D{l}")
            lamb = st.sb([128, 256], F32, "lamb")
            lprod = st.sb([128, 128], F32, "lprod")
            lst = st.sb([128, 8], F32, "lst")
            P.dma("sp", lamb[:], diff_lambda[l].rearrange("a d -> (a d)").partition_broadcast(128), lamb, writes=[lamb])
            P.op("dve", lambda e: e.tensor_tensor(out=lprod[:, 0:64], in0=lamb[:, 0:64], in1=lamb[:, 64:128], op=ALU.mult),
                 reads=[lamb], writes=[lprod])
            P.op("dve", lambda e: e.tensor_tensor(out=lprod[:, 64:128], in0=lamb[:, 128:192], in1=lamb[:, 192:256],
                                                  op=ALU.mult), reads=[lamb], writes=[lprod])
            P.op("dve", lambda e: e.reduce_sum(out=lst[:, 0:1], in_=lprod[:, 0:64], axis=AX.X), reads=[lprod], writes=[lst])
            P.op("dve", lambda e: e.reduce_sum(out=lst[:, 1:2], in_=lprod[:, 64:128], axis=AX.X), reads=[lprod],
                 writes=[lst])
            P.op("act", lambda e: e.activation(out=lst[:, 2:4], in_=lst[:, 0:2], func=AF.Exp), reads=[lst], writes=[lst])
            P.op("dve", lambda e: e.tensor_sub(out=lst[:, 4:5], in0=lst[:, 3:4], in1=lst[:, 2:3]), reads=[lst], writes=[lst])
            P.op("dve", lambda e: e.tensor_scalar(out=lst[:, 5:6], in0=lst[:, 4:5], scalar1=-lam_init, scalar2=None,
                                                  op0=ALU.add), reads=[lst], writes=[lst])
            P.op("dve", lambda e: e.tensor_scalar(out=lst[:, 6:7], in0=colap(("subln", l)), scalar1=1.0 - lam_init,
                                                  scalar2=None, op0=ALU.mult), reads=[cols, lst], writes=[lst])
            qz_r = Rot([[st.sb([128, S], BF16, "qz") for _ in range(2)] for _ in range(2)])
            for pair in qz_r.items:
                for bz in pair:
                    P.op("pool", lambda e, bz=bz: e.memset(bz[:], 0.0), writes=[bz])
            kT_r = Rot([st.sb([128, S], BF16, "kT") for _ in range(2)])
            v_r = Rot([st.sb([128, NT, 128], BF16, "v") for _ in range(2)])
            oT_r = Rot([st.sb([128, S], BF16, "oT") for _ in range(2)])
            G_r = Rot([st.sb([128, 1024], F32, "Gh") for _ in range(2)])
            psZ = Rot([st.ps([128, 512], F32, "Z") for _ in range(3)])
            psO = [st.ps([128, 512], F32, "O") for _ in range(2)]
            psD = [st.ps([128, 512], F32, "Dn") for _ in range(2)]
            T_r = Rot([st.sb([128, 512], F32, "T") for _ in range(3)])
            A_r = Rot([st.sb([128, 512], BF16, "A") for _ in range(4)])
            rec_r = Rot([st.sb([128, 512], F32, "rec") for _ in range(2)])
            R_b = [st.sb([128, 512], F32, "R") for _ in range(2)]
            o_b = st.sb([128, 512], F32, "o")
            sq_b = st.sb([128, 512], BF16, "sq")
            rs_b = st.sb([128, 512], F32, "rs")
            head_res = {}

            def load_head(h):
                if h >= 4 or h in head_res:
                    return
                qz, kT, vv, oT, Gh = qz_r.next(), kT_r.next(), v_r.next(), oT_r.next(), G_r.next()
                for m_ in range(2):
                    ms_ = slice(64 * m_, 64 * m_ + 64)
                    P.dma("sp", qz[m_][ms_, :], featT_d[8 + h, ms_, :], qz[m_], reads=[d_misc], pwrites=[qz[m_]])
                P.dma("sp", kT[:], featT_d[12 + h], kT, reads=[d_misc], writes=[kT])
                P.dma("sp", vv[:], vtok_d[1, :, h * 128:(h + 1) * 128].rearrange("(t p) c -> p t c", p=128), vv,
                      reads=[d_misc], writes=[vv])
                P.dma("sp", Gh[:], gbias_d[h], Gh, reads=[d_misc], writes=[Gh])
                head_res[h] = (qz, kT, vv, oT, Gh)

            steps = [(h, i, s_) for h in range(4) for i in range(NB) for s_ in range(4 * i + 4)]
            cur = {}

            def dsub(i, s):
                dl = 4 * i - s
                return slice(128 * (-dl), 512) if dl < 0 else slice(0, 512)

            def emitZ(h, i, m, s):
                qz, kT, vv, oT, Gh = head_res[h]
                Z = psZ.next()
                sub = dsub(i, s)
                P.op("pe", lambda e: e.matmul(Z[:, sub], lhsT=kT[:, s * 128:(s + 1) * 128],
                                              rhs=qz[m][:, i * 512 + sub.start:(i + 1) * 512],
                                              start=True, stop=True), reads=[kT, qz[m]], writes=[Z])
                cur[("Z", h, i, m, s)] = Z

            load_head(0)
            for m in (0, 1):
                emitZ(0, 0, m, 0)
            for gi, (h, i, s) in enumerate(steps):
                qz, kT, vv, oT, Gh = head_res[h]
                if s == 0 and i == 0:
                    load_head(h + 1)
                n = 4 * i + 4
                qs = slice(i * 512, (i + 1) * 512)
                b31 = tb[:, 31 * 4 + h:31 * 4 + h + 1]
                nxt = steps[gi + 1] if gi + 1 < len(steps) else None
                sub = dsub(i, s)
                As = {}
                for m in (0, 1):
                    Z = cur.pop(("Z", h, i, m, s))
                    A = A_r.next()
                    dlt = 4 * i - s
                    if dlt >= 2:
                        P.op("act", lambda e, A=A, Z=Z: e.activation(out=A[:], in_=Z[:], func=AF.Exp, bias=b31, scale=0.125),
                             reads=[Z, tb], writes=[A])
                    else:
                        c0 = 128 * dlt + 384
                        T = T_r.next()
                        P.op("dve", lambda e, T=T, Z=Z, c0=c0: e.scalar_tensor_tensor(
                            out=T[:, sub], in0=Z[:, sub], scalar=0.125, in1=Gh[:, c0 + sub.start:c0 + 512], op0=ALU.mult,
                            op1=ALU.add),
                             reads=[Z, Gh], writes=[T])
                        P.op("act", lambda e, A=A, T=T: e.activation(out=A[:, sub], in_=T[:, sub], func=AF.Exp),
                             reads=[T], writes=[A])
                    As[m] = A
                if nxt is not None:
                    for m in (0, 1):
                        emitZ(nxt[0], nxt[1], m, nxt[2])
                for m in (0, 1):
                    P.op("pe", lambda e, m=m: e.matmul(psO[m][:, sub], lhsT=vv[:, s, :], rhs=As[m][:, sub], start=(s == 0),
                                                       stop=(s == n - 1), skip_group_check=True),
                         reads=[vv, As[m]], writes=[psO[m]] if s == 0 else (), pwrites=[psO[m]] if s else ())
                for m in (0, 1):
                    P.op("pe", lambda e, m=m: e.matmul(psD[m][:, sub], lhsT=ones_b[:], rhs=As[m][:, sub], start=(s == 0),
                                                       stop=(s == n - 1), skip_group_check=True),
                         reads=[ones_b, As[m]], writes=[psD[m]] if s == 0 else (), pwrites=[psD[m]] if s else ())
                if s == n - 1:
                    for m in (0, 1):
                        rec = rec_r.next()
                        P.op("act", lambda e, m=m, rec=rec: e.activation(out=rec[:], in_=psD[m][:], func=AF.Ln),
                             reads=[psD[m]], writes=[rec])
                        P.op("act", lambda e, rec=rec: e.activation(out=rec[:], in_=rec[:], func=AF.Exp, scale=-1.0),
                             writes=[rec])
                        P.op("dve", lambda e, m=m, rec=rec: e.tensor_tensor(out=R_b[m][:], in0=psO[m][:], in1=rec[:],
                                                                            op=ALU.mult),
                             reads=[psO[m], rec], writes=[R_b[m]])
                    P.op("dve", lambda e: e.scalar_tensor_tensor(out=o_b[:], in0=R_b[1][:], scalar=lst[:, 5:6],
                                                                 in1=R_b[0][:], op0=ALU.mult, op1=ALU.add),
                         reads=[R_b[0], R_b[1], lst], writes=[o_b])
                    P.op("pool", lambda e: e.tensor_tensor(out=sq_b[:], in0=o_b[:], in1=o_b[:], op=ALU.mult),
                         reads=[o_b], writes=[sq_b])
                    Zs = psZ.next()
                    P.op("pe", lambda e, Zs=Zs: e.matmul(Zs[:], lhsT=ones_b[:], rhs=sq_b[:], start=True, stop=True),
                         reads=[ones_b, sq_b], writes=[Zs])
                    P.op("act", lambda e, Zs=Zs: e.activation(out=rs_b[:], in_=Zs[:], func=AF.Ln, bias=epsc[:, 0:1],
                                                              scale=1.0 / 128), reads=[Zs, epsc], writes=[rs_b])
                    P.op("act", lambda e: e.activation(out=rs_b[:], in_=rs_b[:], func=AF.Exp, scale=-0.5),
                         writes=[rs_b])
                    P.op("dve", lambda e, oT=oT, qs=qs: e.scalar_tensor_tensor(out=oT[:, qs], in0=o_b[:], scalar=lst[:, 6:7],
                                                                               in1=rs_b[:], op0=ALU.mult, op1=ALU.mult),
                         reads=[o_b, rs_b, lst], pwrites=[oT])
                    if i == NB - 1:
                        P.dma("pool", obrT_d[4 + h], oT[:], oT, reads=[oT], pwrites=[d_misc])
            st.close()

        if want("E"):
            st = Stage(P, f"E{l}")
            rows31 = st.sb([32, 512], F32, "rows31")
            cwc = st.sb([128, 4, CONV_W], F32, "cwc")
            dg = st.sb([128, 4 * CONV_W, 128], BF16, "dg")
            hpad = st.sb([128, 4, HALO + S], BF16, "hpad")
            ocv = st.sb([128, 4, S], BF16, "ocv")
            a_r = Rot([st.sb([128, S], BF16, "aT") for _ in range(2)])
            g_r = Rot([st.sb([128, S], BF16, "gT") for _ in range(2)])
            sg_r = Rot([st.sb([128, S], BF16, "sg") for _ in range(2)])
            psC = Rot([st.ps([128, 512], F32, "C") for _ in range(3)])
            psS = [st.ps([128, 512], F32, "S1"), st.ps([128, 512], F32, "S2")]
            psT = st.ps([128, 4, 32], F32, "cwT")
            hc_all = [[st.sb([128, 512], F32, "hc") for _ in range(4)] for _ in range(2)]
            hb_r = Rot([st.sb([128, 512], BF16, "hb") for _ in range(3)])
            sq_r = Rot([st.sb([128, 512], BF16, "sq") for _ in range(3)])
            mean = st.sb([128, 512], F32, "mean")
            msq = st.sb([128, 512], F32, "msq")
            var = st.sb([128, 512], F32, "var")
            rstd = st.sb([128, 512], F32, "rstd")
            t1_r = Rot([st.sb([128, 512], F32, "t1") for _ in range(2)])
            t2_r = Rot([st.sb([128, 512], F32, "t2") for _ in range(2)])
            s2_r = Rot([st.sb([128, 512], F32, "s2") for _ in range(2)])
            P.dma("sp", rows31[0:CONV_W, :], conv_w[l].rearrange("w o c -> w (o c)"), rows31, writes=[rows31])
            for c in range(4):
                P.op("pe", lambda e, c=c: e.transpose(out=psT[:, c, 0:CONV_W], in_=rows31[0:CONV_W, c * 128:(c + 1) * 128],
                                                      identity=ident_f[0:CONV_W, 0:CONV_W]),
                     reads=[rows31, ident_f], writes=[psT] if c == 0 else (), pwrites=[psT] if c else ())
            P.op("dve", lambda e: e.tensor_copy(out=cwc[:], in_=psT[:, :, 0:CONV_W]), reads=[psT], writes=[cwc])
            for c in range(4):
                P.op("dve", lambda e, c=c: e.tensor_tensor(
                    out=dg[:, c * CONV_W:(c + 1) * CONV_W, :],
                    in0=ident_b[:].unsqueeze(1).to_broadcast([128, CONV_W, 128]),
                    in1=cwc[:, c, :].unsqueeze(2).to_broadcast([128, CONV_W, 128]), op=ALU.mult),
                     reads=[ident_b, cwc], writes=[dg] if c == 0 else (), pwrites=[dg] if c else ())
            P.op("pool", lambda e: e.memset(hpad[:, :, 0:HALO], 0.0), writes=[hpad])
            for c in range(4):
                aT, gT, sg = a_r.next(), g_r.next(), sg_r.next()
                P.dma("sp", aT[:], featT_d[16 + c], aT, reads=[d_misc], writes=[aT])
                P.dma("sp", gT[:], featT_d[20 + c], gT, reads=[d_misc], writes=[gT])
                P.op("act", lambda e: e.activation(out=sg[:], in_=gT[:], func=AF.Sigmoid), reads=[gT], writes=[sg])
                P.op("dve", lambda e, c=c: e.tensor_tensor(out=hpad[:, c, HALO:HALO + S], in0=aT[:], in1=sg[:],
                                                           op=ALU.mult), reads=[aT, sg], pwrites=[hpad])
            cb = COL[("conv_b", l)]
            cg = COL[("conv_ln_g", l)]
            cbb = COL[("conv_ln_b", l)]
            for blk in range(NB):
                t0 = blk * 512
                hc = hc_all[blk % 2]
                pend = []

                def stats(c, hb, sq):
                    P.op("pe", lambda e, c=c, hb=hb: e.matmul(psS[0][:], lhsT=ones_b[:], rhs=hb[:], start=(c == 0),
                                                              stop=(c == 3)),
                         reads=[ones_b, hb], writes=[psS[0]] if c == 0 else (), pwrites=[psS[0]] if c else ())
                    P.op("pe", lambda e, c=c, sq=sq: e.matmul(psS[1][:], lhsT=ones_b[:], rhs=sq[:], start=(c == 0),
                                                              stop=(c == 3)),
                         reads=[ones_b, sq], writes=[psS[1]] if c == 0 else (), pwrites=[psS[1]] if c else ())

                for c in range(4):
                    ps = psC.next()
                    for w in range(CONV_W):
                        P.op("pe", lambda e, c=c, w=w, ps=ps: e.matmul(
                            ps[:], lhsT=dg[:, c * CONV_W + w, :], rhs=hpad[:, c, t0 + w:t0 + w + 512],
                            start=(w == 0), stop=(w == CONV_W - 1)),
                             reads=[dg, hpad], writes=[ps] if w == 0 else (), pwrites=[ps] if w else (),
                             inc=(w == CONV_W - 1))
                    if pend:
                        stats(*pend.pop(0))
                    hb, sq = hb_r.next(), sq_r.next()
                    P.op("dve", lambda e, c=c, ps=ps, hb=hb: e.tensor_scalar(out=hb[:], in0=ps[:], scalar1=cols[:, cb + c:cb + c + 1],
                                                                             scalar2=None, op0=ALU.add),
                         reads=[ps, cols], writes=[hb])
                    P.op("dve", lambda e, c=c, ps=ps: e.tensor_scalar(out=hc[c][:], in0=ps[:], scalar1=cols[:, cb + c:cb + c + 1],
                                                                      scalar2=None, op0=ALU.add),
                         reads=[ps, cols], writes=[hc[c]])
                    P.op("act", lambda e, c=c, sq=sq: e.activation(out=sq[:], in_=hc[c][:], func=AF.Square),
                         reads=[hc[c]], writes=[sq])
                    pend.append((c, hb, sq))
                while pend:
                    stats(*pend.pop(0))
                P.op("dve", lambda e: e.tensor_scalar(out=mean[:], in0=psS[0][:], scalar1=1.0 / 512, scalar2=None,
                                                      op0=ALU.mult), reads=[psS[0]], writes=[mean])
                P.op("pool", lambda e: e.tensor_tensor(out=msq[:], in0=mean[:], in1=mean[:], op=ALU.mult),
                     reads=[mean], writes=[msq])
                P.op("dve", lambda e: e.scalar_tensor_tensor(out=var[:], in0=psS[1][:], scalar=1.0 / 512, in1=msq[:],
                                                             op0=ALU.mult, op1=ALU.subtract),
                     reads=[psS[1], msq], writes=[var])
                P.op("act", lambda e: e.activation(out=rstd[:], in_=var[:], func=AF.Ln, bias=epsc[:, 0:1]),
                     reads=[var, epsc], writes=[rstd])
                P.op("act", lambda e: e.activation(out=rstd[:], in_=rstd[:], func=AF.Exp, scale=-0.5), writes=[rstd])
                for c in range(4):
                    t1, t2, s2 = t1_r.next(), t2_r.next(), s2_r.next()
                    P.op("pool", lambda e, c=c, t1=t1: e.tensor_tensor(out=t1[:], in0=hc[c][:], in1=mean[:], op=ALU.subtract),
                         reads=[hc[c], mean], writes=[t1])
                    P.op("dve", lambda e, t1=t1, t2=t2: e.tensor_tensor(out=t2[:], in0=t1[:], in1=rstd[:], op=ALU.mult),
                         reads=[t1, rstd], writes=[t2])
                    P.op("dve", lambda e, c=c, t2=t2: e.tensor_scalar(out=t2[:], in0=t2[:], scalar1=cols[:, cg + c:cg + c + 1],
                                                                      scalar2=cols[:, cbb + c:cbb + c + 1], op0=ALU.mult,
                                                                      op1=ALU.add),
                         reads=[cols], writes=[t2])
                    P.op("act", lambda e, t2=t2, s2=s2: e.activation(out=s2[:], in_=t2[:], func=AF.Sigmoid),
                         reads=[t2], writes=[s2])
                    P.op("pool", lambda e, c=c, t2=t2, s2=s2: e.tensor_tensor(out=ocv[:, c, t0:t0 + 512], in0=t2[:],
                                                                              in1=s2[:], op=ALU.mult),
                         reads=[t2, s2], pwrites=[ocv])
            for c in range(4):
                P.dma("pool", obrT_d[8 + c], ocv[:, c, :], ocv, reads=[ocv], pwrites=[d_misc])
            st.close()


        if want("F1"):
            st = Stage(P, f"F1{l}")
            w_g = st.sb([128, KC, 3072], BF16, "wg")
            w_br = st.sb([128, 12, D], BF16, "wbr")
            w_o = st.sb([128, KC, D], BF16, "wo")
            wload(w_g, w_g, w_in[l][:, 4096:7168])
            wload(w_br, w_br[:, 0:4, :], w_sb_proj[l])
            wload(w_br, w_br[:, 4:8, :], w_diff_proj[l], first=False)
            wload(w_br, w_br[:, 8:12, :], w_conv_proj[l], first=False)
            wload(w_o, w_o, w_out[l])
            xn_r = Rot([st.sb([128, KC, 512], BF16, "xnb") for _ in range(2)])
            ob_r = Rot([st.sb([128, 12, 512], BF16, "obr") for _ in range(2)])
            yT_r = Rot([st.sb([128, KC, 512], BF16, "yT") for _ in range(2)])
            psG = Rot([st.ps([128, 512], F32, "G") for _ in range(2)])
            psB = Rot([st.ps([128, 512], F32, "B") for _ in range(2)])
            sg_r = Rot([st.sb([128, 512], F32, "sg") for _ in range(2)])
            acc_r = Rot([st.sb([128, 512], F32, "acc") for _ in range(2)])
            tmp_r = Rot([st.sb([128, 512], F32, "tmp") for _ in range(2)])
            tl = Tail(st, 4)
            xsrc = x_in if l == 0 else xres
            yTs = {}

            def f1_main(blk):
                bs = slice(blk * 512, (blk + 1) * 512)
                xnb, obr, yT = xn_r.next(), ob_r.next(), yT_r.next()
                yTs[blk] = yT
                P.dma("sp", xnb[:], xnT_d[:, :, bs].rearrange("c p t -> p c t"), xnb,
                      reads=d_xnT[4 * blk:4 * blk + 4], writes=[xnb])
                P.dma("sp", obr[:], obrT_d[:, :, bs].rearrange("c p t -> p c t"), obr, reads=[d_misc], writes=[obr])
                for m in range(KC):
                    acc = acc_r.next()
                    for br in range(3):
                        Gp, Bp = psG.next(), psB.next()
                        for k in range(KC):
                            P.op("pe", lambda e, k=k, Gp=Gp, br=br: e.matmul(
                                Gp[:], lhsT=w_g[:, k, br * 1024 + m * 128:br * 1024 + (m + 1) * 128], rhs=xnb[:, k, :],
                                start=(k == 0), stop=(k == KC - 1)),
                                 reads=[w_g, xnb], writes=[Gp] if k == 0 else (), pwrites=[Gp] if k else (),
                                 inc=(k == KC - 1))
                        for k in range(4):
                            P.op("pe", lambda e, k=k, Bp=Bp, br=br: e.matmul(
                                Bp[:], lhsT=w_br[:, br * 4 + k, m * 128:(m + 1) * 128], rhs=obr[:, br * 4 + k, :],
                                start=(k == 0), stop=(k == 3)),
                                 reads=[w_br, obr], writes=[Bp] if k == 0 else (), pwrites=[Bp] if k else (),
                                 inc=(k == 3))
                        sg = sg_r.next()
                        P.op("act", lambda e, Gp=Gp, sg=sg: e.activation(out=sg[:], in_=Gp[:], func=AF.Sigmoid),
                             reads=[Gp], writes=[sg])
                        if br == 0:
                            P.op("dve", lambda e, Bp=Bp, sg=sg: e.tensor_tensor(out=acc[:], in0=Bp[:], in1=sg[:], op=ALU.mult),
                                 reads=[Bp, sg], writes=[acc])
                        else:
                            tmp = tmp_r.next()
                            P.op("dve", lambda e, Bp=Bp, sg=sg, tmp=tmp: e.tensor_tensor(out=tmp[:], in0=Bp[:], in1=sg[:],
                                                                                         op=ALU.mult),
                                 reads=[Bp, sg], writes=[tmp])
                            if br == 1:
                                P.op("dve", lambda e, tmp=tmp: e.tensor_tensor(out=acc[:], in0=acc[:], in1=tmp[:], op=ALU.add),
                                     reads=[tmp], writes=[acc])
                            else:
                                P.op("dve", lambda e, tmp=tmp: e.tensor_tensor(out=yT[:, m, :], in0=acc[:], in1=tmp[:],
                                                                               op=ALU.add),
                                     reads=[tmp, acc], pwrites=[yT])

            f1_main(0)
            for blk in range(NB):
                if blk + 1 < NB:
                    f1_main(blk + 1)
                tl.block(4 * blk, 4, yTs.pop(blk), KC, w_o, xsrc, ("ln_xattn", l))
            st.close()

        if want("F2"):
            st = Stage(P, f"F2{l}")
            wkv = st.sb([128, KC, D], BF16, "wkv")
            wq = st.sb([128, KC, 512], BF16, "wq")
            wo = st.sb([128, 4, D], BF16, "wo")
            wload(wkv, wkv, xa_w_kv[l])
            wload(wq, wq, xa_w_q[l])
            wload(wo, wo, xa_w_o[l])
            tl = Tail(st, 4)
            memT = st.sb([128, KC, MEM], BF16, "memT")
            kTx = st.sb([128, 4, MEM], BF16, "kTx")
            vx = st.sb([128, 2, 512], BF16, "vx")
            mt_r = Rot([st.sb([128, D], F32, "mt") for _ in range(2)])
            psZ = Rot([st.ps([128, 512], F32, "Z") for _ in range(2)])
            psO = st.ps([128, 512], F32, "O")
            psD = st.ps([128, 512], F32, "Dn")
            for mb in range(2):
                mt = mt_r.next()
                P.dma("sp", mt[:], mem_in[mb * 128:(mb + 1) * 128, :], mt, writes=[mt])
                tl.nt.run(mt, ("ln_mem", l), memT, slice(mb * 128, (mb + 1) * 128))
            for h in range(4):
                ps = psZ.next()
                for k in range(KC):
                    P.op("pe", lambda e, k=k, ps=ps, h=h: e.matmul(ps[:, 0:MEM], lhsT=wkv[:, k, h * 128:(h + 1) * 128],
                                                                   rhs=memT[:, k, :], start=(k == 0), stop=(k == KC - 1)),
                         reads=[wkv, memT], writes=[ps] if k == 0 else (), pwrites=[ps] if k else (), inc=(k == KC - 1))
                P.op("dve", lambda e, ps=ps, h=h: e.tensor_copy(out=kTx[:, h, :], in_=ps[:, 0:MEM]), reads=[ps],
                     pwrites=[kTx])
            for mb in range(2):
                ps = psZ.next()
                for k in range(KC):
                    P.op("pe", lambda e, k=k, ps=ps, mb=mb: e.matmul(ps[:], lhsT=memT[:, k, mb * 128:(mb + 1) * 128],
                                                                     rhs=wkv[:, k, 512:1024], start=(k == 0),
                                                                     stop=(k == KC - 1)),
                         reads=[wkv, memT], writes=[ps] if k == 0 else (), pwrites=[ps] if k else (), inc=(k == KC - 1))
                P.op("dve", lambda e, ps=ps, mb=mb: e.tensor_copy(out=vx[:, mb, :], in_=ps[:]), reads=[ps], pwrites=[vx])
            xn_r = Rot([st.sb([128, KC, 512], BF16, "xnb") for _ in range(2)])
            qTx_r = Rot([st.sb([128, 4, 512], BF16, "qTx") for _ in range(2)])
            oTx_r = Rot([st.sb([128, 4, 512], BF16, "oTx") for _ in range(2)])
            A_r = Rot([st.sb([128, 512], BF16, "A") for _ in range(4)])
            rec_r = Rot([st.sb([128, 512], F32, "rec") for _ in range(2)])
            xscale = 1.0 / math.sqrt(128.0)
            oTxs = {}

            def f2_main(blk):
                bs = slice(blk * 512, (blk + 1) * 512)
                xnb, qTx, oTx = xn_r.next(), qTx_r.next(), oTx_r.next()
                oTxs[blk] = oTx
                P.dma("sp", xnb[:], xnT_d[:, :, bs].rearrange("c p t -> p c t"), xnb,
                      reads=d_xnT[4 * blk:4 * blk + 4], writes=[xnb])
                for h in range(4):
                    ps = psZ.next()
                    for k in range(KC):
                        P.op("pe", lambda e, k=k, ps=ps, h=h: e.matmul(ps[:], lhsT=wq[:, k, h * 128:(h + 1) * 128],
                                                                       rhs=xnb[:, k, :], start=(k == 0), stop=(k == KC - 1)),
                             reads=[wq, xnb], writes=[ps] if k == 0 else (), pwrites=[ps] if k else (), inc=(k == KC - 1))
                    P.op("dve", lambda e, ps=ps, h=h: e.tensor_copy(out=qTx[:, h, :], in_=ps[:]), reads=[ps], pwrites=[qTx])
                for h in range(4):
                    As = []
                    for mb in range(2):
                        Z = psZ.next()
                        P.op("pe", lambda e, Z=Z, h=h, mb=mb: e.matmul(Z[:], lhsT=kTx[:, h, mb * 128:(mb + 1) * 128],
                                                                       rhs=qTx[:, h, :], start=True, stop=True),
                             reads=[kTx, qTx], writes=[Z])
                        A = A_r.next()
                        P.op("act", lambda e, Z=Z, A=A: e.activation(out=A[:], in_=Z[:], func=AF.Exp, scale=xscale),
                             reads=[Z], writes=[A])
                        As.append(A)
                    for mb in range(2):
                        P.op("pe", lambda e, h=h, mb=mb: e.matmul(psO[:], lhsT=vx[:, mb, h * 128:(h + 1) * 128],
                                                                  rhs=As[mb][:], start=(mb == 0), stop=(mb == 1)),
                             reads=[vx, As[mb]], writes=[psO] if mb == 0 else (), pwrites=[psO] if mb else ())
                    for mb in range(2):
                        P.op("pe", lambda e, mb=mb: e.matmul(psD[:], lhsT=ones_b[:], rhs=As[mb][:], start=(mb == 0),
                                                             stop=(mb == 1)),
                             reads=[ones_b, As[mb]], writes=[psD] if mb == 0 else (), pwrites=[psD] if mb else ())
                    rec = rec_r.next()
                    P.op("dve", lambda e, rec=rec: e.reciprocal(out=rec[:], in_=psD[:]), reads=[psD], writes=[rec])
                    P.op("dve", lambda e, rec=rec, h=h: e.tensor_tensor(out=oTx[:, h, :], in0=psO[:], in1=rec[:], op=ALU.mult),
                         reads=[psO, rec], pwrites=[oTx])

            f2_main(0)
            for blk in range(NB):
                if blk + 1 < NB:
                    f2_main(blk + 1)
                tl.block(4 * blk, 4, oTxs.pop(blk), 4, wo, xres, ("ln_mlp", l))
            st.close()

        if want("G"):
            st = Stage(P, f"G{l}")
            wu = st.sb([128, KC, DFF], BF16, "wu")
            wd = st.sb([128, 32, D], BF16, "wd")
            wload(wu, wu, w_up[l])
            wload(wd, wd, w_down[l])
            last = (l == nlayers - 1)
            tl = Tail(st, 2, final=last)
            xn_r = Rot([st.sb([128, KC, 256], BF16, "xnb") for _ in range(2)])
            hT = st.sb([128, 32, 256], BF16, "hT")
            r_r = Rot([st.sb([128, 256], F32, "r") for _ in range(2)])
            psU = Rot([st.ps([128, 512], F32, "U") for _ in range(3)])
            for b2 in range(S // 256):
                bs = slice(b2 * 256, (b2 + 1) * 256)
                xnb = xn_r.next()
                P.dma("sp", xnb[:], xnT_d[:, :, bs].rearrange("c p t -> p c t"), xnb,
                      reads=d_xnT[2 * b2:2 * b2 + 2], writes=[xnb])
                for f in range(32):
                    ps = psU.next()
                    for k in range(KC):
                        P.op("pe", lambda e, k=k, ps=ps, f=f: e.matmul(ps[:, 0:256], lhsT=wu[:, k, f * 128:(f + 1) * 128],
                                                                       rhs=xnb[:, k, :], start=(k == 0), stop=(k == KC - 1)),
                             reads=[wu, xnb], writes=[ps] if k == 0 else (), pwrites=[ps] if k else (), inc=(k == KC - 1))
                    r = r_r.next()
                    P.op("dve", lambda e, ps=ps, r=r: e.tensor_scalar(out=r[:], in0=ps[:, 0:256], scalar1=0.0, scalar2=None,
                                                                      op0=ALU.max), reads=[ps], writes=[r])
                    P.op("act", lambda e, r=r, f=f: e.activation(out=hT[:, f, :], in_=r[:], func=AF.Square),
                         reads=[r], pwrites=[hT])
                tl.block(2 * b2, 2, hT, 32, wd, xres, ("ln_mix", l + 1) if not last else None, final=last)
            st.close()

    P.barrier()
    G.es.close()
    return nc, P


INPUT_NAMES = ["x", "mem", "rel_bias", "ln_mix", "w_in", "diff_lambda", "diff_subln", "conv_w", "conv_b",
               "conv_ln_g", "conv_ln_b", "w_sb_proj", "w_diff_proj", "w_conv_proj", "w_out", "ln_xattn", "ln_mem",
               "xa_w_q", "xa_w_kv", "xa_w_o", "ln_mlp", "w_up", "w_down", "ln_final"]


def make_in_maps(inputs, S, ncores=8):
    shared = {k: np.ascontiguousarray(np.asarray(inputs[k], dtype=np.float32)) for k in INPUT_NAMES
              if k not in ("x", "mem")}
    maps = []
    for b in range(ncores):
        m = dict(shared)
        m["x"] = np.ascontiguousarray(np.asarray(inputs["x"][b, :S], dtype=np.float32))
        m["mem"] = np.ascontiguousarray(np.asarray(inputs["mem"][b], dtype=np.float32))
        maps.append(m)
    return maps


def kernel(**inputs):
    S = 4096
    nc, _ = build(S)
    maps = make_in_maps(inputs, S)
    res = run_bass_kernel_spmd(nc, maps, core_ids=list(range(8)))
    return np.stack([np.asarray(r["y"], dtype=np.float32) for r in res.results], axis=0)
```

```python
import math
from contextlib import ExitStack
import numpy as np
import concourse.bass as bass
import concourse.mybir as mybir
from concourse.bass_utils import run_bass_kernel_spmd

F32 = mybir.dt.float32
BF16 = mybir.dt.bfloat16
AF = mybir.ActivationFunctionType
ALU = mybir.AluOpType
AX = mybir.AxisListType

D = 1024
NL = 2
KC = 8
IN_COLS = 7168
DFF = 4096
MEM = 256
EPS = 1e-6
NEG = -30000.0
NUM_BUCKETS = 32
CONV_W = 31
HALO = CONV_W - 1


def _bucket(n):
    if n < 16:
        return n
    v = 16 + int(np.float32(np.log(np.float32(n) / np.float32(16)) / np.float32(math.log(8.0)) * np.float32(16)))
    return min(v, 31)


class Buf:
    __slots__ = ("t", "w", "pw", "r", "dsem", "dcnt", "name")

    def __init__(self, t=None, name=""):
        self.t = t
        self.w = {}
        self.pw = {}
        self.r = {}
        self.dsem = {}
        self.dcnt = {}
        self.name = name

    def __getitem__(self, k):
        return self.t[k]


class Prog:
    ENGS = ("pe", "act", "dve", "pool", "sp")
    ROLL = 30000

    def __init__(self, nc):
        self.nc = nc
        self.eng = {"pe": nc.tensor, "act": nc.scalar, "dve": nc.vector, "pool": nc.gpsimd, "sp": nc.sync}
        self.sem = {}
        self.cnt = {}
        self.seen = {e: {} for e in self.ENGS}
        self.pending = {e: [] for e in self.ENGS}
        self.allsems = {}
        self.free_dsems = {"hw": [], "sw": []}
        self.nsem = 0
        self.ninst = 0
        for e in self.ENGS:
            self._new_eng_sem(e)

    def _alloc_sem(self, name):
        self.nsem += 1
        s = self.nc.alloc_semaphore(f"{name}_{self.nsem}")
        self.allsems[s.num] = [s, 0]
        return s

    def _new_eng_sem(self, e):
        self.sem[e] = self._alloc_sem("e" + e)
        self.cnt[e] = 0

    def _get_dsem(self, kind):
        if self.free_dsems[kind]:
            return self.free_dsems[kind].pop()
        return (self._alloc_sem("d" + kind), 0)

    def _waits(self, eng, reads, writes, pwrites):
        waits = {}

        def add(d):
            for k, ev in d.items():
                if k not in waits or waits[k][1] < ev[1]:
                    waits[k] = ev

        for b in reads:
            add(b.w)
            add(b.pw)
        for b in writes:
            add(b.w)
            add(b.pw)
            add(b.r)
        for b in pwrites:
            add(b.w)
            add(b.r)
        e = self.eng[eng]
        seen = self.seen[eng]
        for k, (s, v) in waits.items():
            if eng == "pe" and k == self.sem["pe"].num:
                continue
            if seen.get(k, 0) >= v:
                continue
            seen[k] = v
            e.wait_ge(s, v)

    def _apply(self, key, ev, reads, writes, pwrites):
        for b in reads:
            b.r[key] = ev
        for b in writes:
            b.w = {key: ev}
            b.pw = {}
            b.r = {}
        for b in pwrites:
            b.pw[key] = ev

    def op(self, eng, fn, reads=(), writes=(), pwrites=(), inc=True):
        self._waits(eng, reads, writes, pwrites)
        ins = fn(self.eng[eng])
        self.ninst += 1
        if not inc:
            self.pending[eng].append((reads, writes, pwrites))
            return
        if self.cnt[eng] >= self.ROLL:
            self._new_eng_sem(eng)
        self.cnt[eng] += 1
        sem = self.sem[eng]
        ins.then_inc(sem, 1)
        self.allsems[sem.num][1] = self.cnt[eng]
        ev = (sem, self.cnt[eng])
        self._apply(sem.num, ev, reads, writes, pwrites)
        for (r, w, pw) in self.pending[eng]:
            self._apply(sem.num, ev, r, w, pw)
        self.pending[eng] = []

    def dma(self, q, out, in_, sb, reads=(), writes=(), pwrites=(), **kw):
        self._waits(q, reads, writes, pwrites)
        kind = "sw" if q == "pool" else "hw"
        if kind not in sb.dsem:
            sb.dsem[kind], sb.dcnt[kind] = self._get_dsem(kind)
        ins = self.eng[q].dma_start(out=out, in_=in_, **kw)
        self.ninst += 1
        sb.dcnt[kind] += 16
        assert sb.dcnt[kind] < 60000
        sem = sb.dsem[kind]
        ins.then_inc(sem, 16)
        self.allsems[sem.num][1] = sb.dcnt[kind]
        ev = (sem, sb.dcnt[kind])
        self._apply(sem.num, ev, reads, writes, pwrites)

    def release(self, bufs):
        for b in bufs:
            for kind, sem in b.dsem.items():
                self.free_dsems[kind].append((sem, b.dcnt[kind]))
            b.dsem = {}
            b.dcnt = {}

    def barrier(self, engs=None):
        for e in (engs or self.ENGS):
            assert not self.pending[e]
            seen = self.seen[e]
            for k, (s, v) in self.allsems.items():
                if v > 0 and seen.get(k, 0) < v:
                    seen[k] = v
                    self.eng[e].wait_ge(s, v)


class Stage:
    def __init__(self, P, name):
        self.P = P
        self.nc = P.nc
        self.name = name
        self.es = ExitStack()
        self.bufs = []
        self.n = 0

    def sb(self, shape, dtype, name="t"):
        self.n += 1
        t = self.es.enter_context(self.nc.sbuf_tensor(f"{self.name}_{name}{self.n}", list(shape), dtype))
        b = Buf(t, f"{self.name}_{name}")
        self.bufs.append(b)
        return b

    def ps(self, shape, dtype=F32, name="p"):
        self.n += 1
        t = self.es.enter_context(self.nc.psum_tensor(f"{self.name}_{name}{self.n}", list(shape), dtype))
        b = Buf(t, f"{self.name}_{name}")
        self.bufs.append(b)
        return b

    def close(self):
        self.P.barrier()
        self.P.release(self.bufs)
        self.es.close()


class Rot:
    def __init__(self, items):
        self.items = items
        self.i = 0

    def next(self):
        b = self.items[self.i % len(self.items)]
        self.i += 1
        return b


def build(S, debug=False, stages=None, nlayers=NL):
    NT = S // 128
    NB = S // 512
    nc = bass.Bass("TRN2", target_bir_lowering=False)
    P = Prog(nc)

    def dram_in(name, shape):
        return nc.dram_tensor(name, list(shape), F32, kind="ExternalInput").ap()

    okind = "ExternalOutput" if debug else "Internal"

    def scratch(name, shape, dtype):
        return nc.dram_tensor(name, list(shape), dtype, kind=okind).ap()

    x_in = dram_in("x", [S, D])
    mem_in = dram_in("mem", [MEM, D])
    rel_bias = dram_in("rel_bias", [NUM_BUCKETS, 4])
    ln_mix = dram_in("ln_mix", [NL, D])
    w_in = dram_in("w_in", [NL, D, IN_COLS])
    diff_lambda = dram_in("diff_lambda", [NL, 4, 64])
    diff_subln = dram_in("diff_subln", [NL, 128])
    conv_w = dram_in("conv_w", [NL, CONV_W, 1, 512])
    conv_b = dram_in("conv_b", [NL, 512])
    conv_ln_g = dram_in("conv_ln_g", [NL, 512])
    conv_ln_b = dram_in("conv_ln_b", [NL, 512])
    w_sb_proj = dram_in("w_sb_proj", [NL, 512, D])
    w_diff_proj = dram_in("w_diff_proj", [NL, 512, D])
    w_conv_proj = dram_in("w_conv_proj", [NL, 512, D])
    w_out = dram_in("w_out", [NL, D, D])
    ln_xattn = dram_in("ln_xattn", [NL, D])
    ln_mem = dram_in("ln_mem", [NL, D])
    xa_w_q = dram_in("xa_w_q", [NL, D, 512])
    xa_w_kv = dram_in("xa_w_kv", [NL, D, D])
    xa_w_o = dram_in("xa_w_o", [NL, 512, D])
    ln_mlp = dram_in("ln_mlp", [NL, D])
    w_up = dram_in("w_up", [NL, D, DFF])
    w_down = dram_in("w_down", [NL, DFF, D])
    ln_final = dram_in("ln_final", [D])
    y_out = nc.dram_tensor("y", [S, D], F32, kind="ExternalOutput").ap()

    xres = scratch("xres", [S, D], F32)
    xnT_d = scratch("xnT", [KC, 128, S], BF16)
    featT_d = scratch("featT", [24, 128, S], BF16)
    vtok_d = scratch("vtok", [2, S, 512], BF16)
    obrT_d = scratch("obrT", [12, 128, S], BF16)
    def wload(dst, dst_ap, src2d, first=True):
        K = src2d.shape[0]
        for k in range(K // 128):
            P.dma("pool", dst_ap[:, k, :], src2d[k * 128:(k + 1) * 128, :], dst,
                  writes=[dst] if (first and k == 0) else (), pwrites=() if (first and k == 0) else [dst])

    d_xres = [Buf(name=f"xres{i}") for i in range(NT)]
    d_xnT = [Buf(name=f"xnT{i}") for i in range(NT)]
    d_misc = Buf(name="dmisc")

    def want(s):
        return stages is None or s in stages

    G = Stage(P, "g")
    ident_f = G.sb([128, 128], F32, "identf")
    ident_b = G.sb([128, 128], BF16, "identb")
    ones_f = G.sb([128, 512], F32, "onesf")
    ones_b = G.sb([128, 128], BF16, "onesb")
    U_b = G.sb([128, 128], BF16, "U")
    LU_b = G.sb([128, 128], BF16, "LU")
    cols = G.sb([128, 128], F32, "cols")
    epsc = G.sb([128, 1], F32, "eps")

    P.op("pool", lambda e: e.memset(ones_f[:], 1.0), writes=[ones_f])
    P.op("pool", lambda e: e.memset(epsc[:], EPS), writes=[epsc])
    P.op("dve", lambda e: e.tensor_copy(out=ones_b[:], in_=ones_f[:, 0:128]), reads=[ones_f], writes=[ones_b])
    P.op("pool", lambda e: e.affine_select(out=ident_f[:], in_=ones_f[:, 0:128], pattern=[[1, 128]],
                                           compare_op=ALU.is_equal, fill=0.0, base=0, channel_multiplier=-1),
         reads=[ones_f], writes=[ident_f])
    P.op("dve", lambda e: e.tensor_copy(out=ident_b[:], in_=ident_f[:]), reads=[ident_f], writes=[ident_b])
    P.op("pool", lambda e: e.affine_select(out=U_b[:], in_=ones_f[:, 0:128], pattern=[[-1, 128]],
                                           compare_op=ALU.is_ge, fill=0.0, base=0, channel_multiplier=1),
         reads=[ones_f], writes=[U_b])
    P.op("pool", lambda e: e.affine_select(out=LU_b[:], in_=ones_f[:, 0:128], pattern=[[1, 128]],
                                           compare_op=ALU.is_gt, fill=0.0, base=0, channel_multiplier=-1),
         reads=[ones_f], writes=[LU_b])
    COL = {}
    with ExitStack() as es0:
        rows_t = es0.enter_context(nc.sbuf_tensor("rows", [128, 128], F32))
        rows = Buf(rows_t, "rows")
        pst = Buf(es0.enter_context(nc.psum_tensor("rowsT", [128, 128], F32)))
        P.op("pool", lambda e: e.memset(rows[:], 0.0), writes=[rows])
        r0 = 0

        def addrows(key, ap, n):
            nonlocal r0
            COL[key] = r0
            P.dma("sp", rows[r0:r0 + n, :], ap.rearrange("(c p) -> c p", p=128), rows, pwrites=[rows])
            r0 += n

        for l in range(NL):
            addrows(("ln_mix", l), ln_mix[l], 8)
            addrows(("ln_xattn", l), ln_xattn[l], 8)
            addrows(("ln_mem", l), ln_mem[l], 8)
            addrows(("ln_mlp", l), ln_mlp[l], 8)
            addrows(("conv_b", l), conv_b[l], 4)
            addrows(("conv_ln_g", l), conv_ln_g[l], 4)
            addrows(("conv_ln_b", l), conv_ln_b[l], 4)
            addrows(("subln", l), diff_subln[l], 1)
        assert r0 <= 128
        P.op("pe", lambda e: e.transpose(out=pst[:], in_=rows[:], identity=ident_f[:]),
             reads=[rows, ident_f], writes=[pst])
        P.op("dve", lambda e: e.tensor_copy(out=cols[:], in_=pst[:]), reads=[pst], writes=[cols])
        P.barrier()

    def colap(key, c=0, n=1):
        b = COL[key] + c
        return cols[:, b:b + n]

    class NormT:
        def __init__(self, st, nxnb=2):
            self.junk = Rot([st.sb([128, D], BF16, "junk") for _ in range(2)])
            self.xnb = Rot([st.sb([128, D], BF16, "xnb") for _ in range(nxnb)])
            self.stat = Rot([st.sb([128, 4], F32, "stat") for _ in range(max(4, nxnb + 2))])
            self.pT = Rot([st.ps([128, KC, 128], BF16, "pT") for _ in range(2)])

        def rstd(self, xt, width=D):
            stt = self.stat.next()
            junk = self.junk.next()
            P.op("act", lambda e: e.activation(out=junk[:, 0:width], in_=xt[:, 0:width], func=AF.Square,
                                               accum_out=stt[:, 0:1]),
                 reads=[xt], writes=[junk, stt])
            P.op("act", lambda e: e.activation(out=stt[:, 1:2], in_=stt[:, 0:1], func=AF.Ln, bias=epsc[:, 0:1],
                                               scale=1.0 / width),
                 reads=[epsc], writes=[stt])
            P.op("act", lambda e: e.activation(out=stt[:, 2:3], in_=stt[:, 1:2], func=AF.Exp, scale=-0.5),
                 writes=[stt])
            return stt

        def prep(self, xt):
            stt = self.rstd(xt)
            xnb = self.xnb.next()
            P.op("act", lambda e: e.activation(out=xnb[:], in_=xt[:], func=AF.Copy, scale=stt[:, 2:3]),
                 reads=[xt, stt], writes=[xnb])
            return xnb

        def finish(self, xnb, gkey, dst, dst_cols):
            pT = self.pT.next()
            for c in range(KC):
                P.op("pe", lambda e, c=c: e.transpose(out=pT[:, c, :], in_=xnb[:, c * 128:(c + 1) * 128],
                                                      identity=ident_b[:]),
                     reads=[xnb, ident_b], pwrites=[pT] if c else (), writes=() if c else [pT], inc=(c == KC - 1))
            g = colap(gkey, 0, KC)
            P.op("dve", lambda e: e.tensor_tensor(out=dst[:, :, dst_cols], in0=pT[:, :, :],
                                                  in1=g.unsqueeze(2).to_broadcast([128, KC, 128]), op=ALU.mult),
                 reads=[pT, cols], pwrites=[dst])

        def run(self, xt, gkey, dst, dst_cols):
            self.finish(self.prep(xt), gkey, dst, dst_cols)

    class Tail:
        def __init__(self, st, ntiles, final=False, nps=2):
            self.nt = NormT(st, nxnb=ntiles)
            self.xt = [st.sb([128, D], F32, "xt") for _ in range(ntiles)]
            self.ps_r = Rot([st.ps([128, 512], F32, "tp") for _ in range(nps)])
            self.ob_r = Rot([st.sb([128, KC, 128], BF16, "tob") for _ in range(2)])
            self.yt_r = Rot([st.sb([128, D], F32, "yt") for _ in range(2)]) if final else None
            self.gfin = None
            if final:
                self.gfin = st.sb([128, D], F32, "gfin")
                P.dma("sp", self.gfin[:], ln_final.partition_broadcast(128), self.gfin, writes=[self.gfin])

        def block(self, t0, ntiles, lhs, K, w, xsrc, gkey_next, final=False):
            xnbs = []
            for tt in range(ntiles):
                t = t0 + tt
                tcol = tt * 128
                xt = self.xt[tt]
                rows = slice(t * 128, (t + 1) * 128)
                P.dma("sp", xt[:], xsrc[rows, :], xt, reads=[d_xres[t]], writes=[xt])
                for half in range(2):
                    ps = self.ps_r.next()
                    hsl = slice(half * 512, (half + 1) * 512)
                    for k in range(K):
                        P.op("pe", lambda e, k=k, ps=ps, hsl=hsl, tcol=tcol: e.matmul(
                            ps[:], lhsT=lhs[:, k, tcol:tcol + 128], rhs=w[:, k, hsl], start=(k == 0), stop=(k == K - 1)),
                             reads=[lhs, w], writes=[ps] if k == 0 else (), pwrites=[ps] if k else (), inc=(k == K - 1))
                    P.op("dve", lambda e, ps=ps, hsl=hsl, xt=xt: e.tensor_tensor(out=xt[:, hsl], in0=xt[:, hsl], in1=ps[:],
                                                                                 op=ALU.add),
                         reads=[ps, xt], pwrites=[xt])
                if final:
                    stt = self.nt.rstd(xt)
                    yt = self.yt_r.next()
                    P.op("dve", lambda e, xt=xt, stt=stt, yt=yt: e.scalar_tensor_tensor(
                        out=yt[:], in0=xt[:], scalar=stt[:, 2:3], in1=self.gfin[:], op0=ALU.mult, op1=ALU.mult),
                         reads=[xt, stt, self.gfin], writes=[yt])
                    P.dma("pool", y_out[rows, :], yt[:], yt, reads=[yt], pwrites=[d_misc])
                else:
                    P.dma("pool", xres[rows, :], xt[:], xt, reads=[xt], writes=[d_xres[t]])
                    xnbs.append(self.nt.prep(xt))
            if not final:
                for tt in range(ntiles):
                    t = t0 + tt
                    rows = slice(t * 128, (t + 1) * 128)
                    ob = self.ob_r.next()
                    self.nt.finish(xnbs[tt], gkey_next, ob, slice(0, 128))
                    P.dma("pool", xnT_d[:, :, rows].rearrange("c p t -> p c t"), ob[:], ob, reads=[ob],
                          writes=[d_xnT[t]])

    gbias_d = nc.dram_tensor("gbias", [4, 128, 1024], F32, kind="Internal").ap()
    tb = G.sb([128, 128], F32, "tb")
    P.dma("sp", tb[:], rel_bias.rearrange("b h -> (b h)").partition_broadcast(128), tb, writes=[tb])
    thr = [0] * 32
    for n in range(0, 200):
        b = _bucket(n)
        for k in range(1, b + 1):
            if thr[k] == 0:
                thr[k] = n
    if want("A0"):
        st = Stage(P, "A0")
        nt = NormT(st)
        xt_r = Rot([st.sb([128, D], F32, "xt") for _ in range(3)])
        ob_r = Rot([st.sb([128, KC, 128], BF16, "ob") for _ in range(3)])
        for t in range(NT):
            xt = xt_r.next()
            P.dma("sp", xt[:], x_in[t * 128:(t + 1) * 128, :], xt, writes=[xt])
            ob = ob_r.next()
            nt.run(xt, ("ln_mix", 0), ob, slice(0, 128))
            P.dma("pool", xnT_d[:, :, t * 128:(t + 1) * 128].rearrange("c p t -> p c t"), ob[:], ob,
                  reads=[ob], writes=[d_xnT[t]])
        st.close()

    for l in range(nlayers):
        lam_init = 0.8 - 0.6 * math.exp(-0.3 * l)

        if want("B"):
            st = Stage(P, f"B{l}")
            xn_all = st.sb([128, KC, S], BF16, "xnall")
            for c in range(KC):
                P.dma("sp", xn_all[:, c, :], xnT_d[c], xn_all, reads=d_xnT, pwrites=[xn_all])
            wb_r = Rot([st.sb([128, KC, 512], BF16, "wb") for _ in range(2)])
            ps_r = Rot([st.ps([128, 512], F32, "ps") for _ in range(4)])
            ob_r = Rot([st.sb([128, 512], BF16, "ob") for _ in range(4)])
            do_gb = (l == 0)
            ev_r = Rot(["act"] if do_gb else ["act", "dve"])
            if do_gb:
                relt = st.sb([128, 1024], F32, "rel")
                dtb = st.sb([128, 128], F32, "dtb")
                acc = st.sb([128, 1024], F32, "acc")
                tmp_r = Rot([st.sb([128, 1024], F32, "tmp") for _ in range(2)])
                gout_r = Rot([st.sb([128, 1024], F32, "gout") for _ in range(2)])
                P.op("pool", lambda e: e.iota(relt[:], pattern=[[1, 1024]], base=-384, channel_multiplier=-1,
                                              allow_small_or_imprecise_dtypes=True), writes=[relt])
                P.op("dve", lambda e: e.tensor_sub(out=dtb[:, 4:128], in0=tb[:, 4:128], in1=tb[:, 0:124]), reads=[tb],
                     writes=[dtb])
                pending_store = []

                def gb_head(h):
                    bs_ = slice(384, 640)
                    P.op("dve", lambda e: e.tensor_scalar(out=acc[:, bs_], in0=relt[:, bs_], scalar1=0.0, scalar2=tb[:, h:h + 1],
                                                          op0=ALU.mult, op1=ALU.add), reads=[relt, tb], writes=[acc])
                    for k in range(1, 32):
                        tmp = tmp_r.next()
                        P.op("dve", lambda e: e.tensor_scalar(out=tmp[:, bs_], in0=relt[:, bs_], scalar1=float(thr[k]) - 0.5,
                                                              scalar2=dtb[:, 4 * k + h:4 * k + h + 1], op0=ALU.is_ge,
                                                              op1=ALU.mult), reads=[relt, dtb], writes=[tmp])
                        P.op("dve", lambda e: e.tensor_tensor(out=acc[:, bs_], in0=acc[:, bs_], in1=tmp[:, bs_], op=ALU.add),
                             reads=[tmp], writes=[acc])
                    gout = gout_r.next()
                    P.op("pool", lambda e: e.memset(gout[:, 0:384], NEG), writes=[gout])
                    P.op("dve", lambda e: e.tensor_scalar(out=gout[:, 640:1024], in0=relt[:, 640:1024], scalar1=0.0,
                                                          scalar2=tb[:, 124 + h:125 + h], op0=ALU.mult, op1=ALU.add),
                         reads=[relt, tb], pwrites=[gout])
                    P.op("pool", lambda e: e.affine_select(out=gout[:, bs_], in_=acc[:, bs_], pattern=[[1, 256]],
                                                           compare_op=ALU.is_ge, fill=NEG, base=0, channel_multiplier=-1),
                         reads=[acc], pwrites=[gout])
                    pending_store.append((h, gout))

                def gb_flush():
                    while pending_store:
                        h, gout = pending_store.pop(0)
                        P.dma("sp", gbias_d[h], gout[:], gout, reads=[gout], pwrites=[d_misc])


            def evac(ps, ob):
                ce = ev_r.next()
                if ce == "act":
                    P.op("act", lambda e: e.copy(out=ob[:], in_=ps[:]), reads=[ps], writes=[ob])
                else:
                    P.op("dve", lambda e: e.tensor_copy(out=ob[:], in_=ps[:]), reads=[ps], writes=[ob])

            groups = [("f", 0, 0), ("f", 512, 4), ("f", 1536, 8), ("f", 2048, 12), ("f", 3072, 16), ("f", 3584, 20),
                      ("t", 1024, 0), ("t", 2560, 1)]
            wbs = []

            def issue_w(gi):
                wb = wb_r.next()
                c0 = groups[gi][1]
                wload(wb, wb, w_in[l][:, c0:c0 + 512])
                wbs.append(wb)

            issue_w(0)
            for gi, (kind, c0, fb) in enumerate(groups):
                if gi + 1 < len(groups):
                    issue_w(gi + 1)
                if do_gb and gi < 4:
                    gb_flush()
                    gb_head(gi)
                if do_gb and gi == 5:
                    gb_flush()
                wb = wbs[gi]
                if kind == "f":
                    for blk in range(NB):
                        for m in range(4):
                            ps = ps_r.next()
                            for k in range(KC):
                                P.op("pe", lambda e, k=k, m=m, blk=blk, ps=ps, wb=wb: e.matmul(
                                    ps[:], lhsT=wb[:, k, m * 128:(m + 1) * 128],
                                    rhs=xn_all[:, k, blk * 512:(blk + 1) * 512], start=(k == 0), stop=(k == KC - 1)),
                                     reads=[wb, xn_all], writes=[ps] if k == 0 else (), pwrites=[ps] if k else (),
                                     inc=(k == KC - 1))
                            ob = ob_r.next()
                            evac(ps, ob)
                            P.dma("sp", featT_d[fb + m, :, blk * 512:(blk + 1) * 512], ob[:], ob, reads=[ob],
                                  pwrites=[d_misc])
                else:
                    for t in range(NT):
                        ps = ps_r.next()
                        for k in range(KC):
                            P.op("pe", lambda e, k=k, t=t, ps=ps, wb=wb: e.matmul(
                                ps[:], lhsT=xn_all[:, k, t * 128:(t + 1) * 128], rhs=wb[:, k, :],
                                start=(k == 0), stop=(k == KC - 1)),
                                 reads=[wb, xn_all], writes=[ps] if k == 0 else (), pwrites=[ps] if k else (),
                                 inc=(k == KC - 1))
                        ob = ob_r.next()
                        evac(ps, ob)
                        P.dma("sp", vtok_d[fb, t * 128:(t + 1) * 128, :], ob[:], ob, reads=[ob], pwrites=[d_misc])
            st.close()


        if want("C"):
            st = Stage(P, f"C{l}")
            qz_r = Rot([[st.sb([128, S], BF16, "qz") for _ in range(2)] for _ in range(2)])
            kT_r = Rot([st.sb([128, S], BF16, "kT") for _ in range(2)])
            vz_r = Rot([[st.sb([128, NT, 128], BF16, "vz") for _ in range(2)] for _ in range(2)])
            oT_r = Rot([st.sb([128, S], BF16, "oT") for _ in range(2)])
            for pair in qz_r.items + vz_r.items:
                for bz in pair:
                    P.op("pool", lambda e, bz=bz: e.memset(bz[:], 0.0), writes=[bz])
            masks = st.sb([128, 4, 512], F32, "masks")
            for r in range(4):
                P.op("pool", lambda e, r=r: e.affine_select(out=masks[:, r, :], in_=ones_f[:], pattern=[[1, 512]],
                                                             compare_op=ALU.is_gt, fill=0.0, base=-128 * r,
                                                             channel_multiplier=-1),
                     reads=[ones_f], writes=[masks] if r == 0 else (), pwrites=[masks] if r else ())
            psZ = Rot([st.ps([128, 512], F32, "Z") for _ in range(4)])
            psP = [st.ps([128, 512], F32, "P") for _ in range(2)]
            psO = Rot([st.ps([128, 512], F32, "O") for _ in range(2)])
            E_r = Rot([st.sb([128, 512], F32, "E") for _ in range(4)])
            SP_r = Rot([st.sb([128, 512], BF16, "SP") for _ in range(4)])
            X_r = Rot([st.sb([128, 512], F32, "X") for _ in range(4)])
            A_r = Rot([st.sb([128, 512], BF16, "A") for _ in range(4)])
            masks_b = st.sb([128, 4, 512], BF16, "masksb")
            P.op("dve", lambda e: e.tensor_copy(out=masks_b[:], in_=masks[:]), reads=[masks], writes=[masks_b])
            pair_res = {}

            def load_pair(j):
                if j >= 4 or j in pair_res:
                    return
                qz, kT, vz, oT = qz_r.next(), kT_r.next(), vz_r.next(), oT_r.next()
                P.dma("sp", kT[:], featT_d[4 + j], kT, reads=[d_misc], writes=[kT])
                for hh in range(2):
                    hs_ = slice(64 * hh, 64 * hh + 64)
                    P.dma("sp", qz[hh][hs_, :], featT_d[j, hs_, :], qz[hh], reads=[d_misc], pwrites=[qz[hh]])
                    P.dma("sp", vz[hh][:, :, hs_],
                          vtok_d[0, :, j * 128 + 64 * hh:j * 128 + 64 * hh + 64].rearrange("(t p) c -> p t c", p=128),
                          vz[hh], reads=[d_misc], pwrites=[vz[hh]])
                pair_res[j] = (qz, kT, vz, oT)

            steps = [(j, i, s_) for j in range(4) for i in range(NB) for s_ in range(4 * i + 4)]
            cur = {}
            Obank = {}

            def sub_of(s):
                return slice(128 * (3 - s), 512) if s < 3 else slice(0, 512)

            def emitZ(j, i, hh, s):
                qz, kT, vz, oT = pair_res[j]
                kb = 4 * i + 3 - s
                sub = sub_of(s)
                Z = psZ.next()
                P.op("pe", lambda e: e.matmul(Z[:, sub], lhsT=kT[:, kb * 128:(kb + 1) * 128],
                                              rhs=qz[hh][:, i * 512 + sub.start:(i + 1) * 512], start=True, stop=True),
                     reads=[kT, qz[hh]], writes=[Z])
                cur[("Z", j, i, hh, s)] = Z

            load_pair(0)
            for hh in (0, 1):
                emitZ(0, 0, hh, 0)
            for gi, (j, i, s) in enumerate(steps):
                qz, kT, vz, oT = pair_res[j]
                n = 4 * i + 4
                if s == 0:
                    Obank[(j, i)] = psO.next()
                    if i == 0:
                        load_pair(j + 1)
                O = Obank[(j, i)]
                qs = slice(i * 512, (i + 1) * 512)
                sub = sub_of(s)
                nxt = steps[gi + 1] if gi + 1 < len(steps) else None
                Es, SPs, Xs, As = {}, {}, {}, {}
                for hh in (0, 1):
                    Z = cur.pop(("Z", j, i, hh, s))
                    E = E_r.next()
                    P.op("act", lambda e, E=E, Z=Z: e.activation(out=E[:, sub], in_=Z[:, sub], func=AF.Exp, scale=0.125),
                         reads=[Z], writes=[E])
                    Es[hh] = E
                for hh in (0, 1):
                    SPb = SP_r.next()
                    P.op("act", lambda e, SPb=SPb, hh=hh: e.activation(out=SPb[:, sub], in_=Es[hh][:, sub], func=AF.Ln,
                                                                       bias=1.0),
                         reads=[Es[hh]], writes=[SPb])
                    SPs[hh] = SPb
                if s < 4:
                    r = 3 - s
                    for hh in (0, 1):
                        P.op("dve", lambda e, hh=hh: e.tensor_tensor(out=SPs[hh][:, sub], in0=SPs[hh][:, sub],
                                                                     in1=masks_b[:, r, sub], op=ALU.mult),
                             reads=[masks_b], writes=[SPs[hh]])
                        P.op("pool", lambda e, hh=hh: e.tensor_tensor(out=Es[hh][:, sub], in0=Es[hh][:, sub],
                                                                      in1=masks[:, r, sub], op=ALU.mult),
                             reads=[masks], writes=[Es[hh]])
                for hh in (0, 1):
                    Pb = psP[hh]
                    P.op("pe", lambda e, Pb=Pb, hh=hh: e.matmul(Pb[:, sub], lhsT=U_b[:], rhs=SPs[hh][:, sub], start=(s == 0),
                                                                stop=False, skip_group_check=True),
                         reads=[U_b, SPs[hh]], writes=[Pb] if s == 0 else (), pwrites=[Pb] if s else ())
                    if nxt is not None:
                        emitZ(nxt[0], nxt[1], hh, nxt[2])
                for hh in (0, 1):
                    X = X_r.next()
                    P.op("act", lambda e, X=X, hh=hh: e.activation(out=X[:, sub], in_=psP[hh][:, sub], func=AF.Exp,
                                                                   scale=-1.0),
                         reads=[psP[hh]], writes=[X])
                    Xs[hh] = X
                for hh in (0, 1):
                    A = A_r.next()
                    P.op("dve", lambda e, A=A, hh=hh: e.tensor_tensor(out=A[:, sub], in0=Es[hh][:, sub], in1=Xs[hh][:, sub],
                                                                      op=ALU.mult),
                         reads=[Es[hh], Xs[hh]], writes=[A])
                    As[hh] = A
                for hh in (0, 1):
                    kb = 4 * i + 3 - s
                    if s < n - 1:
                        P.op("pe", lambda e, hh=hh: e.matmul(psP[hh][:, sub], lhsT=LU_b[:], rhs=SPs[hh][:, sub], start=False,
                                                             stop=True, skip_group_check=True),
                             reads=[LU_b, SPs[hh]], pwrites=[psP[hh]])
                    first = (s == 0 and hh == 0)
                    P.op("pe", lambda e, hh=hh, kb=kb, first=first: e.matmul(
                        O[:, sub], lhsT=vz[hh][:, kb, :], rhs=As[hh][:, sub], start=first, stop=(s == n - 1 and hh == 1),
                        skip_group_check=True),
                         reads=[vz[hh], As[hh]], writes=[O] if first else (), pwrites=() if first else [O])
                if s == n - 1:
                    P.op("dve", lambda e: e.tensor_copy(out=oT[:, qs], in_=O[:]), reads=[O], pwrites=[oT])
                    if i == NB - 1:
                        P.dma("pool", obrT_d[j], oT[:], oT, reads=[oT], pwrites=[d_misc])
            st.close()

        if want("D"):
            st = Stage(P, f"D{l}")
            lamb = st.sb([128, 256], F32, "lamb")
            lprod = st.sb([128, 128], F32, "lprod")
            lst = st.sb([128, 8], F32, "lst")
            P.dma("sp", lamb[:], diff_lambda[l].rearrange("a d -> (a d)").partition_broadcast(128), lamb, writes=[lamb])
            P.op("dve", lambda e: e.tensor_tensor(out=lprod[:, 0:64], in0=lamb[:, 0:64], in1=lamb[:, 64:128], op=ALU.mult),
                 reads=[lamb], writes=[lprod])
            P.op("dve", lambda e: e.tensor_tensor(out=lprod[:, 64:128], in0=lamb[:, 128:192], in1=lamb[:, 192:256],
                                                  op=ALU.mult), reads=[lamb], writes=[lprod])
            P.op("dve", lambda e: e.reduce_sum(out=lst[:, 0:1], in_=lprod[:, 0:64], axis=AX.X), reads=[lprod], writes=[lst])
            P.op("dve", lambda e: e.reduce_sum(out=lst[:, 1:2], in_=lprod[:, 64:128], axis=AX.X), reads=[lprod],
                 writes=[lst])
            P.op("act", lambda e: e.activation(out=lst[:, 2:4], in_=lst[:, 0:2], func=AF.Exp), reads=[lst], writes=[lst])
            P.op("dve", lambda e: e.tensor_sub(out=lst[:, 4:5], in0=lst[:, 3:4], in1=lst[:, 2:3]), reads=[lst], writes=[lst])
            P.op("dve", lambda e: e.tensor_scalar(out=lst[:, 5:6], in0=lst[:, 4:5], scalar1=-lam_init, scalar2=None,
                                                  op0=ALU.add), reads=[lst], writes=[lst])
            P.op("dve", lambda e: e.tensor_scalar(out=lst[:, 6:7], in0=colap(("subln", l)), scalar1=1.0 - lam_init,
                                                  scalar2=None, op0=ALU.mult), reads=[cols, lst], writes=[lst])
            qz_r = Rot([[st.sb([128, S], BF16, "qz") for _ in range(2)] for _ in range(2)])
            for pair in qz_r.items:
                for bz in pair:
                    P.op("pool", lambda e, bz=bz: e.memset(bz[:], 0.0), writes=[bz])
            kT_r = Rot([st.sb([128, S], BF16, "kT") for _ in range(2)])
            v_r = Rot([st.sb([128, NT, 128], BF16, "v") for _ in range(2)])
            oT_r = Rot([st.sb([128, S], BF16, "oT") for _ in range(2)])
            G_r = Rot([st.sb([128, 1024], F32, "Gh") for _ in range(2)])
            psZ = Rot([st.ps([128, 512], F32, "Z") for _ in range(4)])
            psO = [st.ps([128, 512], F32, "O") for _ in range(2)]
            psD = [st.ps([128, 512], F32, "Dn") for _ in range(2)]
            T_r = Rot([st.sb([128, 512], F32, "T") for _ in range(3)])
            A_r = Rot([st.sb([128, 512], BF16, "A") for _ in range(4)])
            rec_r = Rot([st.sb([128, 512], F32, "rec") for _ in range(2)])
            R_b = [st.sb([128, 512], F32, "R") for _ in range(2)]
            o_b = st.sb([128, 512], F32, "o")
            sq_b = st.sb([128, 512], BF16, "sq")
            rs_b = st.sb([128, 512], F32, "rs")
            head_res = {}

            def load_head(h):
                if h >= 4 or h in head_res:
                    return
                qz, kT, vv, oT, Gh = qz_r.next(), kT_r.next(), v_r.next(), oT_r.next(), G_r.next()
                for m_ in range(2):
                    ms_ = slice(64 * m_, 64 * m_ + 64)
                    P.dma("sp", qz[m_][ms_, :], featT_d[8 + h, ms_, :], qz[m_], reads=[d_misc], pwrites=[qz[m_]])
                P.dma("sp", kT[:], featT_d[12 + h], kT, reads=[d_misc], writes=[kT])
                P.dma("sp", vv[:], vtok_d[1, :, h * 128:(h + 1) * 128].rearrange("(t p) c -> p t c", p=128), vv,
                      reads=[d_misc], writes=[vv])
                P.dma("sp", Gh[:], gbias_d[h], Gh, reads=[d_misc], writes=[Gh])
                head_res[h] = (qz, kT, vv, oT, Gh)

            steps = [(h, i, s_) for h in range(4) for i in range(NB) for s_ in range(4 * i + 4)]
            cur = {}

            def dsub(i, s):
                dl = 4 * i - s
                return slice(128 * (-dl), 512) if dl < 0 else slice(0, 512)

            def emitZ(h, i, m, s):
                qz, kT, vv, oT, Gh = head_res[h]
                Z = psZ.next()
                sub = dsub(i, s)
                P.op("pe", lambda e: e.matmul(Z[:, sub], lhsT=kT[:, s * 128:(s + 1) * 128],
                                              rhs=qz[m][:, i * 512 + sub.start:(i + 1) * 512],
                                              start=True, stop=True), reads=[kT, qz[m]], writes=[Z])
                cur[("Z", h, i, m, s)] = Z

            load_head(0)
            for m in (0, 1):
                emitZ(0, 0, m, 0)
            for gi, (h, i, s) in enumerate(steps):
                qz, kT, vv, oT, Gh = head_res[h]
                if s == 0 and i == 0:
                    load_head(h + 1)
                n = 4 * i + 4
                qs = slice(i * 512, (i + 1) * 512)
                b31 = tb[:, 31 * 4 + h:31 * 4 + h + 1]
                nxt = steps[gi + 1] if gi + 1 < len(steps) else None
                sub = dsub(i, s)
                As = {}
                for m in (0, 1):
                    Z = cur.pop(("Z", h, i, m, s))
                    A = A_r.next()
                    dlt = 4 * i - s
                    if dlt >= 2:
                        P.op("act", lambda e, A=A, Z=Z: e.activation(out=A[:], in_=Z[:], func=AF.Exp, bias=b31, scale=0.125),
                             reads=[Z, tb], writes=[A])
                    else:
                        c0 = 128 * dlt + 384
                        T = T_r.next()
                        P.op("dve", lambda e, T=T, Z=Z, c0=c0: e.scalar_tensor_tensor(
                            out=T[:, sub], in0=Z[:, sub], scalar=0.125, in1=Gh[:, c0 + sub.start:c0 + 512], op0=ALU.mult,
                            op1=ALU.add),
                             reads=[Z, Gh], writes=[T])
                        P.op("act", lambda e, A=A, T=T: e.activation(out=A[:, sub], in_=T[:, sub], func=AF.Exp),
                             reads=[T], writes=[A])
                    As[m] = A
                if nxt is not None:
                    for m in (0, 1):
                        emitZ(nxt[0], nxt[1], m, nxt[2])
                for m in (0, 1):
                    P.op("pe", lambda e, m=m: e.matmul(psO[m][:, sub], lhsT=vv[:, s, :], rhs=As[m][:, sub], start=(s == 0),
                                                       stop=(s == n - 1), skip_group_check=True),
                         reads=[vv, As[m]], writes=[psO[m]] if s == 0 else (), pwrites=[psO[m]] if s else ())
                for m in (0, 1):
                    P.op("pe", lambda e, m=m: e.matmul(psD[m][:, sub], lhsT=ones_b[:], rhs=As[m][:, sub], start=(s == 0),
                                                       stop=(s == n - 1), skip_group_check=True),
                         reads=[ones_b, As[m]], writes=[psD[m]] if s == 0 else (), pwrites=[psD[m]] if s else ())
                if s == n - 1:
                    for m in (0, 1):
                        rec = rec_r.next()
                        P.op("act", lambda e, m=m, rec=rec: e.activation(out=rec[:], in_=psD[m][:], func=AF.Ln),
                             reads=[psD[m]], writes=[rec])
                        P.op("act", lambda e, rec=rec: e.activation(out=rec[:], in_=rec[:], func=AF.Exp, scale=-1.0),
                             writes=[rec])
                        P.op("dve", lambda e, m=m, rec=rec: e.tensor_tensor(out=R_b[m][:], in0=psO[m][:], in1=rec[:],
                                                                            op=ALU.mult),
                             reads=[psO[m], rec], writes=[R_b[m]])
                    P.op("dve", lambda e: e.scalar_tensor_tensor(out=o_b[:], in0=R_b[1][:], scalar=lst[:, 5:6],
                                                                 in1=R_b[0][:], op0=ALU.mult, op1=ALU.add),
                         reads=[R_b[0], R_b[1], lst], writes=[o_b])
                    P.op("pool", lambda e: e.tensor_tensor(out=sq_b[:], in0=o_b[:], in1=o_b[:], op=ALU.mult),
                         reads=[o_b], writes=[sq_b])
                    Zs = psZ.next()
                    P.op("pe", lambda e, Zs=Zs: e.matmul(Zs[:], lhsT=ones_b[:], rhs=sq_b[:], start=True, stop=True),
                         reads=[ones_b, sq_b], writes=[Zs])
                    P.op("act", lambda e, Zs=Zs: e.activation(out=rs_b[:], in_=Zs[:], func=AF.Ln, bias=epsc[:, 0:1],
                                                              scale=1.0 / 128), reads=[Zs, epsc], writes=[rs_b])
                    P.op("act", lambda e: e.activation(out=rs_b[:], in_=rs_b[:], func=AF.Exp, scale=-0.5),
                         writes=[rs_b])
                    P.op("dve", lambda e, oT=oT, qs=qs: e.scalar_tensor_tensor(out=oT[:, qs], in0=o_b[:], scalar=lst[:, 6:7],
                                                                               in1=rs_b[:], op0=ALU.mult, op1=ALU.mult),
                         reads=[o_b, rs_b, lst], pwrites=[oT])
                    if i == NB - 1:
                        P.dma("pool", obrT_d[4 + h], oT[:], oT, reads=[oT], pwrites=[d_misc])
            st.close()

        if want("E"):
            st = Stage(P, f"E{l}")
            rows31 = st.sb([32, 512], F32, "rows31")
            cwc = st.sb([128, 4, CONV_W], F32, "cwc")
            dg = st.sb([128, 4 * CONV_W, 128], BF16, "dg")
            hpad = st.sb([128, 4, HALO + S], BF16, "hpad")
            ocv = st.sb([128, 4, S], BF16, "ocv")
            a_r = Rot([st.sb([128, S], BF16, "aT") for _ in range(2)])
            g_r = Rot([st.sb([128, S], BF16, "gT") for _ in range(2)])
            sg_r = Rot([st.sb([128, S], BF16, "sg") for _ in range(2)])
            psC = Rot([st.ps([128, 512], F32, "C") for _ in range(3)])
            psS = [st.ps([128, 512], F32, "S1"), st.ps([128, 512], F32, "S2")]
            psT = st.ps([128, 4, 32], F32, "cwT")
            hc_all = [[st.sb([128, 512], F32, "hc") for _ in range(4)] for _ in range(2)]
            hb_r = Rot([st.sb([128, 512], BF16, "hb") for _ in range(3)])
            sq_r = Rot([st.sb([128, 512], BF16, "sq") for _ in range(3)])
            mean = st.sb([128, 512], F32, "mean")
            msq = st.sb([128, 512], F32, "msq")
            var = st.sb([128, 512], F32, "var")
            rstd = st.sb([128, 512], F32, "rstd")
            t1_r = Rot([st.sb([128, 512], F32, "t1") for _ in range(2)])
            t2_r = Rot([st.sb([128, 512], F32, "t2") for _ in range(2)])
            s2_r = Rot([st.sb([128, 512], F32, "s2") for _ in range(2)])
            P.dma("sp", rows31[0:CONV_W, :], conv_w[l].rearrange("w o c -> w (o c)"), rows31, writes=[rows31])
            for c in range(4):
                P.op("pe", lambda e, c=c: e.transpose(out=psT[:, c, 0:CONV_W], in_=rows31[0:CONV_W, c * 128:(c + 1) * 128],
                                                      identity=ident_f[0:CONV_W, 0:CONV_W]),
                     reads=[rows31, ident_f], writes=[psT] if c == 0 else (), pwrites=[psT] if c else ())
            P.op("dve", lambda e: e.tensor_copy(out=cwc[:], in_=psT[:, :, 0:CONV_W]), reads=[psT], writes=[cwc])
            for c in range(4):
                P.op("dve", lambda e, c=c: e.tensor_tensor(
                    out=dg[:, c * CONV_W:(c + 1) * CONV_W, :],
                    in0=ident_b[:].unsqueeze(1).to_broadcast([128, CONV_W, 128]),
                    in1=cwc[:, c, :].unsqueeze(2).to_broadcast([128, CONV_W, 128]), op=ALU.mult),
                     reads=[ident_b, cwc], writes=[dg] if c == 0 else (), pwrites=[dg] if c else ())
            P.op("pool", lambda e: e.memset(hpad[:, :, 0:HALO], 0.0), writes=[hpad])
            for c in range(4):
                aT, gT, sg = a_r.next(), g_r.next(), sg_r.next()
                P.dma("sp", aT[:], featT_d[16 + c], aT, reads=[d_misc], writes=[aT])
                P.dma("sp", gT[:], featT_d[20 + c], gT, reads=[d_misc], writes=[gT])
                P.op("act", lambda e: e.activation(out=sg[:], in_=gT[:], func=AF.Sigmoid), reads=[gT], writes=[sg])
                P.op("dve", lambda e, c=c: e.tensor_tensor(out=hpad[:, c, HALO:HALO + S], in0=aT[:], in1=sg[:],
                                                           op=ALU.mult), reads=[aT, sg], pwrites=[hpad])
            cb = COL[("conv_b", l)]
            cg = COL[("conv_ln_g", l)]
            cbb = COL[("conv_ln_b", l)]
            for blk in range(NB):
                t0 = blk * 512
                hc = hc_all[blk % 2]
                pend = []

                def stats(c, hb, sq):
                    P.op("pe", lambda e, c=c, hb=hb: e.matmul(psS[0][:], lhsT=ones_b[:], rhs=hb[:], start=(c == 0),
                                                              stop=(c == 3)),
                         reads=[ones_b, hb], writes=[psS[0]] if c == 0 else (), pwrites=[psS[0]] if c else ())
                    P.op("pe", lambda e, c=c, sq=sq: e.matmul(psS[1][:], lhsT=ones_b[:], rhs=sq[:], start=(c == 0),
                                                              stop=(c == 3)),
                         reads=[ones_b, sq], writes=[psS[1]] if c == 0 else (), pwrites=[psS[1]] if c else ())

                for c in range(4):
                    ps = psC.next()
                    for w in range(CONV_W):
                        P.op("pe", lambda e, c=c, w=w, ps=ps: e.matmul(
                            ps[:], lhsT=dg[:, c * CONV_W + w, :], rhs=hpad[:, c, t0 + w:t0 + w + 512],
                            start=(w == 0), stop=(w == CONV_W - 1)),
                             reads=[dg, hpad], writes=[ps] if w == 0 else (), pwrites=[ps] if w else (),
                             inc=(w == CONV_W - 1))
                    if pend:
                        stats(*pend.pop(0))
                    hb, sq = hb_r.next(), sq_r.next()
                    P.op("dve", lambda e, c=c, ps=ps, hb=hb: e.tensor_scalar(out=hb[:], in0=ps[:], scalar1=cols[:, cb + c:cb + c + 1],
                                                                             scalar2=None, op0=ALU.add),
                         reads=[ps, cols], writes=[hb])
                    P.op("dve", lambda e, c=c, ps=ps: e.tensor_scalar(out=hc[c][:], in0=ps[:], scalar1=cols[:, cb + c:cb + c + 1],
                                                                      scalar2=None, op0=ALU.add),
                         reads=[ps, cols], writes=[hc[c]])
                    P.op("act", lambda e, c=c, sq=sq: e.activation(out=sq[:], in_=hc[c][:], func=AF.Square),
                         reads=[hc[c]], writes=[sq])
                    pend.append((c, hb, sq))
                while pend:
                    stats(*pend.pop(0))
                P.op("dve", lambda e: e.tensor_scalar(out=mean[:], in0=psS[0][:], scalar1=1.0 / 512, scalar2=None,
                                                      op0=ALU.mult), reads=[psS[0]], writes=[mean])
                P.op("pool", lambda e: e.tensor_tensor(out=msq[:], in0=mean[:], in1=mean[:], op=ALU.mult),
                     reads=[mean], writes=[msq])
                P.op("dve", lambda e: e.scalar_tensor_tensor(out=var[:], in0=psS[1][:], scalar=1.0 / 512, in1=msq[:],
                                                             op0=ALU.mult, op1=ALU.subtract),
                     reads=[psS[1], msq], writes=[var])
                P.op("act", lambda e: e.activation(out=rstd[:], in_=var[:], func=AF.Ln, bias=epsc[:, 0:1]),
                     reads=[var, epsc], writes=[rstd])
                P.op("act", lambda e: e.activation(out=rstd[:], in_=rstd[:], func=AF.Exp, scale=-0.5), writes=[rstd])
                for c in range(4):
                    t1, t2, s2 = t1_r.next(), t2_r.next(), s2_r.next()
                    P.op("pool", lambda e, c=c, t1=t1: e.tensor_tensor(out=t1[:], in0=hc[c][:], in1=mean[:], op=ALU.subtract),
                         reads=[hc[c], mean], writes=[t1])
                    P.op("dve", lambda e, t1=t1, t2=t2: e.tensor_tensor(out=t2[:], in0=t1[:], in1=rstd[:], op=ALU.mult),
                         reads=[t1, rstd], writes=[t2])
                    P.op("dve", lambda e, c=c, t2=t2: e.tensor_scalar(out=t2[:], in0=t2[:], scalar1=cols[:, cg + c:cg + c + 1],
                                                                      scalar2=cols[:, cbb + c:cbb + c + 1], op0=ALU.mult,
                                                                      op1=ALU.add),
                         reads=[cols], writes=[t2])
                    P.op("act", lambda e, t2=t2, s2=s2: e.activation(out=s2[:], in_=t2[:], func=AF.Sigmoid),
                         reads=[t2], writes=[s2])
                    P.op("pool", lambda e, c=c, t2=t2, s2=s2: e.tensor_tensor(out=ocv[:, c, t0:t0 + 512], in0=t2[:],
                                                                              in1=s2[:], op=ALU.mult),
                         reads=[t2, s2], pwrites=[ocv])
            for c in range(4):
                P.dma("pool", obrT_d[8 + c], ocv[:, c, :], ocv, reads=[ocv], pwrites=[d_misc])
            st.close()


        if want("F1"):
            st = Stage(P, f"F1{l}")
            w_g = st.sb([128, KC, 3072], BF16, "wg")
            w_br = st.sb([128, 12, D], BF16, "wbr")
            w_o = st.sb([128, KC, D], BF16, "wo")
            wload(w_g, w_g, w_in[l][:, 4096:7168])
            wload(w_br, w_br[:, 0:4, :], w_sb_proj[l])
            wload(w_br, w_br[:, 4:8, :], w_diff_proj[l], first=False)
            wload(w_br, w_br[:, 8:12, :], w_conv_proj[l], first=False)
            wload(w_o, w_o, w_out[l])
            xn_r = Rot([st.sb([128, KC, 512], BF16, "xnb") for _ in range(2)])
            ob_r = Rot([st.sb([128, 12, 512], BF16, "obr") for _ in range(2)])
            yT_r = Rot([st.sb([128, KC, 512], BF16, "yT") for _ in range(2)])
            psG = Rot([st.ps([128, 512], F32, "G") for _ in range(2)])
            psB = Rot([st.ps([128, 512], F32, "B") for _ in range(2)])
            sg_r = Rot([st.sb([128, 512], F32, "sg") for _ in range(2)])
            acc_r = Rot([st.sb([128, 512], F32, "acc") for _ in range(2)])
            tmp_r = Rot([st.sb([128, 512], F32, "tmp") for _ in range(2)])
            tl = Tail(st, 4)
            xsrc = x_in if l == 0 else xres
            yTs = {}

            def f1_main(blk):
                bs = slice(blk * 512, (blk + 1) * 512)
                xnb, obr, yT = xn_r.next(), ob_r.next(), yT_r.next()
                yTs[blk] = yT
                P.dma("sp", xnb[:], xnT_d[:, :, bs].rearrange("c p t -> p c t"), xnb,
                      reads=d_xnT[4 * blk:4 * blk + 4], writes=[xnb])
                P.dma("sp", obr[:], obrT_d[:, :, bs].rearrange("c p t -> p c t"), obr, reads=[d_misc], writes=[obr])
                for m in range(KC):
                    acc = acc_r.next()
                    for br in range(3):
                        Gp, Bp = psG.next(), psB.next()
                        for k in range(KC):
                            P.op("pe", lambda e, k=k, Gp=Gp, br=br: e.matmul(
                                Gp[:], lhsT=w_g[:, k, br * 1024 + m * 128:br * 1024 + (m + 1) * 128], rhs=xnb[:, k, :],
                                start=(k == 0), stop=(k == KC - 1)),
                                 reads=[w_g, xnb], writes=[Gp] if k == 0 else (), pwrites=[Gp] if k else (),
                                 inc=(k == KC - 1))
                        for k in range(4):
                            P.op("pe", lambda e, k=k, Bp=Bp, br=br: e.matmul(
                                Bp[:], lhsT=w_br[:, br * 4 + k, m * 128:(m + 1) * 128], rhs=obr[:, br * 4 + k, :],
                                start=(k == 0), stop=(k == 3)),
                                 reads=[w_br, obr], writes=[Bp] if k == 0 else (), pwrites=[Bp] if k else (),
                                 inc=(k == 3))
                        sg = sg_r.next()
                        P.op("act", lambda e, Gp=Gp, sg=sg: e.activation(out=sg[:], in_=Gp[:], func=AF.Sigmoid),
                             reads=[Gp], writes=[sg])
                        if br == 0:
                            P.op("dve", lambda e, Bp=Bp, sg=sg: e.tensor_tensor(out=acc[:], in0=Bp[:], in1=sg[:], op=ALU.mult),
                                 reads=[Bp, sg], writes=[acc])
                        else:
                            tmp = tmp_r.next()
                            P.op("dve", lambda e, Bp=Bp, sg=sg, tmp=tmp: e.tensor_tensor(out=tmp[:], in0=Bp[:], in1=sg[:],
                                                                                         op=ALU.mult),
                                 reads=[Bp, sg], writes=[tmp])
                            if br == 1:
                                P.op("dve", lambda e, tmp=tmp: e.tensor_tensor(out=acc[:], in0=acc[:], in1=tmp[:], op=ALU.add),
                                     reads=[tmp], writes=[acc])
                            else:
                                P.op("dve", lambda e, tmp=tmp: e.tensor_tensor(out=yT[:, m, :], in0=acc[:], in1=tmp[:],
                                                                               op=ALU.add),
                                     reads=[tmp, acc], pwrites=[yT])

            f1_main(0)
            for blk in range(NB):
                if blk + 1 < NB:
                    f1_main(blk + 1)
                tl.block(4 * blk, 4, yTs.pop(blk), KC, w_o, xsrc, ("ln_xattn", l))
            st.close()

        if want("F2"):
            st = Stage(P, f"F2{l}")
            wkv = st.sb([128, KC, D], BF16, "wkv")
            wq = st.sb([128, KC, 512], BF16, "wq")
            wo = st.sb([128, 4, D], BF16, "wo")
            wload(wkv, wkv, xa_w_kv[l])
            wload(wq, wq, xa_w_q[l])
            wload(wo, wo, xa_w_o[l])
            tl = Tail(st, 4)
            memT = st.sb([128, KC, MEM], BF16, "memT")
            kTx = st.sb([128, 4, MEM], BF16, "kTx")
            vx = st.sb([128, 2, 512], BF16, "vx")
            mt_r = Rot([st.sb([128, D], F32, "mt") for _ in range(2)])
            psZ = Rot([st.ps([128, 512], F32, "Z") for _ in range(2)])
            psO = st.ps([128, 512], F32, "O")
            psD = st.ps([128, 512], F32, "Dn")
            for mb in range(2):
                mt = mt_r.next()
                P.dma("sp", mt[:], mem_in[mb * 128:(mb + 1) * 128, :], mt, writes=[mt])
                tl.nt.run(mt, ("ln_mem", l), memT, slice(mb * 128, (mb + 1) * 128))
            for h in range(4):
                ps = psZ.next()
                for k in range(KC):
                    P.op("pe", lambda e, k=k, ps=ps, h=h: e.matmul(ps[:, 0:MEM], lhsT=wkv[:, k, h * 128:(h + 1) * 128],
                                                                   rhs=memT[:, k, :], start=(k == 0), stop=(k == KC - 1)),
                         reads=[wkv, memT], writes=[ps] if k == 0 else (), pwrites=[ps] if k else (), inc=(k == KC - 1))
                P.op("dve", lambda e, ps=ps, h=h: e.tensor_copy(out=kTx[:, h, :], in_=ps[:, 0:MEM]), reads=[ps],
                     pwrites=[kTx])
            for mb in range(2):
                ps = psZ.next()
                for k in range(KC):
                    P.op("pe", lambda e, k=k, ps=ps, mb=mb: e.matmul(ps[:], lhsT=memT[:, k, mb * 128:(mb + 1) * 128],
                                                                     rhs=wkv[:, k, 512:1024], start=(k == 0),
                                                                     stop=(k == KC - 1)),
                         reads=[wkv, memT], writes=[ps] if k == 0 else (), pwrites=[ps] if k else (), inc=(k == KC - 1))
                P.op("dve", lambda e, ps=ps, mb=mb: e.tensor_copy(out=vx[:, mb, :], in_=ps[:]), reads=[ps], pwrites=[vx])
            xn_r = Rot([st.sb([128, KC, 512], BF16, "xnb") for _ in range(2)])
            qTx_r = Rot([st.sb([128, 4, 512], BF16, "qTx") for _ in range(2)])
            oTx_r = Rot([st.sb([128, 4, 512], BF16, "oTx") for _ in range(2)])
            A_r = Rot([st.sb([128, 512], BF16, "A") for _ in range(4)])
            rec_r = Rot([st.sb([128, 512], F32, "rec") for _ in range(2)])
            xscale = 1.0 / math.sqrt(128.0)
            oTxs = {}

            def f2_main(blk):
                bs = slice(blk * 512, (blk + 1) * 512)
                xnb, qTx, oTx = xn_r.next(), qTx_r.next(), oTx_r.next()
                oTxs[blk] = oTx
                P.dma("sp", xnb[:], xnT_d[:, :, bs].rearrange("c p t -> p c t"), xnb,
                      reads=d_xnT[4 * blk:4 * blk + 4], writes=[xnb])
                for h in range(4):
                    ps = psZ.next()
                    for k in range(KC):
                        P.op("pe", lambda e, k=k, ps=ps, h=h: e.matmul(ps[:], lhsT=wq[:, k, h * 128:(h + 1) * 128],
                                                                       rhs=xnb[:, k, :], start=(k == 0), stop=(k == KC - 1)),
                             reads=[wq, xnb], writes=[ps] if k == 0 else (), pwrites=[ps] if k else (), inc=(k == KC - 1))
                    P.op("dve", lambda e, ps=ps, h=h: e.tensor_copy(out=qTx[:, h, :], in_=ps[:]), reads=[ps], pwrites=[qTx])
                for h in range(4):
                    As = []
                    for mb in range(2):
                        Z = psZ.next()
                        P.op("pe", lambda e, Z=Z, h=h, mb=mb: e.matmul(Z[:], lhsT=kTx[:, h, mb * 128:(mb + 1) * 128],
                                                                       rhs=qTx[:, h, :], start=True, stop=True),
                             reads=[kTx, qTx], writes=[Z])
                        A = A_r.next()
                        P.op("act", lambda e, Z=Z, A=A: e.activation(out=A[:], in_=Z[:], func=AF.Exp, scale=xscale),
                             reads=[Z], writes=[A])
                        As.append(A)
                    for mb in range(2):
                        P.op("pe", lambda e, h=h, mb=mb: e.matmul(psO[:], lhsT=vx[:, mb, h * 128:(h + 1) * 128],
                                                                  rhs=As[mb][:], start=(mb == 0), stop=(mb == 1)),
                             reads=[vx, As[mb]], writes=[psO] if mb == 0 else (), pwrites=[psO] if mb else ())
                    for mb in range(2):
                        P.op("pe", lambda e, mb=mb: e.matmul(psD[:], lhsT=ones_b[:], rhs=As[mb][:], start=(mb == 0),
                                                             stop=(mb == 1)),
                             reads=[ones_b, As[mb]], writes=[psD] if mb == 0 else (), pwrites=[psD] if mb else ())
                    rec = rec_r.next()
                    P.op("act", lambda e, rec=rec: e.activation(out=rec[:], in_=psD[:], func=AF.Ln), reads=[psD], writes=[rec])
                    P.op("act", lambda e, rec=rec: e.activation(out=rec[:], in_=rec[:], func=AF.Exp, scale=-1.0), writes=[rec])
                    P.op("dve", lambda e, rec=rec, h=h: e.tensor_tensor(out=oTx[:, h, :], in0=psO[:], in1=rec[:], op=ALU.mult),
                         reads=[psO, rec], pwrites=[oTx])

            f2_main(0)
            for blk in range(NB):
                if blk + 1 < NB:
                    f2_main(blk + 1)
                tl.block(4 * blk, 4, oTxs.pop(blk), 4, wo, xres, ("ln_mlp", l))
            st.close()

        if want("G"):
            st = Stage(P, f"G{l}")
            wu = st.sb([128, KC, DFF], BF16, "wu")
            wd = st.sb([128, 32, D], BF16, "wd")
            wload(wu, wu, w_up[l])
            wload(wd, wd, w_down[l])
            last = (l == nlayers - 1)
            tl = Tail(st, 2, final=last)
            xn_r = Rot([st.sb([128, KC, 256], BF16, "xnb") for _ in range(2)])
            hT = st.sb([128, 32, 256], BF16, "hT")
            r_r = Rot([st.sb([128, 256], F32, "r") for _ in range(2)])
            psU = Rot([st.ps([128, 512], F32, "U") for _ in range(3)])
            for b2 in range(S // 256):
                bs = slice(b2 * 256, (b2 + 1) * 256)
                xnb = xn_r.next()
                P.dma("sp", xnb[:], xnT_d[:, :, bs].rearrange("c p t -> p c t"), xnb,
                      reads=d_xnT[2 * b2:2 * b2 + 2], writes=[xnb])
                for f in range(32):
                    ps = psU.next()
                    for k in range(KC):
                        P.op("pe", lambda e, k=k, ps=ps, f=f: e.matmul(ps[:, 0:256], lhsT=wu[:, k, f * 128:(f + 1) * 128],
                                                                       rhs=xnb[:, k, :], start=(k == 0), stop=(k == KC - 1)),
                             reads=[wu, xnb], writes=[ps] if k == 0 else (), pwrites=[ps] if k else (), inc=(k == KC - 1))
                    r = r_r.next()
                    P.op("dve", lambda e, ps=ps, r=r: e.tensor_scalar(out=r[:], in0=ps[:, 0:256], scalar1=0.0, scalar2=None,
                                                                      op0=ALU.max), reads=[ps], writes=[r])
                    P.op("act", lambda e, r=r, f=f: e.activation(out=hT[:, f, :], in_=r[:], func=AF.Square),
                         reads=[r], pwrites=[hT])
                tl.block(2 * b2, 2, hT, 32, wd, xres, ("ln_mix", l + 1) if not last else None, final=last)
            st.close()

    P.barrier()
    G.es.close()
    return nc, P


INPUT_NAMES = ["x", "mem", "rel_bias", "ln_mix", "w_in", "diff_lambda", "diff_subln", "conv_w", "conv_b",
               "conv_ln_g", "conv_ln_b", "w_sb_proj", "w_diff_proj", "w_conv_proj", "w_out", "ln_xattn", "ln_mem",
               "xa_w_q", "xa_w_kv", "xa_w_o", "ln_mlp", "w_up", "w_down", "ln_final"]


def make_in_maps(inputs, S, ncores=8):
    shared = {k: np.ascontiguousarray(np.asarray(inputs[k], dtype=np.float32)) for k in INPUT_NAMES
              if k not in ("x", "mem")}
    maps = []
    for b in range(ncores):
        m = dict(shared)
        m["x"] = np.ascontiguousarray(np.asarray(inputs["x"][b, :S], dtype=np.float32))
        m["mem"] = np.ascontiguousarray(np.asarray(inputs["mem"][b], dtype=np.float32))
        maps.append(m)
    return maps


def kernel(**inputs):
    S = 4096
    nc, _ = build(S)
    maps = make_in_maps(inputs, S)
    res = run_bass_kernel_spmd(nc, maps, core_ids=list(range(8)))
    return np.stack([np.asarray(r["y"], dtype=np.float32) for r in res.results], axis=0)
```

```python
import math
from contextlib import ExitStack
import numpy as np
import concourse.bass as bass
import concourse.mybir as mybir
from concourse.bass_utils import run_bass_kernel_spmd

F32 = mybir.dt.float32
BF16 = mybir.dt.bfloat16
AF = mybir.ActivationFunctionType
ALU = mybir.AluOpType
AX = mybir.AxisListType

D = 1024
NL = 2
KC = 8
IN_COLS = 7168
DFF = 4096
MEM = 256
EPS = 1e-6
NEG = -30000.0
NUM_BUCKETS = 32
CONV_W = 31
HALO = CONV_W - 1


def _bucket(n):
    if n < 16:
        return n
    v = 16 + int(np.float32(np.log(np.float32(n) / np.float32(16)) / np.float32(math.log(8.0)) * np.float32(16)))
    return min(v, 31)


class Buf:
    __slots__ = ("t", "w", "pw", "r", "dsem", "dcnt", "name")

    def __init__(self, t=None, name=""):
        self.t = t
        self.w = {}
        self.pw = {}
        self.r = {}
        self.dsem = {}
        self.dcnt = {}
        self.name = name

    def __getitem__(self, k):
        return self.t[k]


class Prog:
    ENGS = ("pe", "act", "dve", "pool", "sp")
    ROLL = 30000

    def __init__(self, nc):
        self.nc = nc
        self.eng = {"pe": nc.tensor, "act": nc.scalar, "dve": nc.vector, "pool": nc.gpsimd, "sp": nc.sync}
        self.sem = {}
        self.cnt = {}
        self.seen = {e: {} for e in self.ENGS}
        self.pending = {e: [] for e in self.ENGS}
        self.allsems = {}
        self.free_dsems = {"hw": [], "sw": []}
        self.nsem = 0
        self.ninst = 0
        for e in self.ENGS:
            self._new_eng_sem(e)

    def _alloc_sem(self, name):
        self.nsem += 1
        s = self.nc.alloc_semaphore(f"{name}_{self.nsem}")
        self.allsems[s.num] = [s, 0]
        return s

    def _new_eng_sem(self, e):
        self.sem[e] = self._alloc_sem("e" + e)
        self.cnt[e] = 0

    def _get_dsem(self, kind):
        if self.free_dsems[kind]:
            return self.free_dsems[kind].pop()
        return (self._alloc_sem("d" + kind), 0)

    def _waits(self, eng, reads, writes, pwrites):
        waits = {}

        def add(d):
            for k, ev in d.items():
                if k not in waits or waits[k][1] < ev[1]:
                    waits[k] = ev

        for b in reads:
            add(b.w)
            add(b.pw)
        for b in writes:
            add(b.w)
            add(b.pw)
            add(b.r)
        for b in pwrites:
            add(b.w)
            add(b.r)
        e = self.eng[eng]
        seen = self.seen[eng]
        for k, (s, v) in waits.items():
            if eng == "pe" and k == self.sem["pe"].num:
                continue
            if seen.get(k, 0) >= v:
                continue
            seen[k] = v
            e.wait_ge(s, v)

    def _apply(self, key, ev, reads, writes, pwrites):
        for b in reads:
            b.r[key] = ev
        for b in writes:
            b.w = {key: ev}
            b.pw = {}
            b.r = {}
        for b in pwrites:
            b.pw[key] = ev

    def op(self, eng, fn, reads=(), writes=(), pwrites=(), inc=True):
        self._waits(eng, reads, writes, pwrites)
        ins = fn(self.eng[eng])
        self.ninst += 1
        if not inc:
            self.pending[eng].append((reads, writes, pwrites))
            return
        if self.cnt[eng] >= self.ROLL:
            self._new_eng_sem(eng)
        self.cnt[eng] += 1
        sem = self.sem[eng]
        ins.then_inc(sem, 1)
        self.allsems[sem.num][1] = self.cnt[eng]
        ev = (sem, self.cnt[eng])
        self._apply(sem.num, ev, reads, writes, pwrites)
        for (r, w, pw) in self.pending[eng]:
            self._apply(sem.num, ev, r, w, pw)
        self.pending[eng] = []

    def dma(self, q, out, in_, sb, reads=(), writes=(), pwrites=(), **kw):
        self._waits(q, reads, writes, pwrites)
        kind = "sw" if q == "pool" else "hw"
        if kind not in sb.dsem:
            sb.dsem[kind], sb.dcnt[kind] = self._get_dsem(kind)
        ins = self.eng[q].dma_start(out=out, in_=in_, **kw)
        self.ninst += 1
        sb.dcnt[kind] += 16
        assert sb.dcnt[kind] < 60000
        sem = sb.dsem[kind]
        ins.then_inc(sem, 16)
        self.allsems[sem.num][1] = sb.dcnt[kind]
        ev = (sem, sb.dcnt[kind])
        self._apply(sem.num, ev, reads, writes, pwrites)

    def release(self, bufs):
        for b in bufs:
            for kind, sem in b.dsem.items():
                self.free_dsems[kind].append((sem, b.dcnt[kind]))
            b.dsem = {}
            b.dcnt = {}

    def barrier(self, engs=None):
        for e in (engs or self.ENGS):
            assert not self.pending[e]
            seen = self.seen[e]
            for k, (s, v) in self.allsems.items():
                if v > 0 and seen.get(k, 0) < v:
                    seen[k] = v
                    self.eng[e].wait_ge(s, v)


class Stage:
    def __init__(self, P, name):
        self.P = P
        self.nc = P.nc
        self.name = name
        self.es = ExitStack()
        self.bufs = []
        self.n = 0

    def sb(self, shape, dtype, name="t"):
        self.n += 1
        t = self.es.enter_context(self.nc.sbuf_tensor(f"{self.name}_{name}{self.n}", list(shape), dtype))
        b = Buf(t, f"{self.name}_{name}")
        self.bufs.append(b)
        return b

    def ps(self, shape, dtype=F32, name="p"):
        self.n += 1
        t = self.es.enter_context(self.nc.psum_tensor(f"{self.name}_{name}{self.n}", list(shape), dtype))
        b = Buf(t, f"{self.name}_{name}")
        self.bufs.append(b)
        return b

    def close(self):
        self.P.barrier()
        self.P.release(self.bufs)
        self.es.close()


class Rot:
    def __init__(self, items):
        self.items = items
        self.i = 0

    def next(self):
        b = self.items[self.i % len(self.items)]
        self.i += 1
        return b


def build(S, debug=False, stages=None, nlayers=NL):
    NT = S // 128
    NB = S // 512
    nc = bass.Bass("TRN2", target_bir_lowering=False)
    P = Prog(nc)

    def dram_in(name, shape):
        return nc.dram_tensor(name, list(shape), F32, kind="ExternalInput").ap()

    okind = "ExternalOutput" if debug else "Internal"

    def scratch(name, shape, dtype):
        return nc.dram_tensor(name, list(shape), dtype, kind=okind).ap()

    x_in = dram_in("x", [S, D])
    mem_in = dram_in("mem", [MEM, D])
    rel_bias = dram_in("rel_bias", [NUM_BUCKETS, 4])
    ln_mix = dram_in("ln_mix", [NL, D])
    w_in = dram_in("w_in", [NL, D, IN_COLS])
    diff_lambda = dram_in("diff_lambda", [NL, 4, 64])
    diff_subln = dram_in("diff_subln", [NL, 128])
    conv_w = dram_in("conv_w", [NL, CONV_W, 1, 512])
    conv_b = dram_in("conv_b", [NL, 512])
    conv_ln_g = dram_in("conv_ln_g", [NL, 512])
    conv_ln_b = dram_in("conv_ln_b", [NL, 512])
    w_sb_proj = dram_in("w_sb_proj", [NL, 512, D])
    w_diff_proj = dram_in("w_diff_proj", [NL, 512, D])
    w_conv_proj = dram_in("w_conv_proj", [NL, 512, D])
    w_out = dram_in("w_out", [NL, D, D])
    ln_xattn = dram_in("ln_xattn", [NL, D])
    ln_mem = dram_in("ln_mem", [NL, D])
    xa_w_q = dram_in("xa_w_q", [NL, D, 512])
    xa_w_kv = dram_in("xa_w_kv", [NL, D, D])
    xa_w_o = dram_in("xa_w_o", [NL, 512, D])
    ln_mlp = dram_in("ln_mlp", [NL, D])
    w_up = dram_in("w_up", [NL, D, DFF])
    w_down = dram_in("w_down", [NL, DFF, D])
    ln_final = dram_in("ln_final", [D])
    y_out = nc.dram_tensor("y", [S, D], F32, kind="ExternalOutput").ap()

    xres = scratch("xres", [S, D], F32)
    xnT_d = scratch("xnT", [KC, 128, S], BF16)
    featT_d = scratch("featT", [24, 128, S], BF16)
    vtok_d = scratch("vtok", [2, S, 512], BF16)
    obrT_d = scratch("obrT", [12, 128, S], BF16)
    def wload(dst, dst_ap, src2d, first=True):
        K = src2d.shape[0]
        for k in range(K // 128):
            P.dma("pool", dst_ap[:, k, :], src2d[k * 128:(k + 1) * 128, :], dst,
                  writes=[dst] if (first and k == 0) else (), pwrites=() if (first and k == 0) else [dst])

    d_xres = [Buf(name=f"xres{i}") for i in range(NT)]
    d_xnT = [Buf(name=f"xnT{i}") for i in range(NT)]
    d_misc = Buf(name="dmisc")

    def want(s):
        return stages is None or s in stages

    G = Stage(P, "g")
    ident_f = G.sb([128, 128], F32, "identf")
    ident_b = G.sb([128, 128], BF16, "identb")
    ones_f = G.sb([128, 512], F32, "onesf")
    ones_b = G.sb([128, 128], BF16, "onesb")
    U_b = G.sb([128, 128], BF16, "U")
    LU_b = G.sb([128, 128], BF16, "LU")
    cols = G.sb([128, 128], F32, "cols")
    epsc = G.sb([128, 1], F32, "eps")

    P.op("pool", lambda e: e.memset(ones_f[:], 1.0), writes=[ones_f])
    P.op("pool", lambda e: e.memset(epsc[:], EPS), writes=[epsc])
    P.op("dve", lambda e: e.tensor_copy(out=ones_b[:], in_=ones_f[:, 0:128]), reads=[ones_f], writes=[ones_b])
    P.op("pool", lambda e: e.affine_select(out=ident_f[:], in_=ones_f[:, 0:128], pattern=[[1, 128]],
                                           compare_op=ALU.is_equal, fill=0.0, base=0, channel_multiplier=-1),
         reads=[ones_f], writes=[ident_f])
    P.op("dve", lambda e: e.tensor_copy(out=ident_b[:], in_=ident_f[:]), reads=[ident_f], writes=[ident_b])
    P.op("pool", lambda e: e.affine_select(out=U_b[:], in_=ones_f[:, 0:128], pattern=[[-1, 128]],
                                           compare_op=ALU.is_ge, fill=0.0, base=0, channel_multiplier=1),
         reads=[ones_f], writes=[U_b])
    P.op("pool", lambda e: e.affine_select(out=LU_b[:], in_=ones_f[:, 0:128], pattern=[[1, 128]],
                                           compare_op=ALU.is_gt, fill=0.0, base=0, channel_multiplier=-1),
         reads=[ones_f], writes=[LU_b])
    COL = {}
    with ExitStack() as es0:
        rows_t = es0.enter_context(nc.sbuf_tensor("rows", [128, 128], F32))
        rows = Buf(rows_t, "rows")
        pst = Buf(es0.enter_context(nc.psum_tensor("rowsT", [128, 128], F32)))
        P.op("pool", lambda e: e.memset(rows[:], 0.0), writes=[rows])
        r0 = 0

        def addrows(key, ap, n):
            nonlocal r0
            COL[key] = r0
            P.dma("sp", rows[r0:r0 + n, :], ap.rearrange("(c p) -> c p", p=128), rows, pwrites=[rows])
            r0 += n

        for l in range(NL):
            addrows(("ln_mix", l), ln_mix[l], 8)
            addrows(("ln_xattn", l), ln_xattn[l], 8)
            addrows(("ln_mem", l), ln_mem[l], 8)
            addrows(("ln_mlp", l), ln_mlp[l], 8)
            addrows(("conv_b", l), conv_b[l], 4)
            addrows(("conv_ln_g", l), conv_ln_g[l], 4)
            addrows(("conv_ln_b", l), conv_ln_b[l], 4)
            addrows(("subln", l), diff_subln[l], 1)
        assert r0 <= 128
        P.op("pe", lambda e: e.transpose(out=pst[:], in_=rows[:], identity=ident_f[:]),
             reads=[rows, ident_f], writes=[pst])
        P.op("dve", lambda e: e.tensor_copy(out=cols[:], in_=pst[:]), reads=[pst], writes=[cols])
        P.barrier()

    def colap(key, c=0, n=1):
        b = COL[key] + c
        return cols[:, b:b + n]

    class NormT:
        def __init__(self, st, nxnb=2):
            self.junk = Rot([st.sb([128, D], BF16, "junk") for _ in range(2)])
            self.xnb = Rot([st.sb([128, D], BF16, "xnb") for _ in range(nxnb)])
            self.stat = Rot([st.sb([128, 4], F32, "stat") for _ in range(max(4, nxnb + 2))])
            self.pT = Rot([st.ps([128, KC, 128], BF16, "pT") for _ in range(2)])

        def rstd(self, xt, width=D):
            stt = self.stat.next()
            junk = self.junk.next()
            P.op("act", lambda e: e.activation(out=junk[:, 0:width], in_=xt[:, 0:width], func=AF.Square,
                                               accum_out=stt[:, 0:1]),
                 reads=[xt], writes=[junk, stt])
            P.op("act", lambda e: e.activation(out=stt[:, 1:2], in_=stt[:, 0:1], func=AF.Ln, bias=epsc[:, 0:1],
                                               scale=1.0 / width),
                 reads=[epsc], writes=[stt])
            P.op("act", lambda e: e.activation(out=stt[:, 2:3], in_=stt[:, 1:2], func=AF.Exp, scale=-0.5),
                 writes=[stt])
            return stt

        def prep(self, xt):
            stt = self.rstd(xt)
            xnb = self.xnb.next()
            P.op("act", lambda e: e.activation(out=xnb[:], in_=xt[:], func=AF.Copy, scale=stt[:, 2:3]),
                 reads=[xt, stt], writes=[xnb])
            return xnb

        def finish(self, xnb, gkey, dst, dst_cols):
            pT = self.pT.next()
            for c in range(KC):
                P.op("pe", lambda e, c=c: e.transpose(out=pT[:, c, :], in_=xnb[:, c * 128:(c + 1) * 128],
                                                      identity=ident_b[:]),
                     reads=[xnb, ident_b], pwrites=[pT] if c else (), writes=() if c else [pT], inc=(c == KC - 1))
            g = colap(gkey, 0, KC)
            P.op("dve", lambda e: e.tensor_tensor(out=dst[:, :, dst_cols], in0=pT[:, :, :],
                                                  in1=g.unsqueeze(2).to_broadcast([128, KC, 128]), op=ALU.mult),
                 reads=[pT, cols], pwrites=[dst])

        def run(self, xt, gkey, dst, dst_cols):
            self.finish(self.prep(xt), gkey, dst, dst_cols)

    class Tail:
        def __init__(self, st, ntiles, final=False, nps=2):
            self.nt = NormT(st, nxnb=ntiles)
            self.xt = [st.sb([128, D], F32, "xt") for _ in range(ntiles)]
            self.ps_r = Rot([st.ps([128, 512], F32, "tp") for _ in range(nps)])
            self.ob_r = Rot([st.sb([128, KC, 128], BF16, "tob") for _ in range(2)])
            self.yt_r = Rot([st.sb([128, D], F32, "yt") for _ in range(2)]) if final else None
            self.gfin = None
            if final:
                self.gfin = st.sb([128, D], F32, "gfin")
                P.dma("sp", self.gfin[:], ln_final.partition_broadcast(128), self.gfin, writes=[self.gfin])

        def block(self, t0, ntiles, lhs, K, w, xsrc, gkey_next, final=False):
            xnbs = []
            for tt in range(ntiles):
                t = t0 + tt
                tcol = tt * 128
                xt = self.xt[tt]
                rows = slice(t * 128, (t + 1) * 128)
                P.dma("sp", xt[:], xsrc[rows, :], xt, reads=[d_xres[t]], writes=[xt])
                for half in range(2):
                    ps = self.ps_r.next()
                    hsl = slice(half * 512, (half + 1) * 512)
                    for k in range(K):
                        P.op("pe", lambda e, k=k, ps=ps, hsl=hsl, tcol=tcol: e.matmul(
                            ps[:], lhsT=lhs[:, k, tcol:tcol + 128], rhs=w[:, k, hsl], start=(k == 0), stop=(k == K - 1)),
                             reads=[lhs, w], writes=[ps] if k == 0 else (), pwrites=[ps] if k else (), inc=(k == K - 1))
                    P.op("dve", lambda e, ps=ps, hsl=hsl, xt=xt: e.tensor_tensor(out=xt[:, hsl], in0=xt[:, hsl], in1=ps[:],
                                                                                 op=ALU.add),
                         reads=[ps, xt], pwrites=[xt])
                if final:
                    stt = self.nt.rstd(xt)
                    yt = self.yt_r.next()
                    P.op("dve", lambda e, xt=xt, stt=stt, yt=yt: e.scalar_tensor_tensor(
                        out=yt[:], in0=xt[:], scalar=stt[:, 2:3], in1=self.gfin[:], op0=ALU.mult, op1=ALU.mult),
                         reads=[xt, stt, self.gfin], writes=[yt])
                    P.dma("pool", y_out[rows, :], yt[:], yt, reads=[yt], pwrites=[d_misc])
                else:
                    P.dma("pool", xres[rows, :], xt[:], xt, reads=[xt], writes=[d_xres[t]])
                    xnbs.append(self.nt.prep(xt))
            if not final:
                for tt in range(ntiles):
                    t = t0 + tt
                    rows = slice(t * 128, (t + 1) * 128)
                    ob = self.ob_r.next()
                    self.nt.finish(xnbs[tt], gkey_next, ob, slice(0, 128))
                    P.dma("pool", xnT_d[:, :, rows].rearrange("c p t -> p c t"), ob[:], ob, reads=[ob],
                          writes=[d_xnT[t]])

    gbias_d = nc.dram_tensor("gbias", [4, 128, 1024], F32, kind="Internal").ap()
    tb = G.sb([128, 128], F32, "tb")
    P.dma("sp", tb[:], rel_bias.rearrange("b h -> (b h)").partition_broadcast(128), tb, writes=[tb])
    thr = [0] * 32
    for n in range(0, 200):
        b = _bucket(n)
        for k in range(1, b + 1):
            if thr[k] == 0:
                thr[k] = n
    if want("A0"):
        st = Stage(P, "A0")
        nt = NormT(st)
        xt_r = Rot([st.sb([128, D], F32, "xt") for _ in range(3)])
        ob_r = Rot([st.sb([128, KC, 128], BF16, "ob") for _ in range(3)])
        for t in range(NT):
            xt = xt_r.next()
            P.dma("sp", xt[:], x_in[t * 128:(t + 1) * 128, :], xt, writes=[xt])
            ob = ob_r.next()
            nt.run(xt, ("ln_mix", 0), ob, slice(0, 128))
            P.dma("pool", xnT_d[:, :, t * 128:(t + 1) * 128].rearrange("c p t -> p c t"), ob[:], ob,
                  reads=[ob], writes=[d_xnT[t]])
        st.close()

    for l in range(nlayers):
        lam_init = 0.8 - 0.6 * math.exp(-0.3 * l)

        if want("B"):
            st = Stage(P, f"B{l}")
            xn_all = st.sb([128, KC, S], BF16, "xnall")
            for c in range(KC):
                P.dma("sp", xn_all[:, c, :], xnT_d[c], xn_all, reads=d_xnT, pwrites=[xn_all])
            wb_r = Rot([st.sb([128, KC, 512], BF16, "wb") for _ in range(2)])
            ps_r = Rot([st.ps([128, 512], F32, "ps") for _ in range(4)])
            ob_r = Rot([st.sb([128, 512], BF16, "ob") for _ in range(4)])
            do_gb = (l == 0)
            ev_r = Rot(["act"] if do_gb else ["act", "dve"])
            if do_gb:
                relt = st.sb([128, 1024], F32, "rel")
                dtb = st.sb([128, 128], F32, "dtb")
                acc = st.sb([128, 1024], F32, "acc")
                tmp_r = Rot([st.sb([128, 1024], F32, "tmp") for _ in range(2)])
                gout_r = Rot([st.sb([128, 1024], F32, "gout") for _ in range(2)])
                P.op("pool", lambda e: e.iota(relt[:], pattern=[[1, 1024]], base=-384, channel_multiplier=-1,
                                              allow_small_or_imprecise_dtypes=True), writes=[relt])
                P.op("dve", lambda e: e.tensor_sub(out=dtb[:, 4:128], in0=tb[:, 4:128], in1=tb[:, 0:124]), reads=[tb],
                     writes=[dtb])
                pending_store = []

                def gb_head(h):
                    bs_ = slice(384, 640)
                    P.op("dve", lambda e: e.tensor_scalar(out=acc[:, bs_], in0=relt[:, bs_], scalar1=0.0, scalar2=tb[:, h:h + 1],
                                                          op0=ALU.mult, op1=ALU.add), reads=[relt, tb], writes=[acc])
                    for k in range(1, 32):
                        tmp = tmp_r.next()
                        P.op("dve", lambda e: e.tensor_scalar(out=tmp[:, bs_], in0=relt[:, bs_], scalar1=float(thr[k]) - 0.5,
                                                              scalar2=dtb[:, 4 * k + h:4 * k + h + 1], op0=ALU.is_ge,
                                                              op1=ALU.mult), reads=[relt, dtb], writes=[tmp])
                        P.op("dve", lambda e: e.tensor_tensor(out=acc[:, bs_], in0=acc[:, bs_], in1=tmp[:, bs_], op=ALU.add),
                             reads=[tmp], writes=[acc])
                    gout = gout_r.next()
                    P.op("pool", lambda e: e.memset(gout[:, 0:384], NEG), writes=[gout])
                    P.op("dve", lambda e: e.tensor_scalar(out=gout[:, 640:1024], in0=relt[:, 640:1024], scalar1=0.0,
                                                          scalar2=tb[:, 124 + h:125 + h], op0=ALU.mult, op1=ALU.add),
                         reads=[relt, tb], pwrites=[gout])
                    P.op("pool", lambda e: e.affine_select(out=gout[:, bs_], in_=acc[:, bs_], pattern=[[1, 256]],
                                                           compare_op=ALU.is_ge, fill=NEG, base=0, channel_multiplier=-1),
                         reads=[acc], pwrites=[gout])
                    pending_store.append((h, gout))

                def gb_flush():
                    while pending_store:
                        h, gout = pending_store.pop(0)
                        P.dma("sp", gbias_d[h], gout[:], gout, reads=[gout], pwrites=[d_misc])


            def evac(ps, ob):
                ce = ev_r.next()
                if ce == "act":
                    P.op("act", lambda e: e.copy(out=ob[:], in_=ps[:]), reads=[ps], writes=[ob])
                else:
                    P.op("dve", lambda e: e.tensor_copy(out=ob[:], in_=ps[:]), reads=[ps], writes=[ob])

            groups = [("f", 0, 0), ("f", 512, 4), ("f", 1536, 8), ("f", 2048, 12), ("f", 3072, 16), ("f", 3584, 20),
                      ("t", 1024, 0), ("t", 2560, 1)]
            wbs = []

            def issue_w(gi):
                wb = wb_r.next()
                c0 = groups[gi][1]
                wload(wb, wb, w_in[l][:, c0:c0 + 512])
                wbs.append(wb)

            issue_w(0)
            for gi, (kind, c0, fb) in enumerate(groups):
                if gi + 1 < len(groups):
                    issue_w(gi + 1)
                if do_gb and gi < 4:
                    gb_flush()
                    gb_head(gi)
                if do_gb and gi == 5:
                    gb_flush()
                wb = wbs[gi]
                if kind == "f":
                    for blk in range(NB):
                        for m in range(4):
                            ps = ps_r.next()
                            for k in range(KC):
                                P.op("pe", lambda e, k=k, m=m, blk=blk, ps=ps, wb=wb: e.matmul(
                                    ps[:], lhsT=wb[:, k, m * 128:(m + 1) * 128],
                                    rhs=xn_all[:, k, blk * 512:(blk + 1) * 512], start=(k == 0), stop=(k == KC - 1)),
                                     reads=[wb, xn_all], writes=[ps] if k == 0 else (), pwrites=[ps] if k else (),
                                     inc=(k == KC - 1))
                            ob = ob_r.next()
                            evac(ps, ob)
                            P.dma("sp", featT_d[fb + m, :, blk * 512:(blk + 1) * 512], ob[:], ob, reads=[ob],
                                  pwrites=[d_misc])
                else:
                    for t in range(NT):
                        ps = ps_r.next()
                        for k in range(KC):
                            P.op("pe", lambda e, k=k, t=t, ps=ps, wb=wb: e.matmul(
                                ps[:], lhsT=xn_all[:, k, t * 128:(t + 1) * 128], rhs=wb[:, k, :],
                                start=(k == 0), stop=(k == KC - 1)),
                                 reads=[wb, xn_all], writes=[ps] if k == 0 else (), pwrites=[ps] if k else (),
                                 inc=(k == KC - 1))
                        ob = ob_r.next()
                        evac(ps, ob)
                        P.dma("sp", vtok_d[fb, t * 128:(t + 1) * 128, :], ob[:], ob, reads=[ob], pwrites=[d_misc])
            st.close()


        if want("C"):
            st = Stage(P, f"C{l}")
            qz_r = Rot([[st.sb([128, S], BF16, "qz") for _ in range(2)] for _ in range(2)])
            kT_r = Rot([st.sb([128, S], BF16, "kT") for _ in range(2)])
            vz_r = Rot([[st.sb([128, NT, 128], BF16, "vz") for _ in range(2)] for _ in range(2)])
            oT_r = Rot([st.sb([128, S], BF16, "oT") for _ in range(2)])
            for pair in qz_r.items + vz_r.items:
                for bz in pair:
                    P.op("pool", lambda e, bz=bz: e.memset(bz[:], 0.0), writes=[bz])
            masks = st.sb([128, 4, 512], F32, "masks")
            for r in range(4):
                P.op("pool", lambda e, r=r: e.affine_select(out=masks[:, r, :], in_=ones_f[:], pattern=[[1, 512]],
                                                             compare_op=ALU.is_gt, fill=0.0, base=-128 * r,
                                                             channel_multiplier=-1),
                     reads=[ones_f], writes=[masks] if r == 0 else (), pwrites=[masks] if r else ())
            psZ = Rot([st.ps([128, 512], F32, "Z") for _ in range(4)])
            psP = [st.ps([128, 512], F32, "P") for _ in range(2)]
            psO = Rot([st.ps([128, 512], F32, "O") for _ in range(2)])
            E_r = Rot([st.sb([128, 512], F32, "E") for _ in range(4)])
            SP_r = Rot([st.sb([128, 512], BF16, "SP") for _ in range(4)])
            X_r = Rot([st.sb([128, 512], F32, "X") for _ in range(4)])
            A_r = Rot([st.sb([128, 512], BF16, "A") for _ in range(4)])
            masks_b = st.sb([128, 4, 512], BF16, "masksb")
            P.op("dve", lambda e: e.tensor_copy(out=masks_b[:], in_=masks[:]), reads=[masks], writes=[masks_b])
            pair_res = {}

            def load_pair(j):
                if j >= 4 or j in pair_res:
                    return
                qz, kT, vz, oT = qz_r.next(), kT_r.next(), vz_r.next(), oT_r.next()
                P.dma("sp", kT[:], featT_d[4 + j], kT, reads=[d_misc], writes=[kT])
                for hh in range(2):
                    hs_ = slice(64 * hh, 64 * hh + 64)
                    P.dma("sp", qz[hh][hs_, :], featT_d[j, hs_, :], qz[hh], reads=[d_misc], pwrites=[qz[hh]])
                    P.dma("sp", vz[hh][:, :, hs_],
                          vtok_d[0, :, j * 128 + 64 * hh:j * 128 + 64 * hh + 64].rearrange("(t p) c -> p t c", p=128),
                          vz[hh], reads=[d_misc], pwrites=[vz[hh]])
                pair_res[j] = (qz, kT, vz, oT)

            steps = [(j, i, s_) for j in range(4) for i in range(NB) for s_ in range(4 * i + 4)]
            cur = {}
            Obank = {}

            def sub_of(s):
                return slice(128 * (3 - s), 512) if s < 3 else slice(0, 512)

            def emitZ(j, i, hh, s):
                qz, kT, vz, oT = pair_res[j]
                kb = 4 * i + 3 - s
                sub = sub_of(s)
                Z = psZ.next()
                P.op("pe", lambda e: e.matmul(Z[:, sub], lhsT=kT[:, kb * 128:(kb + 1) * 128],
                                              rhs=qz[hh][:, i * 512 + sub.start:(i + 1) * 512], start=True, stop=True),
                     reads=[kT, qz[hh]], writes=[Z])
                cur[("Z", j, i, hh, s)] = Z

            load_pair(0)
            for hh in (0, 1):
                emitZ(0, 0, hh, 0)
            for gi, (j, i, s) in enumerate(steps):
                qz, kT, vz, oT = pair_res[j]
                n = 4 * i + 4
                if s == 0:
                    Obank[(j, i)] = psO.next()
                    if i == 0:
                        load_pair(j + 1)
                O = Obank[(j, i)]
                qs = slice(i * 512, (i + 1) * 512)
                sub = sub_of(s)
                nxt = steps[gi + 1] if gi + 1 < len(steps) else None
                Es, SPs, Xs, As = {}, {}, {}, {}
                for hh in (0, 1):
                    Z = cur.pop(("Z", j, i, hh, s))
                    E = E_r.next()
                    P.op("act", lambda e, E=E, Z=Z: e.activation(out=E[:, sub], in_=Z[:, sub], func=AF.Exp, scale=0.125),
                         reads=[Z], writes=[E])
                    Es[hh] = E
                for hh in (0, 1):
                    SPb = SP_r.next()
                    P.op("act", lambda e, SPb=SPb, hh=hh: e.activation(out=SPb[:, sub], in_=Es[hh][:, sub], func=AF.Ln,
                                                                       bias=1.0),
                         reads=[Es[hh]], writes=[SPb])
                    SPs[hh] = SPb
                if s < 4:
                    r = 3 - s
                    for hh in (0, 1):
                        P.op("dve", lambda e, hh=hh: e.tensor_tensor(out=SPs[hh][:, sub], in0=SPs[hh][:, sub],
                                                                     in1=masks_b[:, r, sub], op=ALU.mult),
                             reads=[masks_b], writes=[SPs[hh]])
                        P.op("pool", lambda e, hh=hh: e.tensor_tensor(out=Es[hh][:, sub], in0=Es[hh][:, sub],
                                                                      in1=masks[:, r, sub], op=ALU.mult),
                             reads=[masks], writes=[Es[hh]])
                for hh in (0, 1):
                    Pb = psP[hh]
                    P.op("pe", lambda e, Pb=Pb, hh=hh: e.matmul(Pb[:, sub], lhsT=U_b[:], rhs=SPs[hh][:, sub], start=(s == 0),
                                                                stop=False, skip_group_check=True),
                         reads=[U_b, SPs[hh]], writes=[Pb] if s == 0 else (), pwrites=[Pb] if s else ())
                    if nxt is not None:
                        emitZ(nxt[0], nxt[1], hh, nxt[2])
                for hh in (0, 1):
                    X = X_r.next()
                    P.op("act", lambda e, X=X, hh=hh: e.activation(out=X[:, sub], in_=psP[hh][:, sub], func=AF.Exp,
                                                                   scale=-1.0),
                         reads=[psP[hh]], writes=[X])
                    Xs[hh] = X
                for hh in (0, 1):
                    A = A_r.next()
                    P.op("dve", lambda e, A=A, hh=hh: e.tensor_tensor(out=A[:, sub], in0=Es[hh][:, sub], in1=Xs[hh][:, sub],
                                                                      op=ALU.mult),
                         reads=[Es[hh], Xs[hh]], writes=[A])
                    As[hh] = A
                for hh in (0, 1):
                    kb = 4 * i + 3 - s
                    if s < n - 1:
                        P.op("pe", lambda e, hh=hh: e.matmul(psP[hh][:, sub], lhsT=LU_b[:], rhs=SPs[hh][:, sub], start=False,
                                                             stop=True, skip_group_check=True),
                             reads=[LU_b, SPs[hh]], pwrites=[psP[hh]])
                    first = (s == 0 and hh == 0)
                    P.op("pe", lambda e, hh=hh, kb=kb, first=first: e.matmul(
                        O[:, sub], lhsT=vz[hh][:, kb, :], rhs=As[hh][:, sub], start=first, stop=(s == n - 1 and hh == 1),
                        skip_group_check=True),
                         reads=[vz[hh], As[hh]], writes=[O] if first else (), pwrites=() if first else [O])
                if s == n - 1:
                    P.op("dve", lambda e: e.tensor_copy(out=oT[:, qs], in_=O[:]), reads=[O], pwrites=[oT])
                    if i == NB - 1:
                        P.dma("pool", obrT_d[j], oT[:], oT, reads=[oT], pwrites=[d_misc])
            st.close()

        if want("D"):
            st = Stage(P, f"D{l}")
            lamb = st.sb([128, 256], F32, "lamb")
            lprod = st.sb([128, 128], F32, "lprod")
            lst = st.sb([128, 8], F32, "lst")
            P.dma("sp", lamb[:], diff_lambda[l].rearrange("a d -> (a d)").partition_broadcast(128), lamb, writes=[lamb])
            P.op("dve", lambda e: e.tensor_tensor(out=lprod[:, 0:64], in0=lamb[:, 0:64], in1=lamb[:, 64:128], op=ALU.mult),
                 reads=[lamb], writes=[lprod])
            P.op("dve", lambda e: e.tensor_tensor(out=lprod[:, 64:128], in0=lamb[:, 128:192], in1=lamb[:, 192:256],
                                                  op=ALU.mult), reads=[lamb], writes=[lprod])
            P.op("dve", lambda e: e.reduce_sum(out=lst[:, 0:1], in_=lprod[:, 0:64], axis=AX.X), reads=[lprod], writes=[lst])
            P.op("dve", lambda e: e.reduce_sum(out=lst[:, 1:2], in_=lprod[:, 64:128], axis=AX.X), reads=[lprod],
                 writes=[lst])
            P.op("act", lambda e: e.activation(out=lst[:, 2:4], in_=lst[:, 0:2], func=AF.Exp), reads=[lst], writes=[lst])
            P.op("dve", lambda e: e.tensor_sub(out=lst[:, 4:5], in0=lst[:, 3:4], in1=lst[:, 2:3]), reads=[lst], writes=[lst])
            P.op("dve", lambda e: e.tensor_scalar(out=lst[:, 5:6], in0=lst[:, 4:5], scalar1=-lam_init, scalar2=None,
                                                  op0=ALU.add), reads=[lst], writes=[lst])
            P.op("dve", lambda e: e.tensor_scalar(out=lst[:, 6:7], in0=colap(("subln", l)), scalar1=1.0 - lam_init,
                                                  scalar2=None, op0=ALU.mult), reads=[cols, lst], writes=[lst])
            qz_r = Rot([[st.sb([128, S], BF16, "qz") for _ in range(2)] for _ in range(2)])
            for pair in qz_r.items:
                for bz in pair:
                    P.op("pool", lambda e, bz=bz: e.memset(bz[:], 0.0), writes=[bz])
            kT_r = Rot([st.sb([128, S], BF16, "kT") for _ in range(2)])
            v_r = Rot([st.sb([128, NT, 128], BF16, "v") for _ in range(2)])
            oT_r = Rot([st.sb([128, S], BF16, "oT") for _ in range(2)])
            G_r = Rot([st.sb([128, 1024], F32, "Gh") for _ in range(2)])
            psZ = Rot([st.ps([128, 512], F32, "Z") for _ in range(4)])
            psO = [st.ps([128, 512], F32, "O") for _ in range(2)]
            psD = [st.ps([128, 512], F32, "Dn") for _ in range(2)]
            T_r = Rot([st.sb([128, 512], F32, "T") for _ in range(3)])
            A_r = Rot([st.sb([128, 512], BF16, "A") for _ in range(4)])
            rec_r = Rot([st.sb([128, 512], F32, "rec") for _ in range(2)])
            R_b = [st.sb([128, 512], F32, "R") for _ in range(2)]
            o_b = st.sb([128, 512], F32, "o")
            sq_b = st.sb([128, 512], BF16, "sq")
            rs_b = st.sb([128, 512], F32, "rs")
            head_res = {}

            def load_head(h):
                if h >= 4 or h in head_res:
                    return
                qz, kT, vv, oT, Gh = qz_r.next(), kT_r.next(), v_r.next(), oT_r.next(), G_r.next()
                for m_ in range(2):
                    ms_ = slice(64 * m_, 64 * m_ + 64)
                    P.dma("sp", qz[m_][ms_, :], featT_d[8 + h, ms_, :], qz[m_], reads=[d_misc], pwrites=[qz[m_]])
                P.dma("sp", kT[:], featT_d[12 + h], kT, reads=[d_misc], writes=[kT])
                P.dma("sp", vv[:], vtok_d[1, :, h * 128:(h + 1) * 128].rearrange("(t p) c -> p t c", p=128), vv,
                      reads=[d_misc], writes=[vv])
                P.dma("sp", Gh[:], gbias_d[h], Gh, reads=[d_misc], writes=[Gh])
                head_res[h] = (qz, kT, vv, oT, Gh)

            steps = [(h, i, s_) for h in range(4) for i in range(NB) for s_ in range(4 * i + 4)]
            cur = {}

            def dsub(i, s):
                dl = 4 * i - s
                return slice(128 * (-dl), 512) if dl < 0 else slice(0, 512)

            def emitZ(h, i, m, s):
                qz, kT, vv, oT, Gh = head_res[h]
                Z = psZ.next()
                sub = dsub(i, s)
                P.op("pe", lambda e: e.matmul(Z[:, sub], lhsT=kT[:, s * 128:(s + 1) * 128],
                                              rhs=qz[m][:, i * 512 + sub.start:(i + 1) * 512],
                                              start=True, stop=True), reads=[kT, qz[m]], writes=[Z])
                cur[("Z", h, i, m, s)] = Z

            load_head(0)
            for m in (0, 1):
                emitZ(0, 0, m, 0)
            for gi, (h, i, s) in enumerate(steps):
                qz, kT, vv, oT, Gh = head_res[h]
                if s == 0 and i == 0:
                    load_head(h + 1)
                n = 4 * i + 4
                qs = slice(i * 512, (i + 1) * 512)
                b31 = tb[:, 31 * 4 + h:31 * 4 + h + 1]
                nxt = steps[gi + 1] if gi + 1 < len(steps) else None
                sub = dsub(i, s)
                As = {}
                for m in (0, 1):
                    Z = cur.pop(("Z", h, i, m, s))
                    A = A_r.next()
                    dlt = 4 * i - s
                    if dlt >= 2:
                        P.op("act", lambda e, A=A, Z=Z: e.activation(out=A[:], in_=Z[:], func=AF.Exp, bias=b31, scale=0.125),
                             reads=[Z, tb], writes=[A])
                    else:
                        c0 = 128 * dlt + 384
                        T = T_r.next()
                        P.op("dve", lambda e, T=T, Z=Z, c0=c0: e.scalar_tensor_tensor(
                            out=T[:, sub], in0=Z[:, sub], scalar=0.125, in1=Gh[:, c0 + sub.start:c0 + 512], op0=ALU.mult,
                            op1=ALU.add),
                             reads=[Z, Gh], writes=[T])
                        P.op("act", lambda e, A=A, T=T: e.activation(out=A[:, sub], in_=T[:, sub], func=AF.Exp),
                             reads=[T], writes=[A])
                    As[m] = A
                if nxt is not None:
                    for m in (0, 1):
                        emitZ(nxt[0], nxt[1], m, nxt[2])
                for m in (0, 1):
                    P.op("pe", lambda e, m=m: e.matmul(psO[m][:, sub], lhsT=vv[:, s, :], rhs=As[m][:, sub], start=(s == 0),
                                                       stop=(s == n - 1), skip_group_check=True),
                         reads=[vv, As[m]], writes=[psO[m]] if s == 0 else (), pwrites=[psO[m]] if s else ())
                for m in (0, 1):
                    P.op("pe", lambda e, m=m: e.matmul(psD[m][:, sub], lhsT=ones_b[:], rhs=As[m][:, sub], start=(s == 0),
                                                       stop=(s == n - 1), skip_group_check=True),
                         reads=[ones_b, As[m]], writes=[psD[m]] if s == 0 else (), pwrites=[psD[m]] if s else ())
                if s == n - 1:
                    recs = [rec_r.next(), rec_r.next()]
                    for m in (0, 1):
                        P.op("act", lambda e, m=m: e.activation(out=recs[m][:], in_=psD[m][:], func=AF.Ln),
                             reads=[psD[m]], writes=[recs[m]])
                    for m in (0, 1):
                        P.op("act", lambda e, m=m: e.activation(out=recs[m][:], in_=recs[m][:], func=AF.Exp, scale=-1.0),
                             writes=[recs[m]])
                        P.op("dve", lambda e, m=m: e.tensor_tensor(out=R_b[m][:], in0=psO[m][:], in1=recs[m][:],
                                                                   op=ALU.mult),
                             reads=[psO[m], recs[m]], writes=[R_b[m]])
                    P.op("dve", lambda e: e.scalar_tensor_tensor(out=o_b[:], in0=R_b[1][:], scalar=lst[:, 5:6],
                                                                 in1=R_b[0][:], op0=ALU.mult, op1=ALU.add),
                         reads=[R_b[0], R_b[1], lst], writes=[o_b])
                    P.op("pool", lambda e: e.tensor_tensor(out=sq_b[:], in0=o_b[:], in1=o_b[:], op=ALU.mult),
                         reads=[o_b], writes=[sq_b])
                    Zs = psZ.next()
                    P.op("pe", lambda e, Zs=Zs: e.matmul(Zs[:], lhsT=ones_b[:], rhs=sq_b[:], start=True, stop=True),
                         reads=[ones_b, sq_b], writes=[Zs])
                    P.op("act", lambda e, Zs=Zs: e.activation(out=rs_b[:], in_=Zs[:], func=AF.Ln, bias=epsc[:, 0:1],
                                                              scale=1.0 / 128), reads=[Zs, epsc], writes=[rs_b])
                    P.op("act", lambda e: e.activation(out=rs_b[:], in_=rs_b[:], func=AF.Exp, scale=-0.5),
                         writes=[rs_b])
                    P.op("dve", lambda e, oT=oT, qs=qs: e.scalar_tensor_tensor(out=oT[:, qs], in0=o_b[:], scalar=lst[:, 6:7],
                                                                               in1=rs_b[:], op0=ALU.mult, op1=ALU.mult),
                         reads=[o_b, rs_b, lst], pwrites=[oT])
                    if i == NB - 1:
                        P.dma("pool", obrT_d[4 + h], oT[:], oT, reads=[oT], pwrites=[d_misc])
            st.close()

        if want("E"):
            st = Stage(P, f"E{l}")
            rows31 = st.sb([32, 512], F32, "rows31")
            cwc = st.sb([128, 4, CONV_W], F32, "cwc")
            dg = st.sb([128, 4 * CONV_W, 128], BF16, "dg")
            hpad = st.sb([128, 4, HALO + S], BF16, "hpad")
            ocv = st.sb([128, 4, S], BF16, "ocv")
            a_r = Rot([st.sb([128, S], BF16, "aT") for _ in range(2)])
            g_r = Rot([st.sb([128, S], BF16, "gT") for _ in range(2)])
            sg_r = Rot([st.sb([128, S], BF16, "sg") for _ in range(2)])
            psC = Rot([st.ps([128, 512], F32, "C") for _ in range(3)])
            psS = [st.ps([128, 512], F32, "S1"), st.ps([128, 512], F32, "S2")]
            psT = st.ps([128, 4, 32], F32, "cwT")
            hc_all = [[st.sb([128, 512], F32, "hc") for _ in range(4)] for _ in range(2)]
            hb_r = Rot([st.sb([128, 512], BF16, "hb") for _ in range(3)])
            sq_r = Rot([st.sb([128, 512], BF16, "sq") for _ in range(3)])
            mean = st.sb([128, 512], F32, "mean")
            msq = st.sb([128, 512], F32, "msq")
            var = st.sb([128, 512], F32, "var")
            rstd = st.sb([128, 512], F32, "rstd")
            t1_r = Rot([st.sb([128, 512], F32, "t1") for _ in range(2)])
            t2_r = Rot([st.sb([128, 512], F32, "t2") for _ in range(2)])
            s2_r = Rot([st.sb([128, 512], F32, "s2") for _ in range(2)])
            P.dma("sp", rows31[0:CONV_W, :], conv_w[l].rearrange("w o c -> w (o c)"), rows31, writes=[rows31])
            for c in range(4):
                P.op("pe", lambda e, c=c: e.transpose(out=psT[:, c, 0:CONV_W], in_=rows31[0:CONV_W, c * 128:(c + 1) * 128],
                                                      identity=ident_f[0:CONV_W, 0:CONV_W]),
                     reads=[rows31, ident_f], writes=[psT] if c == 0 else (), pwrites=[psT] if c else ())
            P.op("dve", lambda e: e.tensor_copy(out=cwc[:], in_=psT[:, :, 0:CONV_W]), reads=[psT], writes=[cwc])
            for c in range(4):
                P.op("dve", lambda e, c=c: e.tensor_tensor(
                    out=dg[:, c * CONV_W:(c + 1) * CONV_W, :],
                    in0=ident_b[:].unsqueeze(1).to_broadcast([128, CONV_W, 128]),
                    in1=cwc[:, c, :].unsqueeze(2).to_broadcast([128, CONV_W, 128]), op=ALU.mult),
                     reads=[ident_b, cwc], writes=[dg] if c == 0 else (), pwrites=[dg] if c else ())
            P.op("pool", lambda e: e.memset(hpad[:, :, 0:HALO], 0.0), writes=[hpad])
            for c in range(4):
                aT, gT, sg = a_r.next(), g_r.next(), sg_r.next()
                P.dma("sp", aT[:], featT_d[16 + c], aT, reads=[d_misc], writes=[aT])
                P.dma("sp", gT[:], featT_d[20 + c], gT, reads=[d_misc], writes=[gT])
                P.op("act", lambda e: e.activation(out=sg[:], in_=gT[:], func=AF.Sigmoid), reads=[gT], writes=[sg])
                P.op("dve", lambda e, c=c: e.tensor_tensor(out=hpad[:, c, HALO:HALO + S], in0=aT[:], in1=sg[:],
                                                           op=ALU.mult), reads=[aT, sg], pwrites=[hpad])
            cb = COL[("conv_b", l)]
            cg = COL[("conv_ln_g", l)]
            cbb = COL[("conv_ln_b", l)]
            for blk in range(NB):
                t0 = blk * 512
                hc = hc_all[blk % 2]
                pend = []

                def stats(c, hb, sq):
                    P.op("pe", lambda e, c=c, hb=hb: e.matmul(psS[0][:], lhsT=ones_b[:], rhs=hb[:], start=(c == 0),
                                                              stop=(c == 3)),
                         reads=[ones_b, hb], writes=[psS[0]] if c == 0 else (), pwrites=[psS[0]] if c else ())
                    P.op("pe", lambda e, c=c, sq=sq: e.matmul(psS[1][:], lhsT=ones_b[:], rhs=sq[:], start=(c == 0),
                                                              stop=(c == 3)),
                         reads=[ones_b, sq], writes=[psS[1]] if c == 0 else (), pwrites=[psS[1]] if c else ())

                for c in range(4):
                    ps = psC.next()
                    for w in range(CONV_W):
                        P.op("pe", lambda e, c=c, w=w, ps=ps: e.matmul(
                            ps[:], lhsT=dg[:, c * CONV_W + w, :], rhs=hpad[:, c, t0 + w:t0 + w + 512],
                            start=(w == 0), stop=(w == CONV_W - 1)),
                             reads=[dg, hpad], writes=[ps] if w == 0 else (), pwrites=[ps] if w else (),
                             inc=(w == CONV_W - 1))
                    if pend:
                        stats(*pend.pop(0))
                    hb, sq = hb_r.next(), sq_r.next()
                    P.op("dve", lambda e, c=c, ps=ps, hb=hb: e.tensor_scalar(out=hb[:], in0=ps[:], scalar1=cols[:, cb + c:cb + c + 1],
                                                                             scalar2=None, op0=ALU.add),
                         reads=[ps, cols], writes=[hb])
                    P.op("dve", lambda e, c=c, ps=ps: e.tensor_scalar(out=hc[c][:], in0=ps[:], scalar1=cols[:, cb + c:cb + c + 1],
                                                                      scalar2=None, op0=ALU.add),
                         reads=[ps, cols], writes=[hc[c]])
                    P.op("act", lambda e, c=c, sq=sq: e.activation(out=sq[:], in_=hc[c][:], func=AF.Square),
                         reads=[hc[c]], writes=[sq])
                    pend.append((c, hb, sq))
                while pend:
                    stats(*pend.pop(0))
                P.op("dve", lambda e: e.tensor_scalar(out=mean[:], in0=psS[0][:], scalar1=1.0 / 512, scalar2=None,
                                                      op0=ALU.mult), reads=[psS[0]], writes=[mean])
                P.op("pool", lambda e: e.tensor_tensor(out=msq[:], in0=mean[:], in1=mean[:], op=ALU.mult),
                     reads=[mean], writes=[msq])
                P.op("dve", lambda e: e.scalar_tensor_tensor(out=var[:], in0=psS[1][:], scalar=1.0 / 512, in1=msq[:],
                                                             op0=ALU.mult, op1=ALU.subtract),
                     reads=[psS[1], msq], writes=[var])
                P.op("act", lambda e: e.activation(out=rstd[:], in_=var[:], func=AF.Ln, bias=epsc[:, 0:1]),
                     reads=[var, epsc], writes=[rstd])
                P.op("act", lambda e: e.activation(out=rstd[:], in_=rstd[:], func=AF.Exp, scale=-0.5), writes=[rstd])
                for c in range(4):
                    t1, t2, s2 = t1_r.next(), t2_r.next(), s2_r.next()
                    P.op("pool", lambda e, c=c, t1=t1: e.tensor_tensor(out=t1[:], in0=hc[c][:], in1=mean[:], op=ALU.subtract),
                         reads=[hc[c], mean], writes=[t1])
                    P.op("dve", lambda e, t1=t1, t2=t2: e.tensor_tensor(out=t2[:], in0=t1[:], in1=rstd[:], op=ALU.mult),
                         reads=[t1, rstd], writes=[t2])
                    P.op("dve", lambda e, c=c, t2=t2: e.tensor_scalar(out=t2[:], in0=t2[:], scalar1=cols[:, cg + c:cg + c + 1],
                                                                      scalar2=cols[:, cbb + c:cbb + c + 1], op0=ALU.mult,
                                                                      op1=ALU.add),
                         reads=[cols], writes=[t2])
                    P.op("act", lambda e, t2=t2, s2=s2: e.activation(out=s2[:], in_=t2[:], func=AF.Sigmoid),
                         reads=[t2], writes=[s2])
                    P.op("pool", lambda e, c=c, t2=t2, s2=s2: e.tensor_tensor(out=ocv[:, c, t0:t0 + 512], in0=t2[:],
                                                                              in1=s2[:], op=ALU.mult),
                         reads=[t2, s2], pwrites=[ocv])
            for c in range(4):
                P.dma("pool", obrT_d[8 + c], ocv[:, c, :], ocv, reads=[ocv], pwrites=[d_misc])
            st.close()


        if want("F1"):
            st = Stage(P, f"F1{l}")
            w_g = st.sb([128, KC, 3072], BF16, "wg")
            w_br = st.sb([128, 12, D], BF16, "wbr")
            w_o = st.sb([128, KC, D], BF16, "wo")
            wload(w_g, w_g, w_in[l][:, 4096:7168])
            wload(w_br, w_br[:, 0:4, :], w_sb_proj[l])
            wload(w_br, w_br[:, 4:8, :], w_diff_proj[l], first=False)
            wload(w_br, w_br[:, 8:12, :], w_conv_proj[l], first=False)
            wload(w_o, w_o, w_out[l])
            xn_r = Rot([st.sb([128, KC, 512], BF16, "xnb") for _ in range(2)])
            ob_r = Rot([st.sb([128, 12, 512], BF16, "obr") for _ in range(2)])
            yT_r = Rot([st.sb([128, KC, 512], BF16, "yT") for _ in range(2)])
            psG = Rot([st.ps([128, 512], F32, "G") for _ in range(2)])
            psB = Rot([st.ps([128, 512], F32, "B") for _ in range(2)])
            sg_r = Rot([st.sb([128, 512], F32, "sg") for _ in range(2)])
            acc_r = Rot([st.sb([128, 512], F32, "acc") for _ in range(2)])
            tmp_r = Rot([st.sb([128, 512], F32, "tmp") for _ in range(2)])
            tl = Tail(st, 4)
            xsrc = x_in if l == 0 else xres
            yTs = {}

            def f1_main(blk):
                bs = slice(blk * 512, (blk + 1) * 512)
                xnb, obr, yT = xn_r.next(), ob_r.next(), yT_r.next()
                yTs[blk] = yT
                P.dma("sp", xnb[:], xnT_d[:, :, bs].rearrange("c p t -> p c t"), xnb,
                      reads=d_xnT[4 * blk:4 * blk + 4], writes=[xnb])
                P.dma("sp", obr[:], obrT_d[:, :, bs].rearrange("c p t -> p c t"), obr, reads=[d_misc], writes=[obr])
                for m in range(KC):
                    acc = acc_r.next()
                    for br in range(3):
                        Gp, Bp = psG.next(), psB.next()
                        for k in range(KC):
                            P.op("pe", lambda e, k=k, Gp=Gp, br=br: e.matmul(
                                Gp[:], lhsT=w_g[:, k, br * 1024 + m * 128:br * 1024 + (m + 1) * 128], rhs=xnb[:, k, :],
                                start=(k == 0), stop=(k == KC - 1)),
                                 reads=[w_g, xnb], writes=[Gp] if k == 0 else (), pwrites=[Gp] if k else (),
                                 inc=(k == KC - 1))
                        for k in range(4):
                            P.op("pe", lambda e, k=k, Bp=Bp, br=br: e.matmul(
                                Bp[:], lhsT=w_br[:, br * 4 + k, m * 128:(m + 1) * 128], rhs=obr[:, br * 4 + k, :],
                                start=(k == 0), stop=(k == 3)),
                                 reads=[w_br, obr], writes=[Bp] if k == 0 else (), pwrites=[Bp] if k else (),
                                 inc=(k == 3))
                        sg = sg_r.next()
                        P.op("act", lambda e, Gp=Gp, sg=sg: e.activation(out=sg[:], in_=Gp[:], func=AF.Sigmoid),
                             reads=[Gp], writes=[sg])
                        if br == 0:
                            P.op("dve", lambda e, Bp=Bp, sg=sg: e.tensor_tensor(out=acc[:], in0=Bp[:], in1=sg[:], op=ALU.mult),
                                 reads=[Bp, sg], writes=[acc])
                        else:
                            tmp = tmp_r.next()
                            P.op("dve", lambda e, Bp=Bp, sg=sg, tmp=tmp: e.tensor_tensor(out=tmp[:], in0=Bp[:], in1=sg[:],
                                                                                         op=ALU.mult),
                                 reads=[Bp, sg], writes=[tmp])
                            if br == 1:
                                P.op("dve", lambda e, tmp=tmp: e.tensor_tensor(out=acc[:], in0=acc[:], in1=tmp[:], op=ALU.add),
                                     reads=[tmp], writes=[acc])
                            else:
                                P.op("dve", lambda e, tmp=tmp: e.tensor_tensor(out=yT[:, m, :], in0=acc[:], in1=tmp[:],
                                                                               op=ALU.add),
                                     reads=[tmp, acc], pwrites=[yT])

            f1_main(0)
            for blk in range(NB):
                if blk + 1 < NB:
                    f1_main(blk + 1)
                tl.block(4 * blk, 4, yTs.pop(blk), KC, w_o, xsrc, ("ln_xattn", l))
            st.close()

        stGw = None
        wu_pre = None
        if want("F2") and want("G"):
            stGw = Stage(P, f"Gw{l}")
            wu_pre = stGw.sb([128, KC, DFF], BF16, "wu")
        if want("F2"):
            st = Stage(P, f"F2{l}")
            wkv = st.sb([128, KC, D], BF16, "wkv")
            wq = st.sb([128, KC, 512], BF16, "wq")
            wo = st.sb([128, 4, D], BF16, "wo")
            wload(wkv, wkv, xa_w_kv[l])
            wload(wq, wq, xa_w_q[l])
            wload(wo, wo, xa_w_o[l])
            if wu_pre is not None:
                wload(wu_pre, wu_pre, w_up[l])
            tl = Tail(st, 4)
            memT = st.sb([128, KC, MEM], BF16, "memT")
            kTx = st.sb([128, 4, MEM], BF16, "kTx")
            vx = st.sb([128, 2, 512], BF16, "vx")
            mt_r = Rot([st.sb([128, D], F32, "mt") for _ in range(2)])
            psZ = Rot([st.ps([128, 512], F32, "Z") for _ in range(2)])
            psO = st.ps([128, 512], F32, "O")
            psD = st.ps([128, 512], F32, "Dn")
            for mb in range(2):
                mt = mt_r.next()
                P.dma("sp", mt[:], mem_in[mb * 128:(mb + 1) * 128, :], mt, writes=[mt])
                tl.nt.run(mt, ("ln_mem", l), memT, slice(mb * 128, (mb + 1) * 128))
            for h in range(4):
                ps = psZ.next()
                for k in range(KC):
                    P.op("pe", lambda e, k=k, ps=ps, h=h: e.matmul(ps[:, 0:MEM], lhsT=wkv[:, k, h * 128:(h + 1) * 128],
                                                                   rhs=memT[:, k, :], start=(k == 0), stop=(k == KC - 1)),
                         reads=[wkv, memT], writes=[ps] if k == 0 else (), pwrites=[ps] if k else (), inc=(k == KC - 1))
                P.op("dve", lambda e, ps=ps, h=h: e.tensor_copy(out=kTx[:, h, :], in_=ps[:, 0:MEM]), reads=[ps],
                     pwrites=[kTx])
            for mb in range(2):
                ps = psZ.next()
                for k in range(KC):
                    P.op("pe", lambda e, k=k, ps=ps, mb=mb: e.matmul(ps[:], lhsT=memT[:, k, mb * 128:(mb + 1) * 128],
                                                                     rhs=wkv[:, k, 512:1024], start=(k == 0),
                                                                     stop=(k == KC - 1)),
                         reads=[wkv, memT], writes=[ps] if k == 0 else (), pwrites=[ps] if k else (), inc=(k == KC - 1))
                P.op("dve", lambda e, ps=ps, mb=mb: e.tensor_copy(out=vx[:, mb, :], in_=ps[:]), reads=[ps], pwrites=[vx])
            xn_r = Rot([st.sb([128, KC, 512], BF16, "xnb") for _ in range(2)])
            qTx_r = Rot([st.sb([128, 4, 512], BF16, "qTx") for _ in range(2)])
            oTx_r = Rot([st.sb([128, 4, 512], BF16, "oTx") for _ in range(2)])
            A_r = Rot([st.sb([128, 512], BF16, "A") for _ in range(4)])
            rec_r = Rot([st.sb([128, 512], F32, "rec") for _ in range(2)])
            xscale = 1.0 / math.sqrt(128.0)
            oTxs = {}

            def f2_main(blk):
                bs = slice(blk * 512, (blk + 1) * 512)
                xnb, qTx, oTx = xn_r.next(), qTx_r.next(), oTx_r.next()
                oTxs[blk] = oTx
                P.dma("sp", xnb[:], xnT_d[:, :, bs].rearrange("c p t -> p c t"), xnb,
                      reads=d_xnT[4 * blk:4 * blk + 4], writes=[xnb])
                for h in range(4):
                    ps = psZ.next()
                    for k in range(KC):
                        P.op("pe", lambda e, k=k, ps=ps, h=h: e.matmul(ps[:], lhsT=wq[:, k, h * 128:(h + 1) * 128],
                                                                       rhs=xnb[:, k, :], start=(k == 0), stop=(k == KC - 1)),
                             reads=[wq, xnb], writes=[ps] if k == 0 else (), pwrites=[ps] if k else (), inc=(k == KC - 1))
                    P.op("dve", lambda e, ps=ps, h=h: e.tensor_copy(out=qTx[:, h, :], in_=ps[:]), reads=[ps], pwrites=[qTx])
                for h in range(4):
                    As = []
                    for mb in range(2):
                        Z = psZ.next()
                        P.op("pe", lambda e, Z=Z, h=h, mb=mb: e.matmul(Z[:], lhsT=kTx[:, h, mb * 128:(mb + 1) * 128],
                                                                       rhs=qTx[:, h, :], start=True, stop=True),
                             reads=[kTx, qTx], writes=[Z])
                        A = A_r.next()
                        P.op("act", lambda e, Z=Z, A=A: e.activation(out=A[:], in_=Z[:], func=AF.Exp, scale=xscale),
                             reads=[Z], writes=[A])
                        As.append(A)
                    for mb in range(2):
                        P.op("pe", lambda e, h=h, mb=mb: e.matmul(psO[:], lhsT=vx[:, mb, h * 128:(h + 1) * 128],
                                                                  rhs=As[mb][:], start=(mb == 0), stop=(mb == 1)),
                             reads=[vx, As[mb]], writes=[psO] if mb == 0 else (), pwrites=[psO] if mb else ())
                    for mb in range(2):
                        P.op("pe", lambda e, mb=mb: e.matmul(psD[:], lhsT=ones_b[:], rhs=As[mb][:], start=(mb == 0),
                                                             stop=(mb == 1)),
                             reads=[ones_b, As[mb]], writes=[psD] if mb == 0 else (), pwrites=[psD] if mb else ())
                    rec = rec_r.next()
                    P.op("act", lambda e, rec=rec: e.activation(out=rec[:], in_=psD[:], func=AF.Ln), reads=[psD], writes=[rec])
                    P.op("act", lambda e, rec=rec: e.activation(out=rec[:], in_=rec[:], func=AF.Exp, scale=-1.0), writes=[rec])
                    P.op("dve", lambda e, rec=rec, h=h: e.tensor_tensor(out=oTx[:, h, :], in0=psO[:], in1=rec[:], op=ALU.mult),
                         reads=[psO, rec], pwrites=[oTx])

            f2_main(0)
            for blk in range(NB):
                if blk + 1 < NB:
                    f2_main(blk + 1)
                tl.block(4 * blk, 4, oTxs.pop(blk), 4, wo, xres, ("ln_mlp", l))
            st.close()

        if want("G"):
            st = Stage(P, f"G{l}")
            wd = st.sb([128, 32, D], BF16, "wd")
            if wu_pre is not None:
                wu = wu_pre
            else:
                wu = st.sb([128, KC, DFF], BF16, "wu")
                wload(wu, wu, w_up[l])
            wload(wd, wd, w_down[l])
            last = (l == nlayers - 1)
            tl = Tail(st, 2, final=last)
            xn_r = Rot([st.sb([128, KC, 256], BF16, "xnb") for _ in range(2)])
            hT = st.sb([128, 32, 256], BF16, "hT")
            r_r = Rot([st.sb([128, 256], F32, "r") for _ in range(3)])
            psU = Rot([st.ps([128, 512], F32, "U") for _ in range(4)])
            for b2 in range(S // 256):
                bs = slice(b2 * 256, (b2 + 1) * 256)
                xnb = xn_r.next()
                P.dma("sp", xnb[:], xnT_d[:, :, bs].rearrange("c p t -> p c t"), xnb,
                      reads=d_xnT[2 * b2:2 * b2 + 2], writes=[xnb])
                for f in range(32):
                    ps = psU.next()
                    for k in range(KC):
                        P.op("pe", lambda e, k=k, ps=ps, f=f: e.matmul(ps[:, 0:256], lhsT=wu[:, k, f * 128:(f + 1) * 128],
                                                                       rhs=xnb[:, k, :], start=(k == 0), stop=(k == KC - 1)),
                             reads=[wu, xnb], writes=[ps] if k == 0 else (), pwrites=[ps] if k else (), inc=(k == KC - 1))
                    r = r_r.next()
                    P.op("dve", lambda e, ps=ps, r=r: e.tensor_scalar(out=r[:], in0=ps[:, 0:256], scalar1=0.0, scalar2=None,
                                                                      op0=ALU.max), reads=[ps], writes=[r])
                    P.op("act", lambda e, r=r, f=f: e.activation(out=hT[:, f, :], in_=r[:], func=AF.Square),
                         reads=[r], pwrites=[hT])
                tl.block(2 * b2, 2, hT, 32, wd, xres, ("ln_mix", l + 1) if not last else None, final=last)
            st.close()
        if stGw is not None:
            stGw.close()

    P.barrier()
    G.es.close()
    return nc, P


INPUT_NAMES = ["x", "mem", "rel_bias", "ln_mix", "w_in", "diff_lambda", "diff_subln", "conv_w", "conv_b",
               "conv_ln_g", "conv_ln_b", "w_sb_proj", "w_diff_proj", "w_conv_proj", "w_out", "ln_xattn", "ln_mem",
               "xa_w_q", "xa_w_kv", "xa_w_o", "ln_mlp", "w_up", "w_down", "ln_final"]


def make_in_maps(inputs, S, ncores=8):
    shared = {k: np.ascontiguousarray(np.asarray(inputs[k], dtype=np.float32)) for k in INPUT_NAMES
              if k not in ("x", "mem")}
    maps = []
    for b in range(ncores):
        m = dict(shared)
        m["x"] = np.ascontiguousarray(np.asarray(inputs["x"][b, :S], dtype=np.float32))
        m["mem"] = np.ascontiguousarray(np.asarray(inputs["mem"][b], dtype=np.float32))
        maps.append(m)
    return maps


def kernel(**inputs):
    S = 4096
    nc, _ = build(S)
    maps = make_in_maps(inputs, S)
    res = run_bass_kernel_spmd(nc, maps, core_ids=list(range(8)))
    return np.stack([np.asarray(r["y"], dtype=np.float32) for r in res.results], axis=0)
```
